# Optimizing a Trainium2 kernel written in Bass

```python
import math
import jax, jax.numpy as jnp
from jax import lax
import numpy as np


D_MODEL = 1024
BATCH = 16
SEQ = 2048
DEPTH = 1

N_MEM = 256
MLA_HEADS = 4
MLA_NOPE = 128
MLA_ROPE = 64
MLA_V = 128
Q_LORA_RANK = 384
KV_LORA_RANK = 256
Q_BLOCK = 128
ROPE_THETA = 10000.0
GDN_HEADS = 4
GDN_DK = 128
GDN_DV = 128
GDN_CONV = 4
GDN_CHUNK = 64
MEM_HEADS = 4
MEM_DH = 128
D_MIX = MLA_HEADS * MLA_V + GDN_HEADS * GDN_DV + MEM_HEADS * MEM_DH
GDN_QKV = 2 * GDN_HEADS * GDN_DK + GDN_HEADS * GDN_DV
IN_SPLITS = (Q_LORA_RANK, KV_LORA_RANK, MLA_ROPE, GDN_QKV, GDN_HEADS, GDN_HEADS, MEM_HEADS * MEM_DH, D_MIX)
D_IN = sum(IN_SPLITS)
EPS = 1e-6

kernel_name = 'hybrid_mla_gdn_memory_parallel_heads'


def rms_norm(t, gain):
    tf = t.astype(jnp.float32)
    y = tf * lax.rsqrt(jnp.mean(tf * tf, axis=-1, keepdims=True) + EPS)
    return (y * gain.astype(jnp.float32)).astype(t.dtype)


def l2_normalize(t):
    return t * lax.rsqrt(jnp.sum(t * t, axis=-1, keepdims=True) + EPS)


def split_cols(t, sizes):
    idx = [int(i) for i in np.cumsum(sizes)[:-1]]
    return jnp.split(t, idx, axis=-1)


def rope_tables(positions):
    half = MLA_ROPE // 2
    inv_freq = 1.0 / (ROPE_THETA ** (jnp.arange(half, dtype=jnp.float32) / half))
    ang = positions.astype(jnp.float32)[..., None] * inv_freq
    return jnp.cos(ang), jnp.sin(ang)


def apply_rope(t, cos, sin):
    half = t.shape[-1] // 2
    tf = t.astype(jnp.float32)
    t1, t2 = tf[..., :half], tf[..., half:]
    return jnp.concatenate([t1 * cos - t2 * sin, t2 * cos + t1 * sin], axis=-1).astype(t.dtype)


def causal_mla(q_nope, q_rope, k_nope, k_rope, v):
    S = q_nope.shape[1]
    scale = (MLA_NOPE + MLA_ROPE) ** -0.5
    outs = []
    for i in range(S // Q_BLOCK):
        lo, hi = i * Q_BLOCK, (i + 1) * Q_BLOCK
        s = (jnp.einsum('bqhd,bkhd->bhqk', q_nope[:, lo:hi], k_nope[:, :hi])
             + jnp.einsum('bqhr,bkr->bhqk', q_rope[:, lo:hi], k_rope[:, :hi])).astype(jnp.float32) * scale
        mask = jnp.arange(lo, hi)[:, None] >= jnp.arange(hi)[None, :]
        p = jax.nn.softmax(jnp.where(mask, s, -jnp.inf), axis=-1)
        outs.append(jnp.einsum('bhqk,bkhd->bqhd', p.astype(v.dtype), v[:, :hi]))
    return jnp.concatenate(outs, axis=1)


def causal_depthwise_conv(t, w):
    K, C = w.shape
    return lax.conv_general_dilated(t, w[:, None, :].astype(t.dtype), window_strides=(1,),
                                    padding=[(K - 1, 0)], dimension_numbers=('NWC', 'WIO', 'NWC'),
                                    feature_group_count=C)


def gated_delta_rule_chunked(q, k, v, g, beta):
    B, S, H, Dk = q.shape
    Dv = v.shape[-1]
    C = GDN_CHUNK
    N = S // C

    def chunks(t):
        return t.reshape(B, N, C, H, -1).transpose(0, 3, 1, 2, 4)

    q = chunks(q) * (Dk ** -0.5)
    k = chunks(k)
    v = chunks(v)
    g = jnp.cumsum(g.reshape(B, N, C, H).transpose(0, 3, 1, 2), axis=-1)
    beta = beta.reshape(B, N, C, H).transpose(0, 3, 1, 2)[..., None]
    incl = jnp.tril(jnp.ones((C, C), dtype=bool))
    strict = jnp.tril(jnp.ones((C, C), dtype=bool), -1)
    decay = jnp.exp(jnp.where(incl, g[..., :, None] - g[..., None, :], -jnp.inf))
    k_beta = k * beta
    L = jnp.where(strict, jnp.einsum('bhncd,bhnjd->bhncj', k_beta, k) * decay, 0.0)
    eye = jnp.eye(C, dtype=jnp.float32)
    T = lax.linalg.triangular_solve(L + eye, jnp.broadcast_to(eye, L.shape), left_side=True,
                                    lower=True, unit_diagonal=True)
    u = jnp.einsum('bhncj,bhnjv->bhncv', T, v * beta)
    w = jnp.einsum('bhncj,bhnjk->bhnck', T, k_beta * jnp.exp(g)[..., None])
    a_intra = jnp.einsum('bhncd,bhnjd->bhncj', q, k) * decay

    def step(state, xs):
        q_c, k_c, u_c, w_c, g_c, a_c = xs
        v_new = u_c - jnp.einsum('bhck,bhkv->bhcv', w_c, state)
        o = (jnp.einsum('bhck,bhkv->bhcv', q_c * jnp.exp(g_c)[..., None], state)
             + jnp.einsum('bhcj,bhjv->bhcv', a_c, v_new))
        g_last = g_c[..., -1:]
        state = (state * jnp.exp(g_last)[..., None]
                 + jnp.einsum('bhck,bhcv->bhkv', k_c * jnp.exp(g_last - g_c)[..., None], v_new))
        return state, o

    xs = tuple(jnp.moveaxis(t, 2, 0) for t in (q, k, u, w, g, a_intra))
    state0 = jnp.zeros((B, H, Dk, Dv), jnp.float32)
    _, o = lax.scan(step, state0, xs)
    return o.transpose(1, 0, 3, 2, 4).reshape(B, S, H, Dv)


def memory_attention(q, k, v):
    s = jnp.einsum('bqhd,bmhd->bhqm', q, k).astype(jnp.float32) * (MEM_DH ** -0.5)
    p = jax.nn.softmax(s, axis=-1)
    return jnp.einsum('bhqm,bmhd->bqhd', p.astype(v.dtype), v)


def setup_inputs(seed: int = 0) -> dict:
    key = jax.random.key(seed)
    ks = jax.random.split(key, 20)
    f32 = jnp.float32

    def normal(k, shape, scale):
        return jax.random.normal(k, shape, f32) * scale

    def gain(k, shape):
        return 1.0 + 0.02 * jax.random.normal(k, shape, f32)

    x = normal(ks[0], (BATCH, SEQ, D_MODEL), 1.0)
    mem = normal(ks[1], (BATCH, N_MEM, D_MODEL), 1.0)
    positions = (jax.random.randint(ks[2], (BATCH, 1), 0, 4096) + jnp.arange(SEQ)[None, :]).astype(jnp.int32)
    norm_in = gain(ks[3], (DEPTH, D_MODEL))
    w_in = normal(ks[4], (DEPTH, D_MODEL, D_IN), D_MODEL ** -0.5)
    q_a_norm = gain(ks[5], (DEPTH, Q_LORA_RANK))
    w_q_b = normal(ks[6], (DEPTH, Q_LORA_RANK, MLA_HEADS * (MLA_NOPE + MLA_ROPE)), Q_LORA_RANK ** -0.5)
    kv_a_norm = gain(ks[7], (DEPTH, KV_LORA_RANK))
    w_kv_b = normal(ks[8], (DEPTH, KV_LORA_RANK, MLA_HEADS * (MLA_NOPE + MLA_V)), KV_LORA_RANK ** -0.5)
    gdn_conv = normal(ks[9], (DEPTH, GDN_CONV, GDN_QKV), GDN_CONV ** -0.5)
    gdn_a_log = jnp.log(jax.random.uniform(ks[10], (DEPTH, GDN_HEADS), f32, minval=1.0, maxval=16.0))
    dt = jnp.exp(jax.random.uniform(ks[11], (DEPTH, GDN_HEADS), f32, minval=math.log(1e-3), maxval=math.log(1e-1)))
    gdn_dt_bias = dt + jnp.log(-jnp.expm1(-dt))
    gdn_norm = gain(ks[12], (DEPTH, GDN_DV))
    mem_norm = gain(ks[13], (DEPTH, D_MODEL))
    w_mem_kv = normal(ks[14], (DEPTH, D_MODEL, 2 * MEM_HEADS * MEM_DH), D_MODEL ** -0.5)
    w_out = normal(ks[15], (DEPTH, D_MIX, D_MODEL), D_MIX ** -0.5)
    norm_final = gain(ks[16], (D_MODEL,))
    return {'x': x, 'mem': mem, 'positions': positions, 'norm_in': norm_in, 'w_in': w_in,
            'q_a_norm': q_a_norm, 'w_q_b': w_q_b, 'kv_a_norm': kv_a_norm, 'w_kv_b': w_kv_b,
            'gdn_conv': gdn_conv, 'gdn_a_log': gdn_a_log, 'gdn_dt_bias': gdn_dt_bias,
            'gdn_norm': gdn_norm, 'mem_norm': mem_norm, 'w_mem_kv': w_mem_kv, 'w_out': w_out,
            'norm_final': norm_final}


def reference(x, mem, positions, norm_in, w_in, q_a_norm, w_q_b, kv_a_norm, w_kv_b, gdn_conv,
              gdn_a_log, gdn_dt_bias, gdn_norm, mem_norm, w_mem_kv, w_out, norm_final):
    B, S, _ = x.shape
    M = mem.shape[1]
    cos, sin = rope_tables(positions)
    for l in range(DEPTH):
        h = rms_norm(x, norm_in[l])
        c_q, c_kv, k_rope, gdn_qkv, gdn_a, gdn_b, mem_q, gate = split_cols(h @ w_in[l], IN_SPLITS)

        q = (rms_norm(c_q, q_a_norm[l]) @ w_q_b[l]).reshape(B, S, MLA_HEADS, MLA_NOPE + MLA_ROPE)
        q_nope = q[..., :MLA_NOPE]
        q_rope = apply_rope(q[..., MLA_NOPE:], cos[:, :, None], sin[:, :, None])
        kv = (rms_norm(c_kv, kv_a_norm[l]) @ w_kv_b[l]).reshape(B, S, MLA_HEADS, MLA_NOPE + MLA_V)
        k_nope, v_mla = kv[..., :MLA_NOPE], kv[..., MLA_NOPE:]
        k_rope = apply_rope(k_rope, cos, sin)
        o_mla = causal_mla(q_nope, q_rope, k_nope, k_rope, v_mla).reshape(B, S, MLA_HEADS * MLA_V)

        qkv = jax.nn.silu(causal_depthwise_conv(gdn_qkv, gdn_conv[l])).astype(jnp.float32)
        gq, gk, gv = split_cols(qkv, (GDN_HEADS * GDN_DK, GDN_HEADS * GDN_DK, GDN_HEADS * GDN_DV))
        gq = l2_normalize(gq.reshape(B, S, GDN_HEADS, GDN_DK))
        gk = l2_normalize(gk.reshape(B, S, GDN_HEADS, GDN_DK))
        gv = gv.reshape(B, S, GDN_HEADS, GDN_DV)
        beta = jax.nn.sigmoid(gdn_b.astype(jnp.float32))
        g = -jnp.exp(gdn_a_log[l].astype(jnp.float32)) * jax.nn.softplus(
            gdn_a.astype(jnp.float32) + gdn_dt_bias[l].astype(jnp.float32))
        o_gdn = gated_delta_rule_chunked(gq, gk, gv, g, beta)
        o_gdn = rms_norm(o_gdn, gdn_norm[l]).astype(x.dtype).reshape(B, S, GDN_HEADS * GDN_DV)

        mk, mv = split_cols(rms_norm(mem, mem_norm[l]) @ w_mem_kv[l], (MEM_HEADS * MEM_DH, MEM_HEADS * MEM_DH))
        o_mem = memory_attention(mem_q.reshape(B, S, MEM_HEADS, MEM_DH),
                                 mk.reshape(B, M, MEM_HEADS, MEM_DH),
                                 mv.reshape(B, M, MEM_HEADS, MEM_DH)).reshape(B, S, MEM_HEADS * MEM_DH)

        mixed = jnp.concatenate([o_mla, o_gdn, o_mem], axis=-1) * jax.nn.silu(gate)
        x = x + mixed @ w_out[l]
    return rms_norm(x, norm_final)
```

```python
import numpy as np
from contextlib import ExitStack
import concourse.bass as bass
import concourse.mybir as mybir
from concourse.bass_utils import run_bass_kernel_spmd

F32 = mybir.dt.float32
BF16 = mybir.dt.bfloat16
I32 = mybir.dt.int32
AF = mybir.ActivationFunctionType
ALU = mybir.AluOpType

EPS = 1e-6
S_LEN = 2048
NBLK = 4
TB = 512


class Sched:
    ENGS = ("pe", "act", "dve", "pool", "sp")

    def __init__(self, nc):
        self.nc = nc
        self.stream = {e: [] for e in self.ENGS}
        self.cnt = {e: 0 for e in self.ENGS}
        self.waited = {e: {} for e in self.ENGS}
        self.lastw = {}
        self.readers = {}
        self.dma_cnt = {}

    def _deps(self, reads, writes):
        deps = []
        for k in reads:
            ev = self.lastw.get(k)
            if ev is not None:
                deps.append((ev, "raw"))
            if k == "PB" or (isinstance(k, tuple) and k[0] == "P"):
                for r in self.readers.get(k, ()):
                    deps.append((r, "war"))
        for k in writes:
            ev = self.lastw.get(k)
            if ev is not None:
                deps.append((ev, "waw"))
            for r in self.readers.get(k, ()):
                deps.append((r, "war"))
        return deps

    def _add_waits(self, eng, deps):
        for (semkey, val), kind in deps:
            if semkey == eng and eng == "pe":
                continue
            if self.waited[eng].get(semkey, 0) >= val:
                continue
            self.waited[eng][semkey] = val
            self.stream[eng].append(("wait", semkey, val))

    def _record(self, ev, reads, writes):
        for k in reads:
            self.readers.setdefault(k, []).append(ev)
        for k in writes:
            self.lastw[k] = ev
            self.readers[k] = []

    def op(self, eng, fn, reads=(), writes=(), inc=True):
        self._add_waits(eng, self._deps(reads, writes))
        if inc:
            self.cnt[eng] += 1
            ev = (eng, self.cnt[eng])
        else:
            ev = (eng, self.cnt[eng] + 1)
        self.stream[eng].append(("op", fn, inc))
        self._record(ev, reads, writes)
        return ev

    def dma(self, eng, fn, reads=(), writes=(), semkey=None):
        sk = ("dma", semkey)
        n = self.dma_cnt.get(sk, 0)
        deps = self._deps(reads, writes)
        if n > 0:
            deps.append(((sk, 16 * n), "raw"))
        self._add_waits(eng, deps)
        n += 1
        self.dma_cnt[sk] = n
        ev = (sk, 16 * n)
        self.stream[eng].append(("dma", fn, sk))
        self._record(ev, reads, writes)
        return ev

    def wait_event(self, eng, ev):
        self._add_waits(eng, [(ev, "raw")])

    def barrier(self):
        for e in self.ENGS:
            for f in self.ENGS:
                if f != e and self.cnt[f] > 0:
                    self.wait_event(e, (f, self.cnt[f]))
            for sk, n in self.dma_cnt.items():
                self.wait_event(e, (sk, 16 * n))
        self.lastw.clear()
        self.readers.clear()

    def replay(self):
        nc = self.nc
        with ExitStack() as es:
            sems = {}
            for e in self.ENGS:
                sems[e] = es.enter_context(nc.semaphore("s_" + e))
            for i, sk in enumerate(self.dma_cnt):
                sems[sk] = es.enter_context(nc.semaphore("d%d" % i))
            block = es.enter_context(nc.Block())
            reg = {"pe": block.tensor, "act": block.scalar, "dve": block.vector,
                   "pool": block.gpsimd, "sp": block.sync}

            def make(e):
                def body(eng):
                    for it in self.stream[e]:
                        if it[0] == "wait":
                            eng.wait_ge(sems[it[1]], it[2])
                        elif it[0] == "op":
                            ins = it[1](eng)
                            if it[2]:
                                ins.then_inc(sems[e], 1)
                        else:
                            it[1](eng).then_inc(sems[it[2]], 16)
                return body

            for e in self.ENGS:
                if self.stream[e]:
                    reg[e](make(e))


def interleave(gens):
    gens = [g for g in gens if g is not None]
    while gens:
        nxt = []
        for g in gens:
            try:
                next(g)
                nxt.append(g)
            except StopIteration:
                pass
        gens = nxt


class _Stop(Exception):
    pass


def build_program(debug=False, stop=None):
    nc = bass.Bass("TRN2", target_bir_lowering=False)
    D = {}

    def din(name, shape, dt=F32):
        D[name] = nc.dram_tensor(name, shape, dt, kind="ExternalInput").ap()

    din("x", [4096, 1024]); din("mem", [512, 1024]); din("pos", [2, 2048], I32)
    din("norm_in", [1, 1024]); din("w_in", [1024, 4296]); din("qag", [128, 3]); din("w_q_b", [384, 768])
    din("kvag", [128, 2]); din("w_kv_b", [256, 1024]); din("convw", [128, 48]); din("alog", [128, 4])
    din("dtb", [128, 4]); din("gdng", [128, 1]); din("mem_norm", [1, 1024]); din("w_mem_kv", [1024, 1024])
    din("w_out", [1536, 1024]); din("norm_final", [1, 1024]); din("ident", [128, 128]); din("tri", [128, 128])
    din("ropec", [64, 3])
    y = nc.dram_tensor("y", [4096, 1024], F32, kind="ExternalOutput").ap()
    dbg = None
    if debug:
        dbg = nc.dram_tensor("dbg", [128, 12 * 2048], F32, kind="ExternalOutput").ap()

    S = Sched(nc)
    w_in_v = D["w_in"].rearrange("(k p) c -> p k c", p=128)

    def ACT(out, in_, func, reads, writes, **kw):
        S.op("act", lambda e: e.activation(out=out, in_=in_, func=func, **kw), reads, writes)

    def TT(eng, out, a, b, op, reads, writes):
        S.op(eng, lambda e: e.tensor_tensor(out=out, in0=a, in1=b, op=op), reads, writes)

    def TS(eng, out, a, s1, s2, op0, op1, reads, writes):
        if s2 is None:
            S.op(eng, lambda e: e.tensor_scalar(out=out, in0=a, scalar1=s1, scalar2=None, op0=op0), reads, writes)
        else:
            S.op(eng, lambda e: e.tensor_scalar(out=out, in0=a, scalar1=s1, scalar2=s2, op0=op0, op1=op1), reads, writes)

    def STT(eng, out, a, sc, b, op0, op1, reads, writes):
        S.op(eng, lambda e: e.scalar_tensor_tensor(out=out, in0=a, scalar=sc, in1=b, op0=op0, op1=op1), reads, writes)

    def CP(eng, out, in_, reads, writes):
        if eng == "act":
            S.op("act", lambda e: e.copy(out=out, in_=in_), reads, writes)
        else:
            S.op(eng, lambda e: e.tensor_copy(out=out, in_=in_), reads, writes)

    def RECIP(out, in_, reads, writes):
        ACT(out, in_, AF.Ln, reads, writes)
        ACT(out, out, AF.Exp, writes, writes, scale=-1.0)

    def MMG(out, pairs, reads, pkey, last_inc=True):
        n = len(pairs)
        for i, (l, r) in enumerate(pairs):
            S.op("pe", lambda e, l=l, r=r, i=i: e.matmul(out, lhsT=l, rhs=r, start=(i == 0), stop=(i == n - 1)),
                 reads if i == 0 else (), [pkey], inc=(last_inc and i == n - 1))

    def TR(out, in_, ident, reads, pkey, inc=True):
        S.op("pe", lambda e: e.transpose(out=out, in_=in_, identity=ident), reads, [pkey], inc=inc)

    def RSQRT(out, in_, scale, tmp, reads, writes, tmpkey):
        ACT(tmp, in_, AF.Ln, reads, [tmpkey], bias=EPS, scale=scale)
        ACT(out, tmp, AF.Exp, [tmpkey], writes, scale=-0.5)

    def chk(n):
        if stop == n:
            S.barrier()
            raise _Stop()

    with ExitStack() as top:
        def _emit():
            arena = {"t": None, "off": 0, "size": 0}
            arena_use = {}

            def sbt(es, name, shape, dt=F32):
                if es is top:
                    return es.enter_context(nc.sbuf_tensor(name, shape, dt))
                esz = 2 if dt == BF16 else 4
                n = 1
                for d in shape[1:]:
                    n *= d
                nbytes = n * esz
                off = arena["off"]
                arena["off"] = off + (nbytes + 63) // 64 * 64
                assert arena["off"] <= arena["size"], (name, arena["off"], arena["size"])
                ap = arena["t"][0:shape[0], off // 2:off // 2 + nbytes // 2]
                if dt != BF16:
                    ap = ap.bitcast(dt)
                if len(shape) == 3:
                    ap = ap.rearrange("p (a b) -> p a b", a=shape[1])
                elif len(shape) == 4:
                    ap = ap.rearrange("p (a b c) -> p a b c", a=shape[1], b=shape[2])
                return ap

            PS = [top.enter_context(nc.psum_tensor("ps%d" % i, [128, 512], F32)) for i in range(7)]
            PB = top.enter_context(nc.psum_tensor("psb", [128, 1024], BF16))
            pctr = [0]

            rot = {"banks": list(range(7))}
            actr = [0]

            def pnext():
                b = rot["banks"]
                k = b[pctr[0] % len(b)]
                pctr[0] += 1
                return ("P", k), PS[k]

            def pacc():
                i = actr[0] % 2
                actr[0] += 1
                return (("P", 3 + 2 * i), PS[3 + 2 * i]), (("P", 4 + 2 * i), PS[4 + 2 * i])

            IDF = sbt(top, "IDF", [128, 128]); TRI = sbt(top, "TRI", [128, 128]); MINC = sbt(top, "MINC", [128, 128])
            IDB = sbt(top, "IDB", [128, 128], BF16); CMASK = sbt(top, "CMASK", [128, 128], BF16)
            ONESB = sbt(top, "ONESB", [128, 128], BF16); ONESF = sbt(top, "ONESF", [128, 128])
            QAG = sbt(top, "QAG", [128, 3]); KVAG = sbt(top, "KVAG", [128, 2]); CONVW = sbt(top, "CONVW", [128, 48])
            ALOG = sbt(top, "ALOG", [128, 4]); DTB = sbt(top, "DTB", [128, 4]); GDNG = sbt(top, "GDNG", [128, 1])
            NEGA = sbt(top, "NEGA", [128, 4]); ROPEC = sbt(top, "ROPEC", [64, 3])
            KM = sbt(top, "KM", [128, 2, 4, 256], BF16); VM = sbt(top, "VM", [128, 2, 2, 512], BF16)
            POSI = sbt(top, "POSI", [64, TB], I32); YI = sbt(top, "YI", [64, TB], I32)
            XT = sbt(top, "XT", [128, 8, S_LEN], BF16)
            MIX = sbt(top, "MIX", [128, 12, S_LEN], BF16)
            asz = (int(nc.sbuf_bytes_remaining) - 2048) // 64 * 64
            arena["t"] = top.enter_context(nc.sbuf_tensor("ARENA", [128, asz // 2], BF16))
            arena["size"] = asz

            def ld(dst, src, key, eng="sp"):
                S.dma(eng, lambda e: e.dma_start(out=dst, in_=src), writes=[key], semkey=key)

            ld(IDF[:], D["ident"], "IDF"); ld(TRI[:], D["tri"], "TRI"); ld(QAG[:], D["qag"], "QAG")
            ld(KVAG[:], D["kvag"], "KVAG"); ld(CONVW[:], D["convw"], "CONVW"); ld(ALOG[:], D["alog"], "ALOG")
            ld(DTB[:], D["dtb"], "DTB"); ld(GDNG[:], D["gdng"], "GDNG"); ld(ROPEC[:], D["ropec"], "ROPEC")
            CP("dve", IDB[:], IDF[:], ["IDF"], ["IDB"])
            CP("dve", CMASK[:], TRI[:], ["TRI"], ["CMASK"])
            TS("dve", MINC[:], TRI[:], -1.0, 30000.0, ALU.add, ALU.mult, ["TRI"], ["MINC"])
            S.op("pool", lambda e: e.memset(ONESB[:], 1.0), (), ["ONESB"])
            S.op("pool", lambda e: e.memset(ONESF[:], 1.0), (), ["ONESF"])
            ACT(NEGA[:], ALOG[:], AF.Exp, ["ALOG"], ["NEGA"])
            TS("dve", NEGA[:], NEGA[:], -1.0, None, ALU.mult, None, ["NEGA"], ["NEGA"])

            with ExitStack() as es:
                arena["off"] = 0
                WMEM = sbt(es, "WMEM", [128, 8, 1024], BF16)
                GMEM = sbt(es, "GMEM", [128, 1024])
                MLD = [sbt(es, "MLD%d" % i, [128, 1024]) for i in range(2)]
                HBm = sbt(es, "HBm", [128, 1024], BF16)
                SCRm = sbt(es, "SCRm", [128, 1024], BF16)
                MEMT = sbt(es, "MEMT", [128, 8, 256], BF16)
                SSm = sbt(es, "SSm", [128, 4])
                wm_v = D["w_mem_kv"].rearrange("(k p) c -> p k c", p=128)
                for hh in range(2):
                    S.dma("pool", lambda e, hh=hh: e.dma_start(out=WMEM[:, 4 * hh:4 * hh + 4, :], in_=wm_v[:, 4 * hh:4 * hh + 4, :]),
                          writes=[("WMEM", hh)], semkey=("WMEM", hh))
                ld(GMEM[:], D["mem_norm"].partition_broadcast(128), "GMEM")
                for s in range(2):
                    for mt in range(2):
                        r0 = s * 256 + mt * 128
                        S.dma("sp", lambda e, r0=r0, mt=mt: e.dma_start(out=MLD[mt][:], in_=D["mem"][r0:r0 + 128, :]),
                              writes=[("MLD", mt)], semkey=("MLD", mt))
                        S.op("pool", lambda e: e.memset(SSm[:, 0:1], 0.0), (), ["SSm0"])
                        ACT(SCRm[:], MLD[mt][:], AF.Square, [("MLD", mt), "SSm0"], ["SCRm", "SSm0"], accum_out=SSm[:, 0:1])
                        RSQRT(SSm[:, 2:3], SSm[:, 0:1], 1.0 / 1024, SSm[:, 1:2], ["SSm0"], ["SSm3"], "SSm2")
                        STT("dve", HBm[:], MLD[mt][:], SSm[:, 2:3], GMEM[:], ALU.mult, ALU.mult, [("MLD", mt), "SSm3", "GMEM"], ["HBm"])
                        for k in range(8):
                            TR(PB[:, k * 128:(k + 1) * 128], HBm[:, k * 128:(k + 1) * 128], IDB[:], ["HBm", "IDB"] if k == 0 else (), "PB", inc=(k == 7))
                        CP("act", MEMT[:, :, mt * 128:(mt + 1) * 128], PB[:].rearrange("p (k t) -> p k t", k=8), ["PB"], [("MEMT", mt)])
                    for h in range(4):
                        pk, pt = pnext()
                        MMG(pt[:, 0:256], [(WMEM[:, k, h * 128:(h + 1) * 128], MEMT[:, k, :]) for k in range(8)],
                            [("WMEM", 0), ("WMEM", 1), ("MEMT", 0), ("MEMT", 1)], pk)
                        CP("act", KM[:, s, h, :], pt[:, 0:256], [pk], [("KM", s, h)])
                    for mt in range(2):
                        pk, pt = pnext()
                        MMG(pt[:, 0:512], [(MEMT[:, k, mt * 128:(mt + 1) * 128], WMEM[:, k, 512:1024]) for k in range(8)],
                            [("WMEM", 0), ("WMEM", 1), ("MEMT", 0), ("MEMT", 1)], pk)
                        CP("dve", VM[:, s, mt, :], pt[:, 0:512], [pk], [("VM", s, mt)])
                S.barrier()
                if stop == 0:
                    raise _Stop()

            for s in range(2):
                def load_mla_weights(WA, WQB, WKK, WKV):
                        for hh in range(2):
                            S.dma("pool", lambda e, hh=hh: e.dma_start(out=WA[:, 4 * hh:4 * hh + 4, 0:704], in_=w_in_v[:, 4 * hh:4 * hh + 4, 0:704]),
                                  writes=[("WA", hh)], semkey=("WA", hh))
                        S.dma("pool", lambda e: e.dma_start(out=WA[:, :, 704:736], in_=w_in_v[:, :, 672:704]), writes=[("WA", 2)], semkey=("WA", 2))
                        S.dma("pool", lambda e: e.dma_start(out=WA[:, :, 736:768], in_=w_in_v[:, :, 640:672]), writes=[("WA", 3)], semkey=("WA", 3))
                        WAK = [("WA", i) for i in range(4)]
                        wq_v = D["w_q_b"].rearrange("(k p) c -> p k c", p=128)
                        S.dma("pool", lambda e: e.dma_start(out=WQB[:, :, 0:768], in_=wq_v), writes=[("WQB", 0)], semkey=("WQB", 0))
                        for h in range(4):
                            b0 = h * 192 + 128
                            S.dma("pool", lambda e, h=h, b0=b0: e.dma_start(out=WQB[:, :, 768 + h * 64:768 + h * 64 + 32], in_=wq_v[:, :, b0 + 32:b0 + 64]),
                                  writes=[("WQB", 1 + 2 * h)], semkey=("WQB", 1))
                            S.dma("pool", lambda e, h=h, b0=b0: e.dma_start(out=WQB[:, :, 768 + h * 64 + 32:768 + h * 64 + 64], in_=wq_v[:, :, b0:b0 + 32]),
                                  writes=[("WQB", 2 + 2 * h)], semkey=("WQB", 2))
                        WQK = [("WQB", i) for i in range(9)]
                        wkv_v = D["w_kv_b"].rearrange("(k p) (h t d) -> p k h t d", p=128, h=4, t=2)
                        for m in range(2):
                            S.dma("pool", lambda e, m=m: e.dma_start(out=WKK[:, m, :].rearrange("p (h d) -> p h d", h=4), in_=wkv_v[:, m, :, 0, :]), writes=["WKK"], semkey=("WKK", m))
                            S.dma("pool", lambda e, m=m: e.dma_start(out=WKV[:, m, :].rearrange("p (h d) -> p h d", h=4), in_=wkv_v[:, m, :, 1, :]), writes=["WKV"], semkey=("WKV", m))


                xflat = MIX[:, 4:12, :].rearrange("p a b -> p (a b)")
                xoff = [0]

                def xcarve(nel_bf16, dt):
                    ap = xflat[:, xoff[0]:xoff[0] + nel_bf16]
                    xoff[0] += nel_bf16
                    assert xoff[0] <= 16384
                    return ap if dt == BF16 else ap.bitcast(dt)

                GIN = xcarve(2048, F32)
                XLD = [xcarve(2048, F32) for i in range(3)]
                HB = [xcarve(1024, BF16) for i in range(2)]
                SCR = xcarve(1024, BF16)
                SS = [xcarve(8, F32) for i in range(2)]
                ld(GIN[:], D["norm_in"].partition_broadcast(128), "GIN")

                def xtile_a(t, s=s):
                    sl = t % 3
                    r0 = s * S_LEN + t * 128
                    S.dma("sp", lambda e, r0=r0, sl=sl: e.dma_start(out=XLD[sl][:], in_=D["x"][r0:r0 + 128, :]),
                          writes=[("XLD", sl)], semkey=("XLD", sl))
                    ss = SS[t % 2]
                    sk = ("SS", t % 2)
                    S.op("pool", lambda e, ss=ss: e.memset(ss[:, 0:1], 0.0), (), [sk + (0,)])
                    ACT(SCR[:], XLD[sl][:], AF.Square, [("XLD", sl), sk + (0,)], ["SCR", sk + (0,)], accum_out=ss[:, 0:1])
                    RSQRT(ss[:, 2:3], ss[:, 0:1], 1.0 / 1024, ss[:, 1:2], [sk + (0,)], [sk + (3,)], sk + (2,))
                    hb = HB[t % 2]
                    STT("dve", hb[:], XLD[sl][:], ss[:, 2:3], GIN[:], ALU.mult, ALU.mult, [("XLD", sl), sk + (3,), "GIN"], [("HB", t % 2)])

                def xtile_b(t):
                    hb = HB[t % 2]
                    for k in range(8):
                        TR(PB[:, k * 128:(k + 1) * 128], hb[:, k * 128:(k + 1) * 128], IDB[:], [("HB", t % 2), "IDB"] if k == 0 else (), "PB", inc=(k == 7))
                    CP("act" if t % 2 == 0 else "dve", XT[:, :, t * 128:(t + 1) * 128], PB[:].rearrange("p (k t) -> p k t", k=8), ["PB"], [("XT", t)])

                for t in range(4):
                    xtile_a(t)
                    xtile_b(t)

                def xt_keys(j):
                    return [("XT", 4 * j + i) for i in range(4)]

                with ExitStack() as es:
                    arena["off"] = 0
                    WA = sbt(es, "WA", [128, 8, 768], BF16)
                    WQB = sbt(es, "WQB", [128, 3, 1024], BF16)
                    WKK = sbt(es, "WKK", [128, 2, 512], BF16); WKV = sbt(es, "WKV", [128, 2, 512], BF16)
                    KN = sbt(es, "KN", [128, 4, S_LEN], BF16); KR = sbt(es, "KR", [128, S_LEN], BF16)
                    V = sbt(es, "V", [128, 16, 512], BF16)
                    CQ = sbt(es, "CQ", [128, 3, TB], BF16); CKV = sbt(es, "CKV", [128, 2, TB], BF16)
                    SQ = [sbt(es, "SQ%d" % i, [128, TB], BF16) for i in range(5)]
                    TMP = sbt(es, "TMP", [128, TB]); TMPK = sbt(es, "TMPK", [128, TB]); RQ = sbt(es, "RQ", [128, TB]); RKV = sbt(es, "RKV", [128, TB])
                    RKC = sbt(es, "RKC", [128, 4]); TMPC = sbt(es, "TMPC", [128, 4])
                    COS = sbt(es, "COS", [64, TB]); SINS = sbt(es, "SINS", [64, TB])
                    COSR = sbt(es, "COSR", [64, TB]); SINR = sbt(es, "SINR", [64, TB])
                    T1 = sbt(es, "T1", [64, TB]); T2 = sbt(es, "T2", [64, TB])
                    YY = sbt(es, "YY", [64, TB])
                    YF = sbt(es, "YF", [64, TB])
                    QN = sbt(es, "QN", [128, 4, TB], BF16); QR = sbt(es, "QR", [128, 4, TB], BF16)
                    PT = [sbt(es, "PT%d" % i, [128, TB], BF16) for i in range(3)]
                    RD = sbt(es, "RD", [128, TB])
                    load_mla_weights(WA, WQB, WKK, WKV)
                    S.op("pool", lambda e: e.memset(KR[64:128, :], 0.0), (), ["KRpad"])
                    S.op("pool", lambda e: e.memset(QR[64:128, :, :], 0.0), (), ["QRpad"])
                    WAK = [("WA", i) for i in range(4)]
                    WQK = [("WQB", i) for i in range(9)]
                    SC_ATT = float(192 ** -0.5)
                    for j in range(NBLK):
                        T0 = j * TB
                        xk = xt_keys(j)
                        S.dma("sp", lambda e, T0=T0, s=s: e.dma_start(out=POSI[:], in_=D["pos"][s:s + 1, T0:T0 + TB].partition_broadcast(64)),
                              writes=["POSI"], semkey="POSI")
                        CP("dve", YF[:], POSI[:], ["POSI"], ["PF"])
                        for (tab, col, tk) in ((COS, 2, "COS"), (SINS, 1, "SINS")):
                            TS("dve", YY[:], YF[:], ROPEC[:, 0:1], ROPEC[:, col:col + 1], ALU.mult, ALU.add, ["PF", "ROPEC"], ["YY"])
                            CP("dve", YI[:], YY[:], ["YY"], ["YI"])
                            CP("dve", T1[:], YI[:], ["YI"], ["T1"])
                            TT("dve", YY[:], YY[:], T1[:], ALU.subtract, ["YY", "T1"], ["YY"])
                            TS("dve", T2[:], YY[:], 0.5, None, ALU.is_gt, None, ["YY"], ["T2"])
                            TT("dve", YY[:], YY[:], T2[:], ALU.subtract, ["YY", "T2"], ["YY"])
                            ACT(tab[:], YY[:], AF.Sin, ["YY"], [tk], scale=float(2 * np.pi * (1 - 1e-6)))
                        chk(20)
                        for m in range(3):
                            pk, pt = pnext()
                            MMG(pt[:, :], [(WA[:, k, m * 128:(m + 1) * 128], XT[:, k, T0:T0 + TB]) for k in range(8)], WAK + xk, pk)
                            ACT(SQ[m][:], pt[:, :], AF.Square, [pk], [("SQ", m)])
                            TS("dve", CQ[:, m, :], pt[:, :], QAG[:, m:m + 1], None, ALU.mult, None, [pk, "QAG"], [("CQ", m)])
                        for m in range(2):
                            pk, pt = pnext()
                            MMG(pt[:, :], [(WA[:, k, 384 + m * 128:384 + (m + 1) * 128], XT[:, k, T0:T0 + TB]) for k in range(8)], WAK + xk, pk)
                            ACT(SQ[3 + m][:], pt[:, :], AF.Square, [pk], [("SQ", 3 + m)])
                            TS("dve", CKV[:, m, :], pt[:, :], KVAG[:, m:m + 1], None, ALU.mult, None, [pk, "KVAG"], [("CKV", m)])
                        pss, pst = pnext()
                        for m in range(3):
                            S.op("pe", lambda e, m=m, pst=pst: e.matmul(pst[:, :], lhsT=ONESB[:], rhs=SQ[m][:], start=(m == 0), stop=(m == 2)),
                                 [("SQ", m), "ONESB"], [pss], inc=(m == 2))
                        RSQRT(RQ[:], pst[:, :], 1.0 / 384, TMP[:], [pss], ["RQ"], "TMP")
                        pss, pst = pnext()
                        for m in range(2):
                            S.op("pe", lambda e, m=m, pst=pst: e.matmul(pst[:, :], lhsT=ONESB[:], rhs=SQ[3 + m][:], start=(m == 0), stop=(m == 1)),
                                 [("SQ", 3 + m), "ONESB"], [pss], inc=(m == 1))
                        pcs, pct = pnext()
                        for tt in range(4):
                            for m in range(2):
                                S.op("pe", lambda e, m=m, tt=tt, pct=pct: e.matmul(pct[:, tt:tt + 1], lhsT=SQ[3 + m][:, tt * 128:(tt + 1) * 128], rhs=ONESB[:, 0:1],
                                                                                  start=(m == 0), stop=(m == 1)),
                                     [("SQ", 3), ("SQ", 4)], [pcs], inc=(tt == 3 and m == 1))
                        RSQRT(RKV[:], pst[:, :], 1.0 / 256, TMPK[:], [pss], ["RKV"], "TMPK")
                        RSQRT(RKC[:], pct[:, 0:4], 1.0 / 256, TMPC[:], [pcs], ["RKC"], "TMPC")
                        chk(22)
                        pk1, pt1 = pnext()
                        MMG(pt1[0:64, :], [(WA[:, k, 640:704], XT[:, k, T0:T0 + TB]) for k in range(8)], WAK + xk, pk1)
                        pk2, pt2 = pnext()
                        MMG(pt2[0:64, :], [(WA[:, k, 704:768], XT[:, k, T0:T0 + TB]) for k in range(8)], WAK + xk, pk2)
                        TT("dve", T1[:], pt1[0:64, :], COS[:], ALU.mult, [pk1, "COS"], ["T1"])
                        TT("dve", T2[:], pt2[0:64, :], SINS[:], ALU.mult, [pk2, "SINS"], ["T2"])
                        TT("pool", KR[0:64, T0:T0 + TB], T1[:], T2[:], ALU.add, ["T1", "T2"], [("KR", j)])
                        chk(23)
                        TT("pool", COSR[:], COS[:], RQ[0:64, :], ALU.mult, ["COS", "RQ"], ["COSR"])
                        TT("pool", SINR[:], SINS[:], RQ[0:64, :], ALU.mult, ["SINS", "RQ"], ["SINR"])
                        cqk = [("CQ", m) for m in range(3)]
                        for h in range(4):
                            pk, pt = pnext()
                            MMG(pt[:, :], [(WQB[:, m, h * 192:h * 192 + 128], CQ[:, m, :]) for m in range(3)], WQK + cqk, pk)
                            TT("dve", QN[:, h, :], pt[:, :], RQ[:], ALU.mult, [pk, "RQ"], [("QN", h)])
                            pk1, pt1 = pnext()
                            MMG(pt1[0:64, :], [(WQB[:, m, h * 192 + 128:h * 192 + 192], CQ[:, m, :]) for m in range(3)], WQK + cqk, pk1)
                            pk2, pt2 = pnext()
                            MMG(pt2[0:64, :], [(WQB[:, m, 768 + h * 64:768 + (h + 1) * 64], CQ[:, m, :]) for m in range(3)], WQK + cqk, pk2)
                            TT("dve", T1[:], pt1[0:64, :], COSR[:], ALU.mult, [pk1, "COSR"], ["T1"])
                            TT("dve", T2[:], pt2[0:64, :], SINR[:], ALU.mult, [pk2, "SINR"], ["T2"])
                            TT("pool", QR[0:64, h, :], T1[:], T2[:], ALU.add, ["T1", "T2"], [("QR", h)])
                        chk(24)
                        ckk = [("CKV", m) for m in range(2)]
                        for h in range(4):
                            pk, pt = pnext()
                            MMG(pt[:, :], [(WKK[:, m, h * 128:(h + 1) * 128], CKV[:, m, :]) for m in range(2)], ["WKK"] + ckk, pk)
                            TT("dve", KN[:, h, T0:T0 + TB], pt[:, :], RKV[:], ALU.mult, [pk, "RKV"], [("KN", h, j)])
                        for tt in range(4):
                            pk, pt = pnext()
                            MMG(pt[:, :], [(CKV[:, m, tt * 128:(tt + 1) * 128], WKV[:, m, :]) for m in range(2)], ["WKV"] + ckk, pk)
                            TS("dve", V[:, 4 * j + tt, :], pt[:, :], RKC[:, tt:tt + 1], None, ALU.mult, None, [pk, "RKC"], [("V", 4 * j + tt)])
                        chk(25)
                        rot["banks"] = [0, 1, 2]
                        nkt = 4 * j + 4
                        items = [(h, kt) for h in range(4) for kt in range(nkt)]
                        LA = 2
                        pend = {}
                        accs = {}
                        fin = []

                        def emit_S(idx):
                            h, kt = items[idx]
                            r = kt - 4 * j
                            qo = max(r, 0) * 128
                            N = TB - qo
                            pk, pt = pnext()
                            kb = kt // 4
                            MMG(pt[:, 0:N], [(KN[:, h, kt * 128:(kt + 1) * 128], QN[:, h, qo:TB]),
                                             (KR[:, kt * 128:(kt + 1) * 128], QR[:, h, qo:TB])],
                                [("KN", h, kb), ("KR", kb), ("QN", h), ("QR", h), "KRpad", "QRpad"], pk)
                            pend[idx] = (pk, pt, r, qo, N)

                        for idx in range(min(LA, len(items))):
                            emit_S(idx)
                        for idx in range(len(items)):
                            if idx + LA < len(items):
                                emit_S(idx + LA)
                            if j + 1 < NBLK and idx < 16:
                                if idx % 4 == 0:
                                    xtile_a(4 * (j + 1) + idx // 4)
                                elif idx % 4 == 3:
                                    xtile_b(4 * (j + 1) + idx // 4)
                            h, kt = items[idx]
                            if kt == 0:
                                accs[h] = pacc()
                            (pok, pot), (pdk, pdt) = accs[h]
                            pk, pt, r, qo, N = pend.pop(idx)
                            P_ = PT[idx % 3]
                            ptk = ("PT", idx % 3)
                            ACT(P_[:, 0:N], pt[:, 0:N], AF.Exp, [pk], [ptk], scale=SC_ATT)
                            if r >= 0:
                                TT("pool", P_[:, 0:128], P_[:, 0:128], CMASK[:], ALU.mult, [ptk, "CMASK"], [ptk])
                            first = (kt == 0)
                            last = (kt == nkt - 1)
                            S.op("pe", lambda e, pot=pot, kt=kt, h=h, P_=P_, qo=qo, N=N, first=first, last=last:
                                 e.matmul(pot[:, qo:TB], lhsT=V[:, kt, h * 128:(h + 1) * 128], rhs=P_[:, 0:N], start=first, stop=last),
                                 [("V", kt), ptk], [pok], inc=False)
                            S.op("pe", lambda e, pdt=pdt, P_=P_, qo=qo, N=N, first=first, last=last:
                                 e.matmul(pdt[:, qo:TB], lhsT=ONESB[:], rhs=P_[:, 0:N], start=first, stop=last),
                                 [ptk, "ONESB"], [pdk], inc=True)
                            if last:
                                fin.append((idx + 3, h, pok, pot, pdk, pdt))
                            while fin and (fin[0][0] <= idx or idx == len(items) - 1):
                                _, fh, fpok, fpot, fpdk, fpdt = fin.pop(0)
                                RECIP(RD[:], fpdt[:, :], [fpdk], ["RD"])
                                TT("dve", MIX[:, fh, T0:T0 + TB], fpot[:, :], RD[:], ALU.mult, [fpok, "RD"], [("MIX", fh, j)])
                        rot["banks"] = list(range(7))
                    S.barrier()
                    if stop == 2:
                        raise _Stop()

                with ExitStack() as es:
                    arena["off"] = 0
                    WG = sbt(es, "WG", [128, 8, 1544], BF16)
                    HALO = sbt(es, "HALO", [128, 12, 3])
                    RAW = [sbt(es, "RAW%d" % i, [128, 515]) for i in range(2)]
                    ACCD = [sbt(es, "ACCD%d" % i, [128, TB]) for i in range(2)]
                    SL = [sbt(es, "SL%d" % i, [128, TB], BF16) for i in range(2)]
                    SQg = [sbt(es, "SQg%d" % i, [128, TB], BF16) for i in range(2)]
                    RNM = sbt(es, "RNM", [128, TB])
                    GQ2 = [sbt(es, "GQ%d" % i, [128, 4, TB], BF16) for i in range(2)]
                    GK2 = [sbt(es, "GK%d" % i, [128, 4, TB], BF16) for i in range(2)]
                    GV2 = [sbt(es, "GV%d" % i, [128, 4, TB], BF16) for i in range(2)]
                    SCN = ["AB", "BETA", "NBETA", "Z", "AZ", "MX", "E", "L", "G", "GC", "GL", "EG", "ED", "EGL"]
                    SCT = [{n: sbt(es, "%s%d" % (n, i), [128, 32] if n == "AB" else [128, 16]) for n in SCN} for i in range(2)]
                    mixflat = MIX[:, 8:12, :].rearrange("p a b -> p (a b)")
                    moff = [0]

                    def mcarve(dt):
                        n = 512 if dt == BF16 else 1024
                        ap = mixflat[:, moff[0]:moff[0] + n]
                        moff[0] += n
                        assert moff[0] <= 8192
                        return ap if dt == BF16 else ap.bitcast(dt)

                    def iset(i):
                        mk = (lambda nm, dt: sbt(es, nm + "0", [128, 512], dt)) if i == 0 else (lambda nm, dt: mcarve(dt))
                        d = {}
                        d["KEG"] = mk("KEG", BF16); d["VT"] = mk("VT", BF16); d["DINC"] = mk("DINC", F32); d["DSB"] = mk("DSB", F32)
                        d["EGB"] = mk("EGB", BF16)
                        d["ZZ"] = [mk("ZZa", BF16), mk("ZZb", BF16)]
                        d["ZT"] = [mk("ZTa", BF16), mk("ZTb", BF16)]
                        d["PP"] = [mk("PPa", BF16), mk("PPb", BF16)]
                        return d
                    ISET = [iset(0), iset(1)]
                    KD = [sbt(es, "KD%d" % i, [128, 512], BF16) for i in range(3)]
                    AT = [sbt(es, "AT%d" % i, [128, 512], BF16) for i in range(3)]
                    QG = [sbt(es, "QG%d" % i, [128, 512], BF16) for i in range(3)]
                    WT = [sbt(es, "WT%d" % i, [128, 512], BF16) for i in range(3)]
                    UU = [sbt(es, "UU%d" % i, [128, 512]) for i in range(3)]
                    TV = sbt(es, "TV", [128, 512]); VNEW = sbt(es, "VNEW", [128, 512], BF16)
                    ST = sbt(es, "ST", [128, 512]); STT_ = sbt(es, "STT", [128, 512]); STB = sbt(es, "STB", [128, 512], BF16)
                    OSB = sbt(es, "OSB", [128, 512]); SQO = sbt(es, "SQO", [128, 512], BF16)
                    RN = sbt(es, "RN", [128, 512]); TMP2 = sbt(es, "TMP2", [128, 512])

                    S.dma("pool", lambda e: e.dma_start(out=WG[:, :, 1536:1544], in_=w_in_v[:, :, 2240:2248]), writes=[("WG", "ab")], semkey=("WG", 3))
                    for g3 in range(3):
                        S.dma("pool", lambda e, g3=g3: e.dma_start(out=WG[:, :, g3 * 512:(g3 + 1) * 512], in_=w_in_v[:, :, 704 + g3 * 512:704 + (g3 + 1) * 512]),
                              writes=[("WG", g3)], semkey=("WG", g3))
                    S.op("pool", lambda e: e.memset(HALO[:], 0.0), (), ["HALO"])
                    S.op("pool", lambda e: e.memset(ST[:], 0.0), (), ["ST"])
                    S.op("pool", lambda e: e.memset(STB[:], 0.0), (), ["STB"])

                    def h4(ap):
                        return ap.rearrange("p (h i) -> p h i", h=4)

                    def bc4(ap4):
                        return ap4.unsqueeze(2).to_broadcast([128, 4, 128])

                    def blockprep(j):
                        T0 = j * TB
                        xk = xt_keys(j)
                        bp = j % 2
                        GQ, GK, GV = GQ2[bp], GK2[bp], GV2[bp]
                        sc = SCT[j % 2]
                        K = lambda n: (n, j % 2)
                        pk, pt = pnext()
                        for c in range(4):
                            MMG(pt[:, c * 8:(c + 1) * 8], [(XT[:, k, T0 + c * 128:T0 + (c + 1) * 128], WG[:, k, 1536:1544]) for k in range(8)],
                                [("WG", "ab")] + xk, pk, last_inc=(c == 3))
                        CP("act", sc["AB"][:], pt[:, 0:32], [pk], [K("AB")])
                        ab3 = sc["AB"][:].rearrange("p (c e) -> p c e", c=4)
                        v3 = lambda n: sc[n][:].rearrange("p (c h) -> p c h", c=4)
                        ACT(v3("BETA"), ab3[:, :, 4:8], AF.Sigmoid, [K("AB")], [K("BETA")])
                        TS("pool", sc["NBETA"][:], sc["BETA"][:], -1.0, None, ALU.mult, None, [K("BETA")], [K("NBETA")])
                        TT("dve", v3("Z"), ab3[:, :, 0:4], DTB[:].unsqueeze(1).to_broadcast([128, 4, 4]), ALU.add, [K("AB"), "DTB"], [K("Z")])
                        TS("dve", sc["AZ"][:], sc["Z"][:], -1.0, None, ALU.mult, None, [K("Z")], [K("AZ")])
                        TT("dve", sc["AZ"][:], sc["AZ"][:], sc["Z"][:], ALU.max, [K("AZ"), K("Z")], [K("AZ")])
                        ACT(sc["E"][:], sc["AZ"][:], AF.Exp, [K("AZ")], [K("E")], scale=-1.0)
                        ACT(sc["L"][:], sc["E"][:], AF.Ln, [K("E")], [K("L")], bias=1.0)
                        TS("dve", sc["MX"][:], sc["Z"][:], 0.0, None, ALU.max, None, [K("Z")], [K("MX")])
                        TT("dve", sc["L"][:], sc["L"][:], sc["MX"][:], ALU.add, [K("L"), K("MX")], [K("L")])
                        TT("dve", v3("G"), sc["L"][:].rearrange("p (c h) -> p c h", c=4), NEGA[:].unsqueeze(1).to_broadcast([128, 4, 4]), ALU.mult,
                           [K("L"), "NEGA"], [K("G")])
                        pk, pt = pnext()
                        S.op("pe", lambda e, pt=pt: e.matmul(pt[:, 0:16], lhsT=TRI[:], rhs=sc["G"][:], start=True, stop=True), [K("G"), "TRI"], [pk])
                        CP("dve", sc["GC"][:], pt[:, 0:16], [pk], [K("GC")])
                        pk, pt = pnext()
                        S.op("pe", lambda e, pt=pt: e.matmul(pt[:, 0:16], lhsT=ONESF[:], rhs=sc["G"][:], start=True, stop=True), [K("G"), "ONESF"], [pk])
                        CP("dve", sc["GL"][:], pt[:, 0:16], [pk], [K("GL")])
                        ACT(sc["EG"][:], sc["GC"][:], AF.Exp, [K("GC")], [K("EG")])
                        ACT(sc["EGL"][:], sc["GL"][:], AF.Exp, [K("GL")], [K("EGL")])
                        TT("dve", sc["ED"][:], sc["GL"][:], sc["GC"][:], ALU.subtract, [K("GL"), K("GC")], [K("ED")])
                        ACT(sc["ED"][:], sc["ED"][:], AF.Exp, [K("ED")], [K("ED")])
                        yield
                        def finish_norm(ch, h, sl, slk, sq, sqk):
                            pk, pt = pnext()
                            S.op("pe", lambda e, pt=pt, sq=sq: e.matmul(pt[:, :], lhsT=ONESB[:], rhs=sq[:], start=True, stop=True), [sqk, "ONESB"], [pk])
                            RSQRT(RNM[:], pt[:, :], 1.0, RNM[:], [pk], ["RNM"], "RNM")
                            if ch < 4:
                                STT("dve", GQ[:, h, :], sl[:], float(128 ** -0.5), RNM[:], ALU.mult, ALU.mult, [slk, "RNM"], [("GQ", bp, h)])
                            else:
                                TT("dve", GK[:, h, :], sl[:], RNM[:], ALU.mult, [slk, "RNM"], [("GK", bp, h)])

                        pend_norm = None
                        for ch in range(12):
                            raw = RAW[ch % 2]
                            rk = ("RAW", ch % 2)
                            pk, pt = pnext()
                            MMG(pt[:, :], [(WG[:, k, ch * 128:(ch + 1) * 128], XT[:, k, T0:T0 + TB]) for k in range(8)], [("WG", ch // 4)] + xk, pk)
                            CP("pool", raw[:, 0:3], HALO[:, ch, :], ["HALO"], [rk + (0,)])
                            CP("act", raw[:, 3:515], pt[:, :], [pk], [rk + (1,)])
                            CP("pool", HALO[:, ch, :], raw[:, 512:515], [rk + (1,)], ["HALO"])
                            acc = ACCD[ch % 2]
                            ak = ("ACCD", ch % 2)
                            rr = [rk + (0,), rk + (1,), "CONVW"]
                            TS("dve", acc[:], pt[:, :], CONVW[:, ch * 4 + 3:ch * 4 + 4], None, ALU.mult, None, [pk, "CONVW"], [ak])
                            for jt in (2, 1, 0):
                                STT("dve", acc[:], raw[:, jt:jt + TB], CONVW[:, ch * 4 + jt:ch * 4 + jt + 1], acc[:], ALU.mult, ALU.add, rr + [ak], [ak])
                            h = ch % 4
                            if pend_norm is not None:
                                finish_norm(*pend_norm)
                                pend_norm = None
                            if ch >= 8:
                                ACT(GV[:, h, :], acc[:], AF.Silu, [ak], [("GV", bp, h)])
                            else:
                                sl = SL[ch % 2]
                                slk = ("SL", ch % 2)
                                sq = SQg[ch % 2]
                                sqk = ("SQg", ch % 2)
                                ACT(sl[:], acc[:], AF.Silu, [ak], [slk])
                                TT("pool", sq[:], sl[:], sl[:], ALU.mult, [slk], [sqk])
                                pend_norm = (ch, h, sl, slk, sq, sqk)
                            yield

                    def prep(cc):
                        j, c = divmod(cc, 4)
                        sc = SCT[j % 2]
                        K = lambda n: (n, j % 2)
                        si = cc % 2
                        I_ = ISET[si]
                        IK = lambda n: (n, "set", si)
                        par = cc % 3
                        KEG, VT, DINC, DSB, EGB, ZZ, ZT, PP = I_["KEG"], I_["VT"], I_["DINC"], I_["DSB"], I_["EGB"], I_["ZZ"], I_["ZT"], I_["PP"]
                        cs = slice(c * 128, (c + 1) * 128)
                        col4 = lambda n: sc[n][:, c * 4:(c + 1) * 4]
                        bp = j % 2
                        GQ, GK, GV = GQ2[bp], GK2[bp], GV2[bp]
                        gkk = [("GK", bp, h) for h in range(4)]
                        gqk = [("GQ", bp, h) for h in range(4)]
                        gvk = [("GV", bp, h) for h in range(4)]
                        for h in range(4):
                            TR(PB[:, h * 128:(h + 1) * 128], GK[:, h, cs], IDB[:], gkk + ["IDB"] if h == 0 else (), "PB", inc=False)
                        for h in range(4):
                            TR(PB[:, 512 + h * 128:512 + (h + 1) * 128], GV[:, h, cs], IDB[:], gvk if h == 0 else (), "PB", inc=(h == 3))
                        TT("dve", h4(KEG[:]), h4(PB[:, 0:512]), bc4(col4("EG")), ALU.mult, ["PB", K("EG")], [IK("KEG")])
                        TT("dve", h4(KD[par][:]), h4(PB[:, 0:512]), bc4(col4("ED")), ALU.mult, ["PB", K("ED")], [("KD", par)])
                        CP("act", VT[:], PB[:, 512:1024], ["PB"], [IK("VT")])
                        yield
                        pgk, pgt = pnext()
                        for h in range(4):
                            S.op("pe", lambda e, h=h, pgt=pgt: e.matmul(pgt[:, h * 128:(h + 1) * 128],
                                                                        lhsT=sc["G"][:, c * 4 + h:c * 4 + h + 1].to_broadcast([128, 128]), rhs=TRI[:],
                                                                        start=True, stop=True),
                                 [K("G"), "TRI"] if h == 0 else (), [pgk], inc=(h == 3))
                        TT("dve", h4(DINC[:]), h4(pgt[:, :]), bc4(col4("GC")), ALU.subtract, [pgk, K("GC")], [IK("DINC")])
                        ACT(EGB[:], pgt[:, :], AF.Exp, [pgk], [IK("EGB")])
                        TT("pool", h4(DINC[:]), h4(DINC[:]), MINC[:].unsqueeze(1).to_broadcast([128, 4, 128]), ALU.add, [IK("DINC"), "MINC"], [IK("DINC")])
                        ACT(DINC[:], DINC[:], AF.Exp, [IK("DINC")], [IK("DINC")])
                        TT("pool", h4(DSB[:]), h4(DINC[:]), IDF[:].unsqueeze(1).to_broadcast([128, 4, 128]), ALU.subtract, [IK("DINC"), "IDF"], [IK("DSB")])
                        TT("pool", h4(DSB[:]), h4(DSB[:]), bc4(col4("NBETA")), ALU.mult, [IK("DSB"), K("NBETA")], [IK("DSB")])
                        TT("pool", h4(QG[par][:]), GQ[:, :, cs], h4(EGB[:]), ALU.mult, gqk + [IK("EGB")], [("QG", par)])
                        yield
                        pk, pt = pnext()
                        for h in range(4):
                            S.op("pe", lambda e, h=h, pt=pt: e.matmul(pt[:, h * 128:(h + 1) * 128], lhsT=GK[:, h, cs], rhs=GK[:, h, cs], start=True, stop=True),
                                 gkk if h == 0 else (), [pk], inc=(h == 3))
                        TT("dve", ZZ[0][:], pt[:, :], DSB[:], ALU.mult, [pk, IK("DSB")], [IK("ZZ0")])
                        pk, pt = pnext()
                        for h in range(4):
                            S.op("pe", lambda e, h=h, pt=pt: e.matmul(pt[:, h * 128:(h + 1) * 128], lhsT=GK[:, h, cs], rhs=GQ[:, h, cs], start=True, stop=True),
                                 gkk + gqk if h == 0 else (), [pk], inc=(h == 3))
                        TT("dve", AT[par][:], pt[:, :], DINC[:], ALU.mult, [pk, IK("DINC")], [("AT", par)])
                        TT("pool", h4(PP[0][:]), h4(ZZ[0][:]), IDB[:].unsqueeze(1).to_broadcast([128, 4, 128]), ALU.add, [IK("ZZ0"), "IDB"], [IK("PP0")])
                        yield
                        for h in range(4):
                            TR(PB[:, h * 128:(h + 1) * 128], ZZ[0][:, h * 128:(h + 1) * 128], IDB[:], [IK("ZZ0"), "IDB"] if h == 0 else (), "PB", inc=(h == 3))
                        CP("act", ZT[0][:], PB[:, 0:512], ["PB"], [IK("ZT0")])
                        yield
                        def pupd(lv, zt_i):
                            pcur = (lv - 1) % 2
                            pnx = 1 - pcur
                            pkp, ptp = pnext()
                            for h in range(4):
                                hs = slice(h * 128, (h + 1) * 128)
                                S.op("pe", lambda e, hs=hs, ptp=ptp, pcur=pcur, zt_i=zt_i: e.matmul(ptp[:, hs], lhsT=ZT[zt_i][:, hs], rhs=PP[pcur][:, hs], start=True, stop=True),
                                     [IK("ZT%d" % zt_i), IK("PP%d" % pcur)] if h == 0 else (), [pkp], inc=(h == 3))
                            return pkp, ptp, pcur, pnx

                        cur = 0
                        for lv in range(1, 7):
                            nx = 1 - cur
                            zk, ztk = IK("ZZ%d" % cur), IK("ZT%d" % cur)
                            pkt, ptt = pnext()
                            for h in range(4):
                                hs = slice(h * 128, (h + 1) * 128)
                                S.op("pe", lambda e, hs=hs, ptt=ptt, cur=cur: e.matmul(ptt[:, hs], lhsT=ZZ[cur][:, hs], rhs=ZT[cur][:, hs], start=True, stop=True),
                                     [zk, ztk] if h == 0 else (), [pkt], inc=(h == 3))
                            if lv < 6:
                                pkz, ptz = pnext()
                                for h in range(4):
                                    hs = slice(h * 128, (h + 1) * 128)
                                    S.op("pe", lambda e, hs=hs, ptz=ptz, cur=cur: e.matmul(ptz[:, hs], lhsT=ZT[cur][:, hs], rhs=ZZ[cur][:, hs], start=True, stop=True),
                                         [zk, ztk] if h == 0 else (), [pkz], inc=(h == 3))
                            pu = pupd(lv - 1, cur) if lv >= 2 else None
                            CP("act", ZT[nx][:], ptt[:, :], [pkt], [IK("ZT%d" % nx)])
                            if lv < 6:
                                CP("dve", ZZ[nx][:], ptz[:, :], [pkz], [IK("ZZ%d" % nx)])
                            if pu is not None:
                                pkp, ptp, pcur, pnx = pu
                                TT("dve", PP[pnx][:], ptp[:, :], PP[pcur][:], ALU.add, [pkp, IK("PP%d" % pcur)], [IK("PP%d" % pnx)])
                            cur = nx
                            yield
                        pkp, ptp, pcur, pnx = pupd(6, cur)
                        TT("dve", PP[pnx][:], ptp[:, :], PP[pcur][:], ALU.add, [pkp, IK("PP%d" % pcur)], [IK("PP%d" % pnx)])
                        yield
                        TTm = PP[0]
                        tk = IK("PP0")
                        pk, pt = pnext()
                        for h in range(4):
                            hs = slice(h * 128, (h + 1) * 128)
                            S.op("pe", lambda e, hs=hs, pt=pt: e.matmul(pt[:, hs], lhsT=KEG[:, hs], rhs=TTm[:, hs], start=True, stop=True),
                                 [IK("KEG"), tk] if h == 0 else (), [pk], inc=(h == 3))
                        CP("act", WT[par][:], pt[:, :], [pk], [("WT", par)])
                        pk, pt = pnext()
                        for h in range(4):
                            hs = slice(h * 128, (h + 1) * 128)
                            S.op("pe", lambda e, hs=hs, pt=pt: e.matmul(pt[:, hs], lhsT=TTm[:, hs], rhs=VT[:, hs], start=True, stop=True),
                                 [IK("VT"), tk] if h == 0 else (), [pk], inc=(h == 3))
                        CP("dve", UU[par][:], pt[:, :], [pk], [("UU", par)])
                        yield

                    def seq(cc):
                        j, c = divmod(cc, 4)
                        sc = SCT[j % 2]
                        K = lambda n: (n, j % 2)
                        par = cc % 3
                        C0 = cc * 128
                        col4 = lambda n: sc[n][:, c * 4:(c + 1) * 4]
                        pk, pt = pnext()
                        for h in range(4):
                            hs = slice(h * 128, (h + 1) * 128)
                            S.op("pe", lambda e, hs=hs, pt=pt: e.matmul(pt[:, hs], lhsT=WT[par][:, hs], rhs=STB[:, hs], start=True, stop=True),
                                 [("WT", par), "STB"] if h == 0 else (), [pk], inc=(h == 3))
                        TT("dve", TV[:], UU[par][:], pt[:, :], ALU.subtract, [("UU", par), pk], ["TV"])
                        TT("pool", h4(VNEW[:]), h4(TV[:]), bc4(col4("BETA")), ALU.mult, ["TV", K("BETA")], ["VNEW"])
                        yield
                        pok, pot = pnext()
                        for h in range(4):
                            hs = slice(h * 128, (h + 1) * 128)
                            S.op("pe", lambda e, hs=hs, pot=pot: e.matmul(pot[:, hs], lhsT=STB[:, hs], rhs=QG[par][:, hs], start=True, stop=False),
                                 ["STB", ("QG", par)] if h == 0 else (), [pok], inc=False)
                            S.op("pe", lambda e, hs=hs, pot=pot: e.matmul(pot[:, hs], lhsT=VNEW[:, hs], rhs=AT[par][:, hs], start=False, stop=True),
                                 ["VNEW", ("AT", par)] if h == 0 else (), [pok], inc=(h == 3))
                        pk, pt = pnext()
                        for h in range(4):
                            hs = slice(h * 128, (h + 1) * 128)
                            S.op("pe", lambda e, hs=hs, pt=pt: e.matmul(pt[:, hs], lhsT=KD[par][:, hs], rhs=VNEW[:, hs], start=True, stop=True),
                                 [("KD", par), "VNEW"] if h == 0 else (), [pk], inc=(h == 3))
                        TT("pool", h4(STT_[:]), h4(ST[:]), bc4(col4("EGL")), ALU.mult, ["ST", K("EGL")], ["STT"])
                        TT("dve", ST[:], STT_[:], pt[:, :], ALU.add, ["STT", pk], ["ST"])
                        CP("act", STB[:], ST[:], ["ST"], ["STB"])
                        ACT(SQO[:], pot[:, :], AF.Square, [pok], ["SQO"])
                        CP("dve", OSB[:], pot[:, :], [pok], ["OSB"])
                        yield
                        pk, pt = pnext()
                        S.op("pe", lambda e, pt=pt: e.matmul(pt[:, :], lhsT=ONESB[:], rhs=SQO[:], start=True, stop=True), ["SQO", "ONESB"], [pk])
                        RSQRT(RN[:], pt[:, :], 1.0 / 128, TMP2[:], [pk], ["RN"], "TMP2")
                        STT("dve", MIX[:, 4:8, C0:C0 + 128], h4(OSB[:]), GDNG[:, 0:1], h4(RN[:]), ALU.mult, ALU.mult, ["OSB", "RN", "GDNG"],
                            [("MIX", 4 + h, cc // 4) for h in range(4)])
                        yield

                    done = set()
                    active = {}
                    started = set()

                    def ready(t):
                        kind, i = t
                        if kind == "B":
                            return (i == 0) or ((("B", i - 1) in done) and (i < 2 or ("P", 4 * i - 5) in done) and (("Q", 4 * i - 5) in done or 4 * i - 5 < 0))
                        if kind == "P":
                            return (("B", i // 4) in done) and (i < 2 or ("P", i - 2) in done) and (i < 3 or ("Q", i - 3) in done)
                        return (("P", i) in done) and (i == 0 or ("Q", i - 1) in done)

                    tasks = [("B", j) for j in range(4)] + [("P", cc) for cc in range(16)] + [("Q", cc) for cc in range(16)]
                    mk = {"B": blockprep, "P": prep, "Q": seq}
                    while len(done) < len(tasks):
                        for t in tasks:
                            if t not in started and ready(t):
                                started.add(t)
                                active[t] = mk[t[0]](t[1])
                        assert active, "GDN scheduler stalled"
                        for t in list(active.keys()):
                            try:
                                next(active[t])
                            except StopIteration:
                                del active[t]
                                done.add(t)
                    arena_use["gdn"] = arena["off"]
                    S.barrier()
                    if stop == 3:
                        raise _Stop()

                with ExitStack() as es:
                    arena["off"] = 0
                    WGT = sbt(es, "WGT", [128, 8, 1536], BF16)
                    WOUT = sbt(es, "WOUT", [128, 12, 1024], BF16)
                    GFIN = sbt(es, "GFIN", [128, 1024])
                    WM = sbt(es, "WM", [128, 8, 512], BF16)
                    MQ = sbt(es, "MQ", [128, 4, TB], BF16)
                    PTm = [sbt(es, "PTm%d" % i, [128, TB], BF16) for i in range(3)]
                    RDm = sbt(es, "RDm", [128, TB])
                    S.dma("pool", lambda e: e.dma_start(out=WM[:], in_=w_in_v[:, :, 2248:2760]), writes=["WM"], semkey="WM")
                    for hh in range(2):
                        S.dma("pool", lambda e, hh=hh, WGT=WGT: e.dma_start(out=WGT[:, 4 * hh:4 * hh + 4, :], in_=w_in_v[:, 4 * hh:4 * hh + 4, 2760:4296]),
                              writes=[("WGT", hh)], semkey=("WGT", hh))
                    wo_v = D["w_out"].rearrange("(k p) c -> p k c", p=128)
                    for q in range(3):
                        S.dma("pool", lambda e, q=q, WOUT=WOUT: e.dma_start(out=WOUT[:, 4 * q:4 * q + 4, :], in_=wo_v[:, 4 * q:4 * q + 4, :]),
                              writes=[("WOUT", q)], semkey=("WOUT", q))
                    S.dma("sp", lambda e, GFIN=GFIN: e.dma_start(out=GFIN[:], in_=D["norm_final"].partition_broadcast(128)), writes=["GFIN"], semkey="GFIN")
                    SC_M = float(128 ** -0.5)
                    pti = 0
                    for j in range(NBLK):
                        T0 = j * TB
                        xk = xt_keys(j)
                        for h in range(4):
                            pk, pt = pnext()
                            MMG(pt[:, :], [(WM[:, k, h * 128:(h + 1) * 128], XT[:, k, T0:T0 + TB]) for k in range(8)], ["WM"] + xk, pk)
                            CP("act" if h % 2 else "dve", MQ[:, h, :], pt[:, :], [pk], [("MQ", h)])
                        rot["banks"] = [0, 1, 2]
                        mitems = [(h, mt) for h in range(4) for mt in range(2)]
                        mpend = {}
                        maccs = {}
                        mfin = []

                        def emit_MS(idx):
                            h, mt = mitems[idx]
                            pk, pt = pnext()
                            S.op("pe", lambda e, pt=pt, h=h, mt=mt, s=s: e.matmul(pt[:, :], lhsT=KM[:, s, h, mt * 128:(mt + 1) * 128], rhs=MQ[:, h, :], start=True, stop=True),
                                 [("MQ", h)], [pk])
                            mpend[idx] = (pk, pt)

                        for idx in range(2):
                            emit_MS(idx)
                        for idx in range(len(mitems)):
                            if idx + 2 < len(mitems):
                                emit_MS(idx + 2)
                            h, mt = mitems[idx]
                            if mt == 0:
                                maccs[h] = pacc()
                            (pok, pot), (pdk, pdt) = maccs[h]
                            pk, pt = mpend.pop(idx)
                            P_ = PTm[pti % 3]
                            ptk = ("PTm", pti % 3)
                            pti += 1
                            ACT(P_[:], pt[:, :], AF.Exp, [pk], [ptk], scale=SC_M)
                            S.op("pe", lambda e, pot=pot, h=h, mt=mt, P_=P_, s=s: e.matmul(pot[:, :], lhsT=VM[:, s, mt, h * 128:(h + 1) * 128], rhs=P_[:], start=(mt == 0), stop=(mt == 1)),
                                 [ptk], [pok], inc=False)
                            S.op("pe", lambda e, pdt=pdt, mt=mt, P_=P_: e.matmul(pdt[:, :], lhsT=ONESB[:], rhs=P_[:], start=(mt == 0), stop=(mt == 1)),
                                 [ptk], [pdk], inc=True)
                            if mt == 1:
                                mfin.append((idx + 2, h, pok, pot, pdk, pdt))
                            while mfin and (mfin[0][0] <= idx or idx == len(mitems) - 1):
                                _, fh, fpok, fpot, fpdk, fpdt = mfin.pop(0)
                                RECIP(RDm[:], fpdt[:, :], [fpdk], ["RDm"])
                                TT("dve", MIX[:, 8 + fh, T0:T0 + TB], fpot[:, :], RDm[:], ALU.mult, [fpok, "RDm"], [("MIX", 8 + fh, j)])
                        rot["banks"] = list(range(7))
                    S.barrier()
                    if stop == 4:
                        raise _Stop()

                if dbg is not None and s == 0:
                    with ExitStack() as es:
                        arena["off"] = 96 * 1024
                        DB = sbt(es, "DB", [128, 2048])
                        for ch in range(12):
                            CP("dve", DB[:], MIX[:, ch, :], [], ["DB"])
                            S.dma("sp", lambda e, ch=ch: e.dma_start(out=dbg[:, ch * 2048:(ch + 1) * 2048], in_=DB[:]), reads=["DB"], semkey="DB")
                        S.barrier()
                        if stop == 5:
                            raise _Stop()

                with ExitStack() as es:
                    arena["off"] = 0
                    WGT = sbt(es, "WGT", [128, 8, 1536], BF16)
                    WOUT = sbt(es, "WOUT", [128, 12, 1024], BF16)
                    GFIN = sbt(es, "GFIN", [128, 1024])
                    XL = [sbt(es, "XL%d" % i, [128, 1024]) for i in range(3)]
                    SG = [sbt(es, "SG%d" % i, [128, TB], BF16) for i in range(2)]
                    SCRo = sbt(es, "SCRo", [128, 1024], BF16)
                    SSo = [sbt(es, "SSo%d" % i, [128, 4]) for i in range(2)]
                    WGTK = [("WGT", 0), ("WGT", 1)]
                    WOK = [("WOUT", q) for q in range(3)]
                    ti = 0

                    def gate_block(j):
                        T0 = j * TB
                        xk = xt_keys(j)
                        for ch in range(12):
                            pk, pt = pnext()
                            MMG(pt[:, :], [(WGT[:, k, ch * 128:(ch + 1) * 128], XT[:, k, T0:T0 + TB]) for k in range(8)], WGTK + xk, pk)
                            sg = SG[ch % 2]
                            ACT(sg[:], pt[:, :], AF.Silu, [pk], [("SG", ch % 2)])
                            TT("pool", MIX[:, ch, T0:T0 + TB], MIX[:, ch, T0:T0 + TB], sg[:], ALU.mult, [("SG", ch % 2), ("MIX", ch, j)], [("MIX", ch, j)])

                    gate_block(0)
                    for j in range(NBLK):
                        T0 = j * TB
                        if j + 1 < NBLK:
                            gate_block(j + 1)
                        mk = [("MIX", ch, j) for ch in range(12)]
                        for tt in range(4):
                            sl = ti % 3
                            ti += 1
                            xl = XL[sl]
                            xlk = ("XL", sl)
                            r0 = s * S_LEN + T0 + tt * 128
                            S.dma("sp", lambda e, r0=r0, xl=xl: e.dma_start(out=xl[:], in_=D["x"][r0:r0 + 128, :]), writes=[xlk], semkey=xlk)
                            for half in range(2):
                                pk, pt = pnext()
                                MMG(pt[:, :], [(MIX[:, ch, T0 + tt * 128:T0 + (tt + 1) * 128], WOUT[:, ch, half * 512:(half + 1) * 512]) for ch in range(12)],
                                    mk + WOK, pk)
                                TT("dve", xl[:, half * 512:(half + 1) * 512], xl[:, half * 512:(half + 1) * 512], pt[:, :], ALU.add, [pk, xlk], [xlk])
                            ss = SSo[tt % 2]
                            sk = ("SSo", tt % 2)
                            S.op("pool", lambda e, ss=ss: e.memset(ss[:, 0:1], 0.0), (), [sk + (0,)])
                            ACT(SCRo[:], xl[:], AF.Square, [xlk, sk + (0,)], ["SCRo", sk + (0,)], accum_out=ss[:, 0:1])
                            RSQRT(ss[:, 2:3], ss[:, 0:1], 1.0 / 1024, ss[:, 1:2], [sk + (0,)], [sk + (3,)], sk + (2,))
                            STT("dve", xl[:], xl[:], ss[:, 2:3], GFIN[:], ALU.mult, ALU.mult, [xlk, sk + (3,), "GFIN"], [xlk])
                            S.dma("sp", lambda e, r0=r0, xl=xl: e.dma_start(out=y[r0:r0 + 128, :], in_=xl[:]), reads=[xlk], semkey=("YST", sl))
                    S.barrier()
                    if stop == 6:
                        raise _Stop()

        try:
            _emit()
        except _Stop:
            pass
        for sk, n in list(S.dma_cnt.items()):
            S.wait_event("sp", (sk, 16 * n))
        S.replay()
    return nc


_NC_CACHE = {}
_STOP = None


def _consts():
    ident = np.eye(128, dtype=np.float32)
    tri = np.triu(np.ones((128, 128), dtype=np.float32))
    half = 32
    invf = 1.0 / (10000.0 ** (np.arange(half, dtype=np.float32) / half))
    invf64 = np.concatenate([invf, invf]).astype(np.float64)
    ropec = np.zeros((64, 3), dtype=np.float32)
    ropec[:, 0] = (invf64 / (2 * np.pi)).astype(np.float32)
    ropec[:32, 1] = 0.5
    ropec[32:, 1] = 0.0
    ropec[:, 2] = 0.25
    return ident, tri, ropec


def kernel(x, mem, positions, norm_in, w_in, q_a_norm, w_q_b, kv_a_norm, w_kv_b, gdn_conv,
           gdn_a_log, gdn_dt_bias, gdn_norm, mem_norm, w_mem_kv, w_out, norm_final, _debug=False):
    f = lambda a: np.ascontiguousarray(np.asarray(a, dtype=np.float32))
    x = f(x); mem = f(mem)
    positions = np.ascontiguousarray(np.asarray(positions, dtype=np.int32))
    ident, tri, ropec = _consts()
    shared = {
        "norm_in": f(norm_in).reshape(1, 1024),
        "w_in": f(w_in).reshape(1024, 4296),
        "qag": np.ascontiguousarray(f(q_a_norm).reshape(3, 128).T),
        "w_q_b": f(w_q_b).reshape(384, 768),
        "kvag": np.ascontiguousarray(f(kv_a_norm).reshape(2, 128).T),
        "w_kv_b": f(w_kv_b).reshape(256, 1024),
        "convw": np.ascontiguousarray(f(gdn_conv).reshape(4, 12, 128).transpose(2, 1, 0).reshape(128, 48)),
        "alog": np.ascontiguousarray(np.broadcast_to(f(gdn_a_log).reshape(1, 4), (128, 4))),
        "dtb": np.ascontiguousarray(np.broadcast_to(f(gdn_dt_bias).reshape(1, 4), (128, 4))),
        "gdng": np.ascontiguousarray(f(gdn_norm).reshape(128, 1)),
        "mem_norm": f(mem_norm).reshape(1, 1024),
        "w_mem_kv": f(w_mem_kv).reshape(1024, 1024),
        "w_out": f(w_out).reshape(1536, 1024),
        "norm_final": f(norm_final).reshape(1, 1024),
        "ident": ident, "tri": tri, "ropec": ropec,
    }
    key = bool(_debug)
    if key not in _NC_CACHE:
        _NC_CACHE[key] = build_program(debug=key, stop=_STOP)
    nc = _NC_CACHE[key]
    in_maps = []
    for c in range(8):
        m = dict(shared)
        m["x"] = np.ascontiguousarray(x[2 * c:2 * c + 2].reshape(4096, 1024))
        m["mem"] = np.ascontiguousarray(mem[2 * c:2 * c + 2].reshape(512, 1024))
        m["pos"] = np.ascontiguousarray(positions[2 * c:2 * c + 2])
        in_maps.append(m)
    res = run_bass_kernel_spmd(nc, in_maps, core_ids=list(range(8)))
    out = np.concatenate([np.asarray(r["y"], dtype=np.float32).reshape(2, 2048, 1024) for r in res.results], axis=0)
    if _debug:
        return out, [np.asarray(r["dbg"]) for r in res.results]
    return out
```

```python
import numpy as np
from contextlib import ExitStack
import concourse.bass as bass
import concourse.mybir as mybir
from concourse.bass_utils import run_bass_kernel_spmd

F32 = mybir.dt.float32
BF16 = mybir.dt.bfloat16
I32 = mybir.dt.int32
AF = mybir.ActivationFunctionType
ALU = mybir.AluOpType

EPS = 1e-6
S_LEN = 2048
NBLK = 4
TB = 512


class Sched:
    ENGS = ("pe", "act", "dve", "pool", "sp")

    def __init__(self, nc):
        self.nc = nc
        self.stream = {e: [] for e in self.ENGS}
        self.cnt = {e: 0 for e in self.ENGS}
        self.waited = {e: {} for e in self.ENGS}
        self.lastw = {}
        self.readers = {}
        self.dma_cnt = {}

    def _deps(self, reads, writes):
        deps = []
        for k in reads:
            ev = self.lastw.get(k)
            if ev is not None:
                deps.append((ev, "raw"))
            if k == "PB" or (isinstance(k, tuple) and k[0] == "P"):
                for r in self.readers.get(k, ()):
                    deps.append((r, "war"))
        for k in writes:
            ev = self.lastw.get(k)
            if ev is not None:
                deps.append((ev, "waw"))
            for r in self.readers.get(k, ()):
                deps.append((r, "war"))
        return deps

    def _add_waits(self, eng, deps):
        for (semkey, val), kind in deps:
            if semkey == eng and eng == "pe":
                continue
            if self.waited[eng].get(semkey, 0) >= val:
                continue
            self.waited[eng][semkey] = val
            self.stream[eng].append(("wait", semkey, val))

    def _record(self, ev, reads, writes):
        for k in reads:
            self.readers.setdefault(k, []).append(ev)
        for k in writes:
            self.lastw[k] = ev
            self.readers[k] = []

    def op(self, eng, fn, reads=(), writes=(), inc=True):
        self._add_waits(eng, self._deps(reads, writes))
        if inc:
            self.cnt[eng] += 1
            ev = (eng, self.cnt[eng])
        else:
            ev = (eng, self.cnt[eng] + 1)
        self.stream[eng].append(("op", fn, inc))
        self._record(ev, reads, writes)
        return ev

    def dma(self, eng, fn, reads=(), writes=(), semkey=None):
        sk = ("dma", semkey)
        n = self.dma_cnt.get(sk, 0)
        deps = self._deps(reads, writes)
        if n > 0:
            deps.append(((sk, 16 * n), "raw"))
        self._add_waits(eng, deps)
        n += 1
        self.dma_cnt[sk] = n
        ev = (sk, 16 * n)
        self.stream[eng].append(("dma", fn, sk))
        self._record(ev, reads, writes)
        return ev

    def wait_event(self, eng, ev):
        self._add_waits(eng, [(ev, "raw")])

    def barrier(self):
        for e in self.ENGS:
            for f in self.ENGS:
                if f != e and self.cnt[f] > 0:
                    self.wait_event(e, (f, self.cnt[f]))
            for sk, n in self.dma_cnt.items():
                self.wait_event(e, (sk, 16 * n))
        self.lastw.clear()
        self.readers.clear()

    def replay(self):
        nc = self.nc
        with ExitStack() as es:
            sems = {}
            for e in self.ENGS:
                sems[e] = es.enter_context(nc.semaphore("s_" + e))
            for i, sk in enumerate(self.dma_cnt):
                sems[sk] = es.enter_context(nc.semaphore("d%d" % i))
            block = es.enter_context(nc.Block())
            reg = {"pe": block.tensor, "act": block.scalar, "dve": block.vector,
                   "pool": block.gpsimd, "sp": block.sync}

            def make(e):
                def body(eng):
                    for it in self.stream[e]:
                        if it[0] == "wait":
                            eng.wait_ge(sems[it[1]], it[2])
                        elif it[0] == "op":
                            ins = it[1](eng)
                            if it[2]:
                                ins.then_inc(sems[e], 1)
                        else:
                            it[1](eng).then_inc(sems[it[2]], 16)
                return body

            for e in self.ENGS:
                if self.stream[e]:
                    reg[e](make(e))


def interleave(gens):
    gens = [g for g in gens if g is not None]
    while gens:
        nxt = []
        for g in gens:
            try:
                next(g)
                nxt.append(g)
            except StopIteration:
                pass
        gens = nxt


class _Stop(Exception):
    pass


def build_program(debug=False, stop=None):
    nc = bass.Bass("TRN2", target_bir_lowering=False)
    D = {}

    def din(name, shape, dt=F32):
        D[name] = nc.dram_tensor(name, shape, dt, kind="ExternalInput").ap()

    din("x", [4096, 1024]); din("mem", [512, 1024]); din("pos", [2, 2048], I32)
    din("norm_in", [1, 1024]); din("w_in", [1024, 4296]); din("qag", [128, 3]); din("w_q_b", [384, 768])
    din("kvag", [128, 2]); din("w_kv_b", [256, 1024]); din("convw", [128, 48]); din("alog", [128, 4])
    din("dtb", [128, 4]); din("gdng", [128, 1]); din("mem_norm", [1, 1024]); din("w_mem_kv", [1024, 1024])
    din("w_out", [1536, 1024]); din("norm_final", [1, 1024]); din("ident", [128, 128]); din("tri", [128, 128])
    din("ropec", [64, 3])
    y = nc.dram_tensor("y", [4096, 1024], F32, kind="ExternalOutput").ap()
    dbg = None
    if debug:
        dbg = nc.dram_tensor("dbg", [128, 12 * 2048], F32, kind="ExternalOutput").ap()

    S = Sched(nc)
    w_in_v = D["w_in"].rearrange("(k p) c -> p k c", p=128)

    def ACT(out, in_, func, reads, writes, **kw):
        S.op("act", lambda e: e.activation(out=out, in_=in_, func=func, **kw), reads, writes)

    def TT(eng, out, a, b, op, reads, writes):
        S.op(eng, lambda e: e.tensor_tensor(out=out, in0=a, in1=b, op=op), reads, writes)

    def TS(eng, out, a, s1, s2, op0, op1, reads, writes):
        if s2 is None:
            S.op(eng, lambda e: e.tensor_scalar(out=out, in0=a, scalar1=s1, scalar2=None, op0=op0), reads, writes)
        else:
            S.op(eng, lambda e: e.tensor_scalar(out=out, in0=a, scalar1=s1, scalar2=s2, op0=op0, op1=op1), reads, writes)

    def STT(eng, out, a, sc, b, op0, op1, reads, writes):
        S.op(eng, lambda e: e.scalar_tensor_tensor(out=out, in0=a, scalar=sc, in1=b, op0=op0, op1=op1), reads, writes)

    def CP(eng, out, in_, reads, writes):
        if eng == "act":
            S.op("act", lambda e: e.copy(out=out, in_=in_), reads, writes)
        else:
            S.op(eng, lambda e: e.tensor_copy(out=out, in_=in_), reads, writes)

    def RECIP(out, in_, reads, writes):
        ACT(out, in_, AF.Ln, reads, writes)
        ACT(out, out, AF.Exp, writes, writes, scale=-1.0)

    def MMG(out, pairs, reads, pkey, last_inc=True):
        n = len(pairs)
        for i, (l, r) in enumerate(pairs):
            S.op("pe", lambda e, l=l, r=r, i=i: e.matmul(out, lhsT=l, rhs=r, start=(i == 0), stop=(i == n - 1)),
                 reads if i == 0 else (), [pkey], inc=(last_inc and i == n - 1))

    def TR(out, in_, ident, reads, pkey, inc=True):
        S.op("pe", lambda e: e.transpose(out=out, in_=in_, identity=ident), reads, [pkey], inc=inc)

    def RSQRT(out, in_, scale, tmp, reads, writes, tmpkey):
        ACT(tmp, in_, AF.Ln, reads, [tmpkey], bias=EPS, scale=scale)
        ACT(out, tmp, AF.Exp, [tmpkey], writes, scale=-0.5)

    def chk(n):
        if stop == n:
            S.barrier()
            raise _Stop()

    with ExitStack() as top:
        def _emit():
            arena = {"t": None, "off": 0, "size": 0}
            arena_use = {}

            def sbt(es, name, shape, dt=F32):
                if es is top:
                    return es.enter_context(nc.sbuf_tensor(name, shape, dt))
                esz = 2 if dt == BF16 else 4
                n = 1
                for d in shape[1:]:
                    n *= d
                nbytes = n * esz
                off = arena["off"]
                arena["off"] = off + (nbytes + 63) // 64 * 64
                assert arena["off"] <= arena["size"], (name, arena["off"], arena["size"])
                ap = arena["t"][0:shape[0], off // 2:off // 2 + nbytes // 2]
                if dt != BF16:
                    ap = ap.bitcast(dt)
                if len(shape) == 3:
                    ap = ap.rearrange("p (a b) -> p a b", a=shape[1])
                elif len(shape) == 4:
                    ap = ap.rearrange("p (a b c) -> p a b c", a=shape[1], b=shape[2])
                return ap

            PS = [top.enter_context(nc.psum_tensor("ps%d" % i, [128, 512], F32)) for i in range(7)]
            PB = top.enter_context(nc.psum_tensor("psb", [128, 1024], BF16))
            pctr = [0]

            rot = {"banks": list(range(7))}
            actr = [0]

            def pnext():
                b = rot["banks"]
                k = b[pctr[0] % len(b)]
                pctr[0] += 1
                return ("P", k), PS[k]

            def pacc():
                i = actr[0] % 2
                actr[0] += 1
                return (("P", 3 + 2 * i), PS[3 + 2 * i]), (("P", 4 + 2 * i), PS[4 + 2 * i])

            IDF = sbt(top, "IDF", [128, 128]); TRI = sbt(top, "TRI", [128, 128]); MINC = sbt(top, "MINC", [128, 128])
            IDB = sbt(top, "IDB", [128, 128], BF16); CMASK = sbt(top, "CMASK", [128, 128], BF16)
            ONESB = sbt(top, "ONESB", [128, 128], BF16); ONESF = sbt(top, "ONESF", [128, 128])
            QAG = sbt(top, "QAG", [128, 3]); KVAG = sbt(top, "KVAG", [128, 2]); CONVW = sbt(top, "CONVW", [128, 48])
            ALOG = sbt(top, "ALOG", [128, 4]); DTB = sbt(top, "DTB", [128, 4]); GDNG = sbt(top, "GDNG", [128, 1])
            NEGA = sbt(top, "NEGA", [128, 4]); ROPEC = sbt(top, "ROPEC", [64, 3])
            KM = sbt(top, "KM", [128, 2, 4, 256], BF16); VM = sbt(top, "VM", [128, 2, 2, 512], BF16)
            POSI = sbt(top, "POSI", [64, TB], I32); YI = sbt(top, "YI", [64, TB], I32)
            XT = sbt(top, "XT", [128, 8, S_LEN], BF16)
            MIX = sbt(top, "MIX", [128, 12, S_LEN], BF16)
            asz = (int(nc.sbuf_bytes_remaining) - 2048) // 64 * 64
            arena["t"] = top.enter_context(nc.sbuf_tensor("ARENA", [128, asz // 2], BF16))
            arena["size"] = asz

            def ld(dst, src, key, eng="sp"):
                S.dma(eng, lambda e: e.dma_start(out=dst, in_=src), writes=[key], semkey=key)

            ld(IDF[:], D["ident"], "IDF"); ld(TRI[:], D["tri"], "TRI"); ld(QAG[:], D["qag"], "QAG")
            ld(KVAG[:], D["kvag"], "KVAG"); ld(CONVW[:], D["convw"], "CONVW"); ld(ALOG[:], D["alog"], "ALOG")
            ld(DTB[:], D["dtb"], "DTB"); ld(GDNG[:], D["gdng"], "GDNG"); ld(ROPEC[:], D["ropec"], "ROPEC")
            CP("dve", IDB[:], IDF[:], ["IDF"], ["IDB"])
            CP("dve", CMASK[:], TRI[:], ["TRI"], ["CMASK"])
            TS("dve", MINC[:], TRI[:], -1.0, 30000.0, ALU.add, ALU.mult, ["TRI"], ["MINC"])
            S.op("pool", lambda e: e.memset(ONESB[:], 1.0), (), ["ONESB"])
            S.op("pool", lambda e: e.memset(ONESF[:], 1.0), (), ["ONESF"])
            ACT(NEGA[:], ALOG[:], AF.Exp, ["ALOG"], ["NEGA"])
            TS("dve", NEGA[:], NEGA[:], -1.0, None, ALU.mult, None, ["NEGA"], ["NEGA"])

            with ExitStack() as es:
                arena["off"] = 0
                WMEM = sbt(es, "WMEM", [128, 8, 1024], BF16)
                GMEM = sbt(es, "GMEM", [128, 1024])
                MLD = [sbt(es, "MLD%d" % i, [128, 1024]) for i in range(2)]
                HBm = sbt(es, "HBm", [128, 1024], BF16)
                SCRm = sbt(es, "SCRm", [128, 1024], BF16)
                MEMT = sbt(es, "MEMT", [128, 8, 256], BF16)
                SSm = sbt(es, "SSm", [128, 4])
                wm_v = D["w_mem_kv"].rearrange("(k p) c -> p k c", p=128)
                for hh in range(2):
                    S.dma("pool", lambda e, hh=hh: e.dma_start(out=WMEM[:, 4 * hh:4 * hh + 4, :], in_=wm_v[:, 4 * hh:4 * hh + 4, :]),
                          writes=[("WMEM", hh)], semkey=("WMEM", hh))
                ld(GMEM[:], D["mem_norm"].partition_broadcast(128), "GMEM")
                for s in range(2):
                    for mt in range(2):
                        r0 = s * 256 + mt * 128
                        S.dma("sp", lambda e, r0=r0, mt=mt: e.dma_start(out=MLD[mt][:], in_=D["mem"][r0:r0 + 128, :]),
                              writes=[("MLD", mt)], semkey=("MLD", mt))
                        S.op("pool", lambda e: e.memset(SSm[:, 0:1], 0.0), (), ["SSm0"])
                        ACT(SCRm[:], MLD[mt][:], AF.Square, [("MLD", mt), "SSm0"], ["SCRm", "SSm0"], accum_out=SSm[:, 0:1])
                        RSQRT(SSm[:, 2:3], SSm[:, 0:1], 1.0 / 1024, SSm[:, 1:2], ["SSm0"], ["SSm3"], "SSm2")
                        STT("dve", HBm[:], MLD[mt][:], SSm[:, 2:3], GMEM[:], ALU.mult, ALU.mult, [("MLD", mt), "SSm3", "GMEM"], ["HBm"])
                        for k in range(8):
                            TR(PB[:, k * 128:(k + 1) * 128], HBm[:, k * 128:(k + 1) * 128], IDB[:], ["HBm", "IDB"] if k == 0 else (), "PB", inc=(k == 7))
                        CP("act", MEMT[:, :, mt * 128:(mt + 1) * 128], PB[:].rearrange("p (k t) -> p k t", k=8), ["PB"], [("MEMT", mt)])
                    for h in range(4):
                        pk, pt = pnext()
                        MMG(pt[:, 0:256], [(WMEM[:, k, h * 128:(h + 1) * 128], MEMT[:, k, :]) for k in range(8)],
                            [("WMEM", 0), ("WMEM", 1), ("MEMT", 0), ("MEMT", 1)], pk)
                        CP("act", KM[:, s, h, :], pt[:, 0:256], [pk], [("KM", s, h)])
                    for mt in range(2):
                        pk, pt = pnext()
                        MMG(pt[:, 0:512], [(MEMT[:, k, mt * 128:(mt + 1) * 128], WMEM[:, k, 512:1024]) for k in range(8)],
                            [("WMEM", 0), ("WMEM", 1), ("MEMT", 0), ("MEMT", 1)], pk)
                        CP("dve", VM[:, s, mt, :], pt[:, 0:512], [pk], [("VM", s, mt)])
                S.barrier()
                if stop == 0:
                    raise _Stop()

            for s in range(2):
                def load_mla_weights(WA, WQB, WKK, WKV):
                        for hh in range(2):
                            S.dma("pool", lambda e, hh=hh: e.dma_start(out=WA[:, 4 * hh:4 * hh + 4, 0:704], in_=w_in_v[:, 4 * hh:4 * hh + 4, 0:704]),
                                  writes=[("WA", hh)], semkey=("WA", hh))
                        S.dma("pool", lambda e: e.dma_start(out=WA[:, :, 704:736], in_=w_in_v[:, :, 672:704]), writes=[("WA", 2)], semkey=("WA", 2))
                        S.dma("pool", lambda e: e.dma_start(out=WA[:, :, 736:768], in_=w_in_v[:, :, 640:672]), writes=[("WA", 3)], semkey=("WA", 3))
                        WAK = [("WA", i) for i in range(4)]
                        wq_v = D["w_q_b"].rearrange("(k p) c -> p k c", p=128)
                        S.dma("pool", lambda e: e.dma_start(out=WQB[:, :, 0:768], in_=wq_v), writes=[("WQB", 0)], semkey=("WQB", 0))
                        for h in range(4):
                            b0 = h * 192 + 128
                            S.dma("pool", lambda e, h=h, b0=b0: e.dma_start(out=WQB[:, :, 768 + h * 64:768 + h * 64 + 32], in_=wq_v[:, :, b0 + 32:b0 + 64]),
                                  writes=[("WQB", 1 + 2 * h)], semkey=("WQB", 1))
                            S.dma("pool", lambda e, h=h, b0=b0: e.dma_start(out=WQB[:, :, 768 + h * 64 + 32:768 + h * 64 + 64], in_=wq_v[:, :, b0:b0 + 32]),
                                  writes=[("WQB", 2 + 2 * h)], semkey=("WQB", 2))
                        WQK = [("WQB", i) for i in range(9)]
                        wkv_v = D["w_kv_b"].rearrange("(k p) (h t d) -> p k h t d", p=128, h=4, t=2)
                        for m in range(2):
                            S.dma("pool", lambda e, m=m: e.dma_start(out=WKK[:, m, :].rearrange("p (h d) -> p h d", h=4), in_=wkv_v[:, m, :, 0, :]), writes=["WKK"], semkey=("WKK", m))
                            S.dma("pool", lambda e, m=m: e.dma_start(out=WKV[:, m, :].rearrange("p (h d) -> p h d", h=4), in_=wkv_v[:, m, :, 1, :]), writes=["WKV"], semkey=("WKV", m))


                xflat = MIX[:, 4:12, :].rearrange("p a b -> p (a b)")
                xoff = [0]

                def xcarve(nel_bf16, dt):
                    ap = xflat[:, xoff[0]:xoff[0] + nel_bf16]
                    xoff[0] += nel_bf16
                    assert xoff[0] <= 16384
                    return ap if dt == BF16 else ap.bitcast(dt)

                GIN = xcarve(2048, F32)
                XLD = [xcarve(2048, F32) for i in range(3)]
                HB = [xcarve(1024, BF16) for i in range(2)]
                SCR = xcarve(1024, BF16)
                SS = [xcarve(8, F32) for i in range(2)]
                ld(GIN[:], D["norm_in"].partition_broadcast(128), "GIN")

                def xtile_a(t, s=s):
                    sl = t % 3
                    r0 = s * S_LEN + t * 128
                    S.dma("sp", lambda e, r0=r0, sl=sl: e.dma_start(out=XLD[sl][:], in_=D["x"][r0:r0 + 128, :]),
                          writes=[("XLD", sl)], semkey=("XLD", sl))
                    ss = SS[t % 2]
                    sk = ("SS", t % 2)
                    S.op("pool", lambda e, ss=ss: e.memset(ss[:, 0:1], 0.0), (), [sk + (0,)])
                    ACT(SCR[:], XLD[sl][:], AF.Square, [("XLD", sl), sk + (0,)], ["SCR", sk + (0,)], accum_out=ss[:, 0:1])
                    RSQRT(ss[:, 2:3], ss[:, 0:1], 1.0 / 1024, ss[:, 1:2], [sk + (0,)], [sk + (3,)], sk + (2,))
                    hb = HB[t % 2]
                    STT("dve", hb[:], XLD[sl][:], ss[:, 2:3], GIN[:], ALU.mult, ALU.mult, [("XLD", sl), sk + (3,), "GIN"], [("HB", t % 2)])

                def xtile_b(t):
                    hb = HB[t % 2]
                    for k in range(8):
                        TR(PB[:, k * 128:(k + 1) * 128], hb[:, k * 128:(k + 1) * 128], IDB[:], [("HB", t % 2), "IDB"] if k == 0 else (), "PB", inc=(k == 7))
                    CP("act" if t % 2 == 0 else "dve", XT[:, :, t * 128:(t + 1) * 128], PB[:].rearrange("p (k t) -> p k t", k=8), ["PB"], [("XT", t)])

                for t in range(4):
                    xtile_a(t)
                    xtile_b(t)

                def xt_keys(j):
                    return [("XT", 4 * j + i) for i in range(4)]

                with ExitStack() as es:
                    arena["off"] = 0
                    WA = sbt(es, "WA", [128, 8, 768], BF16)
                    WQB = sbt(es, "WQB", [128, 3, 1024], BF16)
                    WKK = sbt(es, "WKK", [128, 2, 512], BF16); WKV = sbt(es, "WKV", [128, 2, 512], BF16)
                    KN = sbt(es, "KN", [128, 4, S_LEN], BF16); KR = sbt(es, "KR", [128, S_LEN], BF16)
                    V = sbt(es, "V", [128, 16, 512], BF16)
                    CQ = sbt(es, "CQ", [128, 3, TB], BF16); CKV = sbt(es, "CKV", [128, 2, TB], BF16)
                    SQ = [sbt(es, "SQ%d" % i, [128, TB], BF16) for i in range(5)]
                    TMP = sbt(es, "TMP", [128, TB]); TMPK = sbt(es, "TMPK", [128, TB]); RQ = sbt(es, "RQ", [128, TB]); RKV = sbt(es, "RKV", [128, TB])
                    RKC = sbt(es, "RKC", [128, 4]); TMPC = sbt(es, "TMPC", [128, 4])
                    COS = sbt(es, "COS", [64, TB]); SINS = sbt(es, "SINS", [64, TB])
                    COSR = sbt(es, "COSR", [64, TB]); SINR = sbt(es, "SINR", [64, TB])
                    T1 = sbt(es, "T1", [64, TB]); T2 = sbt(es, "T2", [64, TB])
                    YY = sbt(es, "YY", [64, TB])
                    YF = sbt(es, "YF", [64, TB])
                    QN = sbt(es, "QN", [128, 4, TB], BF16); QR = sbt(es, "QR", [128, 4, TB], BF16)
                    PT = [sbt(es, "PT%d" % i, [128, TB], BF16) for i in range(3)]
                    RD = sbt(es, "RD", [128, TB])
                    load_mla_weights(WA, WQB, WKK, WKV)
                    S.op("pool", lambda e: e.memset(KR[64:128, :], 0.0), (), ["KRpad"])
                    S.op("pool", lambda e: e.memset(QR[64:128, :, :], 0.0), (), ["QRpad"])
                    WAK = [("WA", i) for i in range(4)]
                    WQK = [("WQB", i) for i in range(9)]
                    SC_ATT = float(192 ** -0.5)
                    for j in range(NBLK):
                        T0 = j * TB
                        xk = xt_keys(j)
                        S.dma("sp", lambda e, T0=T0, s=s: e.dma_start(out=POSI[:], in_=D["pos"][s:s + 1, T0:T0 + TB].partition_broadcast(64)),
                              writes=["POSI"], semkey="POSI")
                        CP("dve", YF[:], POSI[:], ["POSI"], ["PF"])
                        for (tab, col, tk) in ((COS, 2, "COS"), (SINS, 1, "SINS")):
                            TS("dve", YY[:], YF[:], ROPEC[:, 0:1], ROPEC[:, col:col + 1], ALU.mult, ALU.add, ["PF", "ROPEC"], ["YY"])
                            CP("dve", YI[:], YY[:], ["YY"], ["YI"])
                            CP("dve", T1[:], YI[:], ["YI"], ["T1"])
                            TT("dve", YY[:], YY[:], T1[:], ALU.subtract, ["YY", "T1"], ["YY"])
                            TS("dve", T2[:], YY[:], 0.5, None, ALU.is_gt, None, ["YY"], ["T2"])
                            TT("dve", YY[:], YY[:], T2[:], ALU.subtract, ["YY", "T2"], ["YY"])
                            ACT(tab[:], YY[:], AF.Sin, ["YY"], [tk], scale=float(2 * np.pi * (1 - 1e-6)))
                        chk(20)
                        for m in range(3):
                            pk, pt = pnext()
                            MMG(pt[:, :], [(WA[:, k, m * 128:(m + 1) * 128], XT[:, k, T0:T0 + TB]) for k in range(8)], WAK + xk, pk)
                            ACT(SQ[m][:], pt[:, :], AF.Square, [pk], [("SQ", m)])
                            TS("dve", CQ[:, m, :], pt[:, :], QAG[:, m:m + 1], None, ALU.mult, None, [pk, "QAG"], [("CQ", m)])
                        for m in range(2):
                            pk, pt = pnext()
                            MMG(pt[:, :], [(WA[:, k, 384 + m * 128:384 + (m + 1) * 128], XT[:, k, T0:T0 + TB]) for k in range(8)], WAK + xk, pk)
                            ACT(SQ[3 + m][:], pt[:, :], AF.Square, [pk], [("SQ", 3 + m)])
                            TS("dve", CKV[:, m, :], pt[:, :], KVAG[:, m:m + 1], None, ALU.mult, None, [pk, "KVAG"], [("CKV", m)])
                        pss, pst = pnext()
                        for m in range(3):
                            S.op("pe", lambda e, m=m, pst=pst: e.matmul(pst[:, :], lhsT=ONESB[:], rhs=SQ[m][:], start=(m == 0), stop=(m == 2)),
                                 [("SQ", m), "ONESB"], [pss], inc=(m == 2))
                        RSQRT(RQ[:], pst[:, :], 1.0 / 384, TMP[:], [pss], ["RQ"], "TMP")
                        pss, pst = pnext()
                        for m in range(2):
                            S.op("pe", lambda e, m=m, pst=pst: e.matmul(pst[:, :], lhsT=ONESB[:], rhs=SQ[3 + m][:], start=(m == 0), stop=(m == 1)),
                                 [("SQ", 3 + m), "ONESB"], [pss], inc=(m == 1))
                        pcs, pct = pnext()
                        for tt in range(4):
                            for m in range(2):
                                S.op("pe", lambda e, m=m, tt=tt, pct=pct: e.matmul(pct[:, tt:tt + 1], lhsT=SQ[3 + m][:, tt * 128:(tt + 1) * 128], rhs=ONESB[:, 0:1],
                                                                                  start=(m == 0), stop=(m == 1)),
                                     [("SQ", 3), ("SQ", 4)], [pcs], inc=(tt == 3 and m == 1))
                        RSQRT(RKV[:], pst[:, :], 1.0 / 256, TMPK[:], [pss], ["RKV"], "TMPK")
                        RSQRT(RKC[:], pct[:, 0:4], 1.0 / 256, TMPC[:], [pcs], ["RKC"], "TMPC")
                        chk(22)
                        pk1, pt1 = pnext()
                        MMG(pt1[0:64, :], [(WA[:, k, 640:704], XT[:, k, T0:T0 + TB]) for k in range(8)], WAK + xk, pk1)
                        pk2, pt2 = pnext()
                        MMG(pt2[0:64, :], [(WA[:, k, 704:768], XT[:, k, T0:T0 + TB]) for k in range(8)], WAK + xk, pk2)
                        TT("dve", T1[:], pt1[0:64, :], COS[:], ALU.mult, [pk1, "COS"], ["T1"])
                        TT("dve", T2[:], pt2[0:64, :], SINS[:], ALU.mult, [pk2, "SINS"], ["T2"])
                        TT("pool", KR[0:64, T0:T0 + TB], T1[:], T2[:], ALU.add, ["T1", "T2"], [("KR", j)])
                        chk(23)
                        TT("pool", COSR[:], COS[:], RQ[0:64, :], ALU.mult, ["COS", "RQ"], ["COSR"])
                        TT("pool", SINR[:], SINS[:], RQ[0:64, :], ALU.mult, ["SINS", "RQ"], ["SINR"])
                        cqk = [("CQ", m) for m in range(3)]
                        for h in range(4):
                            pk, pt = pnext()
                            MMG(pt[:, :], [(WQB[:, m, h * 192:h * 192 + 128], CQ[:, m, :]) for m in range(3)], WQK + cqk, pk)
                            TT("dve", QN[:, h, :], pt[:, :], RQ[:], ALU.mult, [pk, "RQ"], [("QN", h)])
                            pk1, pt1 = pnext()
                            MMG(pt1[0:64, :], [(WQB[:, m, h * 192 + 128:h * 192 + 192], CQ[:, m, :]) for m in range(3)], WQK + cqk, pk1)
                            pk2, pt2 = pnext()
                            MMG(pt2[0:64, :], [(WQB[:, m, 768 + h * 64:768 + (h + 1) * 64], CQ[:, m, :]) for m in range(3)], WQK + cqk, pk2)
                            TT("dve", T1[:], pt1[0:64, :], COSR[:], ALU.mult, [pk1, "COSR"], ["T1"])
                            TT("dve", T2[:], pt2[0:64, :], SINR[:], ALU.mult, [pk2, "SINR"], ["T2"])
                            TT("pool", QR[0:64, h, :], T1[:], T2[:], ALU.add, ["T1", "T2"], [("QR", h)])
                        chk(24)
                        ckk = [("CKV", m) for m in range(2)]
                        for h in range(4):
                            pk, pt = pnext()
                            MMG(pt[:, :], [(WKK[:, m, h * 128:(h + 1) * 128], CKV[:, m, :]) for m in range(2)], ["WKK"] + ckk, pk)
                            TT("dve", KN[:, h, T0:T0 + TB], pt[:, :], RKV[:], ALU.mult, [pk, "RKV"], [("KN", h, j)])
                        for tt in range(4):
                            pk, pt = pnext()
                            MMG(pt[:, :], [(CKV[:, m, tt * 128:(tt + 1) * 128], WKV[:, m, :]) for m in range(2)], ["WKV"] + ckk, pk)
                            TS("dve", V[:, 4 * j + tt, :], pt[:, :], RKC[:, tt:tt + 1], None, ALU.mult, None, [pk, "RKC"], [("V", 4 * j + tt)])
                        chk(25)
                        rot["banks"] = [0, 1, 2]
                        nkt = 4 * j + 4
                        items = [(h, kt) for h in range(4) for kt in range(nkt)]
                        LA = 2
                        pend = {}
                        accs = {}
                        fin = []

                        def emit_S(idx):
                            h, kt = items[idx]
                            r = kt - 4 * j
                            qo = max(r, 0) * 128
                            N = TB - qo
                            pk, pt = pnext()
                            kb = kt // 4
                            MMG(pt[:, 0:N], [(KN[:, h, kt * 128:(kt + 1) * 128], QN[:, h, qo:TB]),
                                             (KR[:, kt * 128:(kt + 1) * 128], QR[:, h, qo:TB])],
                                [("KN", h, kb), ("KR", kb), ("QN", h), ("QR", h), "KRpad", "QRpad"], pk)
                            pend[idx] = (pk, pt, r, qo, N)

                        for idx in range(min(LA, len(items))):
                            emit_S(idx)
                        for idx in range(len(items)):
                            if idx + LA < len(items):
                                emit_S(idx + LA)
                            if j + 1 < NBLK and idx < 16:
                                if idx % 4 == 0:
                                    xtile_a(4 * (j + 1) + idx // 4)
                                elif idx % 4 == 3:
                                    xtile_b(4 * (j + 1) + idx // 4)
                            h, kt = items[idx]
                            if kt == 0:
                                accs[h] = pacc()
                            (pok, pot), (pdk, pdt) = accs[h]
                            pk, pt, r, qo, N = pend.pop(idx)
                            P_ = PT[idx % 3]
                            ptk = ("PT", idx % 3)
                            ACT(P_[:, 0:N], pt[:, 0:N], AF.Exp, [pk], [ptk], scale=SC_ATT)
                            if r >= 0:
                                TT("pool", P_[:, 0:128], P_[:, 0:128], CMASK[:], ALU.mult, [ptk, "CMASK"], [ptk])
                            first = (kt == 0)
                            last = (kt == nkt - 1)
                            S.op("pe", lambda e, pot=pot, kt=kt, h=h, P_=P_, qo=qo, N=N, first=first, last=last:
                                 e.matmul(pot[:, qo:TB], lhsT=V[:, kt, h * 128:(h + 1) * 128], rhs=P_[:, 0:N], start=first, stop=last),
                                 [("V", kt), ptk], [pok], inc=False)
                            S.op("pe", lambda e, pdt=pdt, P_=P_, qo=qo, N=N, first=first, last=last:
                                 e.matmul(pdt[:, qo:TB], lhsT=ONESB[:], rhs=P_[:, 0:N], start=first, stop=last),
                                 [ptk, "ONESB"], [pdk], inc=True)
                            if last:
                                fin.append((idx + 3, h, pok, pot, pdk, pdt))
                            while fin and (fin[0][0] <= idx or idx == len(items) - 1):
                                _, fh, fpok, fpot, fpdk, fpdt = fin.pop(0)
                                RECIP(RD[:], fpdt[:, :], [fpdk], ["RD"])
                                TT("dve", MIX[:, fh, T0:T0 + TB], fpot[:, :], RD[:], ALU.mult, [fpok, "RD"], [("MIX", fh, j)])
                        rot["banks"] = list(range(7))
                    S.barrier()
                    if stop == 2:
                        raise _Stop()

                with ExitStack() as es:
                    arena["off"] = 0
                    WG = sbt(es, "WG", [128, 8, 1544], BF16)
                    HALO = sbt(es, "HALO", [128, 12, 3])
                    RAW = [sbt(es, "RAW%d" % i, [128, 515]) for i in range(2)]
                    ACCD = [sbt(es, "ACCD%d" % i, [128, TB]) for i in range(2)]
                    SL = [sbt(es, "SL%d" % i, [128, TB], BF16) for i in range(2)]
                    SQg = [sbt(es, "SQg%d" % i, [128, TB], BF16) for i in range(2)]
                    RNM = sbt(es, "RNM", [128, TB])
                    GQ2 = [sbt(es, "GQ%d" % i, [128, 4, TB], BF16) for i in range(2)]
                    GK2 = [sbt(es, "GK%d" % i, [128, 4, TB], BF16) for i in range(2)]
                    GV2 = [sbt(es, "GV%d" % i, [128, 4, TB], BF16) for i in range(2)]
                    SCN = ["AB", "BETA", "NBETA", "Z", "AZ", "MX", "E", "L", "G", "GC", "GL", "EG", "ED", "EGL"]
                    SCT = [{n: sbt(es, "%s%d" % (n, i), [128, 32] if n == "AB" else [128, 16]) for n in SCN} for i in range(2)]
                    mixflat = MIX[:, 8:12, :].rearrange("p a b -> p (a b)")
                    moff = [0]

                    def mcarve(dt):
                        n = 512 if dt == BF16 else 1024
                        ap = mixflat[:, moff[0]:moff[0] + n]
                        moff[0] += n
                        assert moff[0] <= 8192
                        return ap if dt == BF16 else ap.bitcast(dt)

                    def iset(i):
                        mk = (lambda nm, dt: sbt(es, nm + "0", [128, 512], dt)) if i == 0 else (lambda nm, dt: mcarve(dt))
                        d = {}
                        d["KEG"] = mk("KEG", BF16); d["VT"] = mk("VT", BF16); d["DINC"] = mk("DINC", F32); d["DSB"] = mk("DSB", F32)
                        d["EGB"] = mk("EGB", BF16)
                        d["ZZ"] = [mk("ZZa", BF16), mk("ZZb", BF16)]
                        d["ZT"] = [mk("ZTa", BF16), mk("ZTb", BF16)]
                        d["PP"] = [mk("PPa", BF16), mk("PPb", BF16)]
                        return d
                    ISET = [iset(0), iset(1)]
                    KD = [sbt(es, "KD%d" % i, [128, 512], BF16) for i in range(3)]
                    AT = [sbt(es, "AT%d" % i, [128, 512], BF16) for i in range(3)]
                    QG = [sbt(es, "QG%d" % i, [128, 512], BF16) for i in range(3)]
                    WT = [sbt(es, "WT%d" % i, [128, 512], BF16) for i in range(3)]
                    UU = [sbt(es, "UU%d" % i, [128, 512]) for i in range(3)]
                    TV = sbt(es, "TV", [128, 512]); VNEW = sbt(es, "VNEW", [128, 512], BF16)
                    ST = sbt(es, "ST", [128, 512]); STT_ = sbt(es, "STT", [128, 512]); STB = sbt(es, "STB", [128, 512], BF16)
                    OSB = sbt(es, "OSB", [128, 512]); SQO = sbt(es, "SQO", [128, 512], BF16)
                    RN = sbt(es, "RN", [128, 512]); TMP2 = sbt(es, "TMP2", [128, 512])

                    S.dma("pool", lambda e: e.dma_start(out=WG[:, :, 1536:1544], in_=w_in_v[:, :, 2240:2248]), writes=[("WG", "ab")], semkey=("WG", 3))
                    for g3 in range(3):
                        S.dma("pool", lambda e, g3=g3: e.dma_start(out=WG[:, :, g3 * 512:(g3 + 1) * 512], in_=w_in_v[:, :, 704 + g3 * 512:704 + (g3 + 1) * 512]),
                              writes=[("WG", g3)], semkey=("WG", g3))
                    S.op("pool", lambda e: e.memset(HALO[:], 0.0), (), ["HALO"])
                    S.op("pool", lambda e: e.memset(ST[:], 0.0), (), ["ST"])
                    S.op("pool", lambda e: e.memset(STB[:], 0.0), (), ["STB"])

                    def h4(ap):
                        return ap.rearrange("p (h i) -> p h i", h=4)

                    def bc4(ap4):
                        return ap4.unsqueeze(2).to_broadcast([128, 4, 128])

                    def blockprep(j):
                        T0 = j * TB
                        xk = xt_keys(j)
                        bp = j % 2
                        GQ, GK, GV = GQ2[bp], GK2[bp], GV2[bp]
                        sc = SCT[j % 2]
                        K = lambda n: (n, j % 2)
                        pk, pt = pnext()
                        for c in range(4):
                            MMG(pt[:, c * 8:(c + 1) * 8], [(XT[:, k, T0 + c * 128:T0 + (c + 1) * 128], WG[:, k, 1536:1544]) for k in range(8)],
                                [("WG", "ab")] + xk, pk, last_inc=(c == 3))
                        CP("act", sc["AB"][:], pt[:, 0:32], [pk], [K("AB")])
                        ab3 = sc["AB"][:].rearrange("p (c e) -> p c e", c=4)
                        v3 = lambda n: sc[n][:].rearrange("p (c h) -> p c h", c=4)
                        ACT(v3("BETA"), ab3[:, :, 4:8], AF.Sigmoid, [K("AB")], [K("BETA")])
                        TS("pool", sc["NBETA"][:], sc["BETA"][:], -1.0, None, ALU.mult, None, [K("BETA")], [K("NBETA")])
                        TT("dve", v3("Z"), ab3[:, :, 0:4], DTB[:].unsqueeze(1).to_broadcast([128, 4, 4]), ALU.add, [K("AB"), "DTB"], [K("Z")])
                        TS("dve", sc["AZ"][:], sc["Z"][:], -1.0, None, ALU.mult, None, [K("Z")], [K("AZ")])
                        TT("dve", sc["AZ"][:], sc["AZ"][:], sc["Z"][:], ALU.max, [K("AZ"), K("Z")], [K("AZ")])
                        ACT(sc["E"][:], sc["AZ"][:], AF.Exp, [K("AZ")], [K("E")], scale=-1.0)
                        ACT(sc["L"][:], sc["E"][:], AF.Ln, [K("E")], [K("L")], bias=1.0)
                        TS("dve", sc["MX"][:], sc["Z"][:], 0.0, None, ALU.max, None, [K("Z")], [K("MX")])
                        TT("dve", sc["L"][:], sc["L"][:], sc["MX"][:], ALU.add, [K("L"), K("MX")], [K("L")])
                        TT("dve", v3("G"), sc["L"][:].rearrange("p (c h) -> p c h", c=4), NEGA[:].unsqueeze(1).to_broadcast([128, 4, 4]), ALU.mult,
                           [K("L"), "NEGA"], [K("G")])
                        pk, pt = pnext()
                        S.op("pe", lambda e, pt=pt: e.matmul(pt[:, 0:16], lhsT=TRI[:], rhs=sc["G"][:], start=True, stop=True), [K("G"), "TRI"], [pk])
                        CP("dve", sc["GC"][:], pt[:, 0:16], [pk], [K("GC")])
                        pk, pt = pnext()
                        S.op("pe", lambda e, pt=pt: e.matmul(pt[:, 0:16], lhsT=ONESF[:], rhs=sc["G"][:], start=True, stop=True), [K("G"), "ONESF"], [pk])
                        CP("dve", sc["GL"][:], pt[:, 0:16], [pk], [K("GL")])
                        ACT(sc["EG"][:], sc["GC"][:], AF.Exp, [K("GC")], [K("EG")])
                        ACT(sc["EGL"][:], sc["GL"][:], AF.Exp, [K("GL")], [K("EGL")])
                        TT("dve", sc["ED"][:], sc["GL"][:], sc["GC"][:], ALU.subtract, [K("GL"), K("GC")], [K("ED")])
                        ACT(sc["ED"][:], sc["ED"][:], AF.Exp, [K("ED")], [K("ED")])
                        yield
                        def finish_norm(ch, h, sl, slk, sq, sqk):
                            pk, pt = pnext()
                            S.op("pe", lambda e, pt=pt, sq=sq: e.matmul(pt[:, :], lhsT=ONESB[:], rhs=sq[:], start=True, stop=True), [sqk, "ONESB"], [pk])
                            RSQRT(RNM[:], pt[:, :], 1.0, RNM[:], [pk], ["RNM"], "RNM")
                            if ch < 4:
                                STT("dve", GQ[:, h, :], sl[:], float(128 ** -0.5), RNM[:], ALU.mult, ALU.mult, [slk, "RNM"], [("GQ", bp, h)])
                            else:
                                TT("dve", GK[:, h, :], sl[:], RNM[:], ALU.mult, [slk, "RNM"], [("GK", bp, h)])

                        pend_norm = None
                        for ch in range(12):
                            raw = RAW[ch % 2]
                            rk = ("RAW", ch % 2)
                            pk, pt = pnext()
                            MMG(pt[:, :], [(WG[:, k, ch * 128:(ch + 1) * 128], XT[:, k, T0:T0 + TB]) for k in range(8)], [("WG", ch // 4)] + xk, pk)
                            CP("pool", raw[:, 0:3], HALO[:, ch, :], ["HALO"], [rk + (0,)])
                            CP("act", raw[:, 3:515], pt[:, :], [pk], [rk + (1,)])
                            CP("pool", HALO[:, ch, :], raw[:, 512:515], [rk + (1,)], ["HALO"])
                            acc = ACCD[ch % 2]
                            ak = ("ACCD", ch % 2)
                            rr = [rk + (0,), rk + (1,), "CONVW"]
                            TS("dve", acc[:], pt[:, :], CONVW[:, ch * 4 + 3:ch * 4 + 4], None, ALU.mult, None, [pk, "CONVW"], [ak])
                            for jt in (2, 1, 0):
                                STT("dve", acc[:], raw[:, jt:jt + TB], CONVW[:, ch * 4 + jt:ch * 4 + jt + 1], acc[:], ALU.mult, ALU.add, rr + [ak], [ak])
                            h = ch % 4
                            if pend_norm is not None:
                                finish_norm(*pend_norm)
                                pend_norm = None
                            if ch >= 8:
                                ACT(GV[:, h, :], acc[:], AF.Silu, [ak], [("GV", bp, h)])
                            else:
                                sl = SL[ch % 2]
                                slk = ("SL", ch % 2)
                                sq = SQg[ch % 2]
                                sqk = ("SQg", ch % 2)
                                ACT(sl[:], acc[:], AF.Silu, [ak], [slk])
                                TT("pool", sq[:], sl[:], sl[:], ALU.mult, [slk], [sqk])
                                pend_norm = (ch, h, sl, slk, sq, sqk)
                            yield

                    def prep(cc):
                        j, c = divmod(cc, 4)
                        sc = SCT[j % 2]
                        K = lambda n: (n, j % 2)
                        si = cc % 2
                        I_ = ISET[si]
                        IK = lambda n: (n, "set", si)
                        par = cc % 3
                        KEG, VT, DINC, DSB, EGB, ZZ, ZT, PP = I_["KEG"], I_["VT"], I_["DINC"], I_["DSB"], I_["EGB"], I_["ZZ"], I_["ZT"], I_["PP"]
                        cs = slice(c * 128, (c + 1) * 128)
                        col4 = lambda n: sc[n][:, c * 4:(c + 1) * 4]
                        bp = j % 2
                        GQ, GK, GV = GQ2[bp], GK2[bp], GV2[bp]
                        gkk = [("GK", bp, h) for h in range(4)]
                        gqk = [("GQ", bp, h) for h in range(4)]
                        gvk = [("GV", bp, h) for h in range(4)]
                        for h in range(4):
                            TR(PB[:, h * 128:(h + 1) * 128], GK[:, h, cs], IDB[:], gkk + ["IDB"] if h == 0 else (), "PB", inc=False)
                        for h in range(4):
                            TR(PB[:, 512 + h * 128:512 + (h + 1) * 128], GV[:, h, cs], IDB[:], gvk if h == 0 else (), "PB", inc=(h == 3))
                        TT("dve", h4(KEG[:]), h4(PB[:, 0:512]), bc4(col4("EG")), ALU.mult, ["PB", K("EG")], [IK("KEG")])
                        TT("dve", h4(KD[par][:]), h4(PB[:, 0:512]), bc4(col4("ED")), ALU.mult, ["PB", K("ED")], [("KD", par)])
                        CP("act", VT[:], PB[:, 512:1024], ["PB"], [IK("VT")])
                        yield
                        pgk, pgt = pnext()
                        for h in range(4):
                            S.op("pe", lambda e, h=h, pgt=pgt: e.matmul(pgt[:, h * 128:(h + 1) * 128],
                                                                        lhsT=sc["G"][:, c * 4 + h:c * 4 + h + 1].to_broadcast([128, 128]), rhs=TRI[:],
                                                                        start=True, stop=True),
                                 [K("G"), "TRI"] if h == 0 else (), [pgk], inc=(h == 3))
                        TT("dve", h4(DINC[:]), h4(pgt[:, :]), bc4(col4("GC")), ALU.subtract, [pgk, K("GC")], [IK("DINC")])
                        ACT(EGB[:], pgt[:, :], AF.Exp, [pgk], [IK("EGB")])
                        TT("pool", h4(DINC[:]), h4(DINC[:]), MINC[:].unsqueeze(1).to_broadcast([128, 4, 128]), ALU.add, [IK("DINC"), "MINC"], [IK("DINC")])
                        ACT(DINC[:], DINC[:], AF.Exp, [IK("DINC")], [IK("DINC")])
                        TT("pool", h4(DSB[:]), h4(DINC[:]), IDF[:].unsqueeze(1).to_broadcast([128, 4, 128]), ALU.subtract, [IK("DINC"), "IDF"], [IK("DSB")])
                        TT("pool", h4(DSB[:]), h4(DSB[:]), bc4(col4("NBETA")), ALU.mult, [IK("DSB"), K("NBETA")], [IK("DSB")])
                        TT("pool", h4(QG[par][:]), GQ[:, :, cs], h4(EGB[:]), ALU.mult, gqk + [IK("EGB")], [("QG", par)])
                        yield
                        pk, pt = pnext()
                        for h in range(4):
                            S.op("pe", lambda e, h=h, pt=pt: e.matmul(pt[:, h * 128:(h + 1) * 128], lhsT=GK[:, h, cs], rhs=GK[:, h, cs], start=True, stop=True),
                                 gkk if h == 0 else (), [pk], inc=(h == 3))
                        TT("dve", ZZ[0][:], pt[:, :], DSB[:], ALU.mult, [pk, IK("DSB")], [IK("ZZ0")])
                        pk, pt = pnext()
                        for h in range(4):
                            S.op("pe", lambda e, h=h, pt=pt: e.matmul(pt[:, h * 128:(h + 1) * 128], lhsT=GK[:, h, cs], rhs=GQ[:, h, cs], start=True, stop=True),
                                 gkk + gqk if h == 0 else (), [pk], inc=(h == 3))
                        TT("dve", AT[par][:], pt[:, :], DINC[:], ALU.mult, [pk, IK("DINC")], [("AT", par)])
                        TT("pool", h4(PP[0][:]), h4(ZZ[0][:]), IDB[:].unsqueeze(1).to_broadcast([128, 4, 128]), ALU.add, [IK("ZZ0"), "IDB"], [IK("PP0")])
                        yield
                        for h in range(4):
                            TR(PB[:, h * 128:(h + 1) * 128], ZZ[0][:, h * 128:(h + 1) * 128], IDB[:], [IK("ZZ0"), "IDB"] if h == 0 else (), "PB", inc=(h == 3))
                        CP("act", ZT[0][:], PB[:, 0:512], ["PB"], [IK("ZT0")])
                        yield
                        def pupd(lv, zt_i):
                            pcur = (lv - 1) % 2
                            pnx = 1 - pcur
                            pkp, ptp = pnext()
                            for h in range(4):
                                hs = slice(h * 128, (h + 1) * 128)
                                S.op("pe", lambda e, hs=hs, ptp=ptp, pcur=pcur, zt_i=zt_i: e.matmul(ptp[:, hs], lhsT=ZT[zt_i][:, hs], rhs=PP[pcur][:, hs], start=True, stop=True),
                                     [IK("ZT%d" % zt_i), IK("PP%d" % pcur)] if h == 0 else (), [pkp], inc=(h == 3))
                            return pkp, ptp, pcur, pnx

                        cur = 0
                        for lv in range(1, 7):
                            nx = 1 - cur
                            zk, ztk = IK("ZZ%d" % cur), IK("ZT%d" % cur)
                            pkt, ptt = pnext()
                            for h in range(4):
                                hs = slice(h * 128, (h + 1) * 128)
                                S.op("pe", lambda e, hs=hs, ptt=ptt, cur=cur: e.matmul(ptt[:, hs], lhsT=ZZ[cur][:, hs], rhs=ZT[cur][:, hs], start=True, stop=True),
                                     [zk, ztk] if h == 0 else (), [pkt], inc=(h == 3))
                            if lv < 6:
                                pkz, ptz = pnext()
                                for h in range(4):
                                    hs = slice(h * 128, (h + 1) * 128)
                                    S.op("pe", lambda e, hs=hs, ptz=ptz, cur=cur: e.matmul(ptz[:, hs], lhsT=ZT[cur][:, hs], rhs=ZZ[cur][:, hs], start=True, stop=True),
                                         [zk, ztk] if h == 0 else (), [pkz], inc=(h == 3))
                            pu = pupd(lv - 1, cur) if lv >= 2 else None
                            CP("act", ZT[nx][:], ptt[:, :], [pkt], [IK("ZT%d" % nx)])
                            if lv < 6:
                                CP("dve", ZZ[nx][:], ptz[:, :], [pkz], [IK("ZZ%d" % nx)])
                            if pu is not None:
                                pkp, ptp, pcur, pnx = pu
                                TT("dve", PP[pnx][:], ptp[:, :], PP[pcur][:], ALU.add, [pkp, IK("PP%d" % pcur)], [IK("PP%d" % pnx)])
                            cur = nx
                            yield
                        pkp, ptp, pcur, pnx = pupd(6, cur)
                        TT("dve", PP[pnx][:], ptp[:, :], PP[pcur][:], ALU.add, [pkp, IK("PP%d" % pcur)], [IK("PP%d" % pnx)])
                        yield
                        TTm = PP[0]
                        tk = IK("PP0")
                        pk, pt = pnext()
                        for h in range(4):
                            hs = slice(h * 128, (h + 1) * 128)
                            S.op("pe", lambda e, hs=hs, pt=pt: e.matmul(pt[:, hs], lhsT=KEG[:, hs], rhs=TTm[:, hs], start=True, stop=True),
                                 [IK("KEG"), tk] if h == 0 else (), [pk], inc=(h == 3))
                        CP("act", WT[par][:], pt[:, :], [pk], [("WT", par)])
                        pk, pt = pnext()
                        for h in range(4):
                            hs = slice(h * 128, (h + 1) * 128)
                            S.op("pe", lambda e, hs=hs, pt=pt: e.matmul(pt[:, hs], lhsT=TTm[:, hs], rhs=VT[:, hs], start=True, stop=True),
                                 [IK("VT"), tk] if h == 0 else (), [pk], inc=(h == 3))
                        CP("dve", UU[par][:], pt[:, :], [pk], [("UU", par)])
                        yield

                    def seq(cc):
                        j, c = divmod(cc, 4)
                        sc = SCT[j % 2]
                        K = lambda n: (n, j % 2)
                        par = cc % 3
                        C0 = cc * 128
                        col4 = lambda n: sc[n][:, c * 4:(c + 1) * 4]
                        pk, pt = pnext()
                        for h in range(4):
                            hs = slice(h * 128, (h + 1) * 128)
                            S.op("pe", lambda e, hs=hs, pt=pt: e.matmul(pt[:, hs], lhsT=WT[par][:, hs], rhs=STB[:, hs], start=True, stop=True),
                                 [("WT", par), "STB"] if h == 0 else (), [pk], inc=(h == 3))
                        TT("dve", TV[:], UU[par][:], pt[:, :], ALU.subtract, [("UU", par), pk], ["TV"])
                        TT("pool", h4(VNEW[:]), h4(TV[:]), bc4(col4("BETA")), ALU.mult, ["TV", K("BETA")], ["VNEW"])
                        yield
                        pok, pot = pnext()
                        for h in range(4):
                            hs = slice(h * 128, (h + 1) * 128)
                            S.op("pe", lambda e, hs=hs, pot=pot: e.matmul(pot[:, hs], lhsT=STB[:, hs], rhs=QG[par][:, hs], start=True, stop=False),
                                 ["STB", ("QG", par)] if h == 0 else (), [pok], inc=False)
                            S.op("pe", lambda e, hs=hs, pot=pot: e.matmul(pot[:, hs], lhsT=VNEW[:, hs], rhs=AT[par][:, hs], start=False, stop=True),
                                 ["VNEW", ("AT", par)] if h == 0 else (), [pok], inc=(h == 3))
                        pk, pt = pnext()
                        for h in range(4):
                            hs = slice(h * 128, (h + 1) * 128)
                            S.op("pe", lambda e, hs=hs, pt=pt: e.matmul(pt[:, hs], lhsT=KD[par][:, hs], rhs=VNEW[:, hs], start=True, stop=True),
                                 [("KD", par), "VNEW"] if h == 0 else (), [pk], inc=(h == 3))
                        TT("pool", h4(STT_[:]), h4(ST[:]), bc4(col4("EGL")), ALU.mult, ["ST", K("EGL")], ["STT"])
                        TT("dve", ST[:], STT_[:], pt[:, :], ALU.add, ["STT", pk], ["ST"])
                        CP("pool", STB[:], ST[:], ["ST"], ["STB"])
                        ACT(SQO[:], pot[:, :], AF.Square, [pok], ["SQO"])
                        CP("dve", OSB[:], pot[:, :], [pok], ["OSB"])
                        yield
                        pk, pt = pnext()
                        S.op("pe", lambda e, pt=pt: e.matmul(pt[:, :], lhsT=ONESB[:], rhs=SQO[:], start=True, stop=True), ["SQO", "ONESB"], [pk])
                        RSQRT(RN[:], pt[:, :], 1.0 / 128, TMP2[:], [pk], ["RN"], "TMP2")
                        STT("dve", MIX[:, 4:8, C0:C0 + 128], h4(OSB[:]), GDNG[:, 0:1], h4(RN[:]), ALU.mult, ALU.mult, ["OSB", "RN", "GDNG"],
                            [("MIX", 4 + h, cc // 4) for h in range(4)])
                        yield

                    done = set()
                    active = {}
                    started = set()

                    def ready(t):
                        kind, i = t
                        if kind == "B":
                            return (i == 0) or ((("B", i - 1) in done) and (i < 2 or ("P", 4 * i - 5) in done) and (("Q", 4 * i - 5) in done or 4 * i - 5 < 0))
                        if kind == "P":
                            return (("B", i // 4) in done) and (i < 2 or ("P", i - 2) in done) and (i < 3 or ("Q", i - 3) in done)
                        return (("P", i) in done) and (i == 0 or ("Q", i - 1) in done)

                    tasks = [("B", j) for j in range(4)] + [("P", cc) for cc in range(16)] + [("Q", cc) for cc in range(16)]
                    mk = {"B": blockprep, "P": prep, "Q": seq}
                    while len(done) < len(tasks):
                        for t in tasks:
                            if t not in started and ready(t):
                                started.add(t)
                                active[t] = mk[t[0]](t[1])
                        assert active, "GDN scheduler stalled"
                        prio = {"Q": 0, "P": 1, "B": 2}
                        for t in sorted(active.keys(), key=lambda t: (prio[t[0]], t[1])):
                            try:
                                next(active[t])
                            except StopIteration:
                                del active[t]
                                done.add(t)
                    arena_use["gdn"] = arena["off"]
                    S.barrier()
                    if stop == 3:
                        raise _Stop()

                with ExitStack() as es:
                    arena["off"] = 0
                    WGT = sbt(es, "WGT", [128, 8, 1536], BF16)
                    WOUT = sbt(es, "WOUT", [128, 12, 1024], BF16)
                    GFIN = sbt(es, "GFIN", [128, 1024])
                    WM = sbt(es, "WM", [128, 8, 512], BF16)
                    MQ = sbt(es, "MQ", [128, 4, TB], BF16)
                    PTm = [sbt(es, "PTm%d" % i, [128, TB], BF16) for i in range(3)]
                    RDm = sbt(es, "RDm", [128, TB])
                    S.dma("pool", lambda e: e.dma_start(out=WM[:], in_=w_in_v[:, :, 2248:2760]), writes=["WM"], semkey="WM")
                    for hh in range(2):
                        S.dma("pool", lambda e, hh=hh, WGT=WGT: e.dma_start(out=WGT[:, 4 * hh:4 * hh + 4, :], in_=w_in_v[:, 4 * hh:4 * hh + 4, 2760:4296]),
                              writes=[("WGT", hh)], semkey=("WGT", hh))
                    wo_v = D["w_out"].rearrange("(k p) c -> p k c", p=128)
                    for q in range(3):
                        S.dma("pool", lambda e, q=q, WOUT=WOUT: e.dma_start(out=WOUT[:, 4 * q:4 * q + 4, :], in_=wo_v[:, 4 * q:4 * q + 4, :]),
                              writes=[("WOUT", q)], semkey=("WOUT", q))
                    S.dma("sp", lambda e, GFIN=GFIN: e.dma_start(out=GFIN[:], in_=D["norm_final"].partition_broadcast(128)), writes=["GFIN"], semkey="GFIN")
                    SC_M = float(128 ** -0.5)
                    pti = 0
                    for j in range(NBLK):
                        T0 = j * TB
                        xk = xt_keys(j)
                        for h in range(4):
                            pk, pt = pnext()
                            MMG(pt[:, :], [(WM[:, k, h * 128:(h + 1) * 128], XT[:, k, T0:T0 + TB]) for k in range(8)], ["WM"] + xk, pk)
                            CP("act" if h % 2 else "dve", MQ[:, h, :], pt[:, :], [pk], [("MQ", h)])
                        rot["banks"] = [0, 1, 2]
                        mitems = [(h, mt) for h in range(4) for mt in range(2)]
                        mpend = {}
                        maccs = {}
                        mfin = []

                        def emit_MS(idx):
                            h, mt = mitems[idx]
                            pk, pt = pnext()
                            S.op("pe", lambda e, pt=pt, h=h, mt=mt, s=s: e.matmul(pt[:, :], lhsT=KM[:, s, h, mt * 128:(mt + 1) * 128], rhs=MQ[:, h, :], start=True, stop=True),
                                 [("MQ", h)], [pk])
                            mpend[idx] = (pk, pt)

                        for idx in range(2):
                            emit_MS(idx)
                        for idx in range(len(mitems)):
                            if idx + 2 < len(mitems):
                                emit_MS(idx + 2)
                            h, mt = mitems[idx]
                            if mt == 0:
                                maccs[h] = pacc()
                            (pok, pot), (pdk, pdt) = maccs[h]
                            pk, pt = mpend.pop(idx)
                            P_ = PTm[pti % 3]
                            ptk = ("PTm", pti % 3)
                            pti += 1
                            ACT(P_[:], pt[:, :], AF.Exp, [pk], [ptk], scale=SC_M)
                            S.op("pe", lambda e, pot=pot, h=h, mt=mt, P_=P_, s=s: e.matmul(pot[:, :], lhsT=VM[:, s, mt, h * 128:(h + 1) * 128], rhs=P_[:], start=(mt == 0), stop=(mt == 1)),
                                 [ptk], [pok], inc=False)
                            S.op("pe", lambda e, pdt=pdt, mt=mt, P_=P_: e.matmul(pdt[:, :], lhsT=ONESB[:], rhs=P_[:], start=(mt == 0), stop=(mt == 1)),
                                 [ptk], [pdk], inc=True)
                            if mt == 1:
                                mfin.append((idx + 2, h, pok, pot, pdk, pdt))
                            while mfin and (mfin[0][0] <= idx or idx == len(mitems) - 1):
                                _, fh, fpok, fpot, fpdk, fpdt = mfin.pop(0)
                                RECIP(RDm[:], fpdt[:, :], [fpdk], ["RDm"])
                                TT("dve", MIX[:, 8 + fh, T0:T0 + TB], fpot[:, :], RDm[:], ALU.mult, [fpok, "RDm"], [("MIX", 8 + fh, j)])
                        rot["banks"] = list(range(7))
                    S.barrier()
                    if stop == 4:
                        raise _Stop()

                if dbg is not None and s == 0:
                    with ExitStack() as es:
                        arena["off"] = 96 * 1024
                        DB = sbt(es, "DB", [128, 2048])
                        for ch in range(12):
                            CP("dve", DB[:], MIX[:, ch, :], [], ["DB"])
                            S.dma("sp", lambda e, ch=ch: e.dma_start(out=dbg[:, ch * 2048:(ch + 1) * 2048], in_=DB[:]), reads=["DB"], semkey="DB")
                        S.barrier()
                        if stop == 5:
                            raise _Stop()

                with ExitStack() as es:
                    arena["off"] = 0
                    WGT = sbt(es, "WGT", [128, 8, 1536], BF16)
                    WOUT = sbt(es, "WOUT", [128, 12, 1024], BF16)
                    GFIN = sbt(es, "GFIN", [128, 1024])
                    XL = [sbt(es, "XL%d" % i, [128, 1024]) for i in range(3)]
                    SG = [sbt(es, "SG%d" % i, [128, TB], BF16) for i in range(2)]
                    SCRo = sbt(es, "SCRo", [128, 1024], BF16)
                    SSo = [sbt(es, "SSo%d" % i, [128, 4]) for i in range(2)]
                    WGTK = [("WGT", 0), ("WGT", 1)]
                    WOK = [("WOUT", q) for q in range(3)]
                    ti = 0

                    def gate_block(j):
                        T0 = j * TB
                        xk = xt_keys(j)
                        for ch in range(12):
                            pk, pt = pnext()
                            MMG(pt[:, :], [(WGT[:, k, ch * 128:(ch + 1) * 128], XT[:, k, T0:T0 + TB]) for k in range(8)], WGTK + xk, pk)
                            sg = SG[ch % 2]
                            ACT(sg[:], pt[:, :], AF.Silu, [pk], [("SG", ch % 2)])
                            TT("pool", MIX[:, ch, T0:T0 + TB], MIX[:, ch, T0:T0 + TB], sg[:], ALU.mult, [("SG", ch % 2), ("MIX", ch, j)], [("MIX", ch, j)])

                    gate_block(0)
                    for j in range(NBLK):
                        T0 = j * TB
                        if j + 1 < NBLK:
                            gate_block(j + 1)
                        mk = [("MIX", ch, j) for ch in range(12)]
                        for tt in range(4):
                            sl = ti % 3
                            ti += 1
                            xl = XL[sl]
                            xlk = ("XL", sl)
                            r0 = s * S_LEN + T0 + tt * 128
                            S.dma("sp", lambda e, r0=r0, xl=xl: e.dma_start(out=xl[:], in_=D["x"][r0:r0 + 128, :]), writes=[xlk], semkey=xlk)
                            for half in range(2):
                                pk, pt = pnext()
                                MMG(pt[:, :], [(MIX[:, ch, T0 + tt * 128:T0 + (tt + 1) * 128], WOUT[:, ch, half * 512:(half + 1) * 512]) for ch in range(12)],
                                    mk + WOK, pk)
                                TT("dve", xl[:, half * 512:(half + 1) * 512], xl[:, half * 512:(half + 1) * 512], pt[:, :], ALU.add, [pk, xlk], [xlk])
                            ss = SSo[tt % 2]
                            sk = ("SSo", tt % 2)
                            S.op("pool", lambda e, ss=ss: e.memset(ss[:, 0:1], 0.0), (), [sk + (0,)])
                            ACT(SCRo[:], xl[:], AF.Square, [xlk, sk + (0,)], ["SCRo", sk + (0,)], accum_out=ss[:, 0:1])
                            RSQRT(ss[:, 2:3], ss[:, 0:1], 1.0 / 1024, ss[:, 1:2], [sk + (0,)], [sk + (3,)], sk + (2,))
                            STT("dve", xl[:], xl[:], ss[:, 2:3], GFIN[:], ALU.mult, ALU.mult, [xlk, sk + (3,), "GFIN"], [xlk])
                            S.dma("sp", lambda e, r0=r0, xl=xl: e.dma_start(out=y[r0:r0 + 128, :], in_=xl[:]), reads=[xlk], semkey=("YST", sl))
                    S.barrier()
                    if stop == 6:
                        raise _Stop()

        try:
            _emit()
        except _Stop:
            pass
        for sk, n in list(S.dma_cnt.items()):
            S.wait_event("sp", (sk, 16 * n))
        S.replay()
    return nc


_NC_CACHE = {}
_STOP = None


def _consts():
    ident = np.eye(128, dtype=np.float32)
    tri = np.triu(np.ones((128, 128), dtype=np.float32))
    half = 32
    invf = 1.0 / (10000.0 ** (np.arange(half, dtype=np.float32) / half))
    invf64 = np.concatenate([invf, invf]).astype(np.float64)
    ropec = np.zeros((64, 3), dtype=np.float32)
    ropec[:, 0] = (invf64 / (2 * np.pi)).astype(np.float32)
    ropec[:32, 1] = 0.5
    ropec[32:, 1] = 0.0
    ropec[:, 2] = 0.25
    return ident, tri, ropec


def kernel(x, mem, positions, norm_in, w_in, q_a_norm, w_q_b, kv_a_norm, w_kv_b, gdn_conv,
           gdn_a_log, gdn_dt_bias, gdn_norm, mem_norm, w_mem_kv, w_out, norm_final, _debug=False):
    f = lambda a: np.ascontiguousarray(np.asarray(a, dtype=np.float32))
    x = f(x); mem = f(mem)
    positions = np.ascontiguousarray(np.asarray(positions, dtype=np.int32))
    ident, tri, ropec = _consts()
    shared = {
        "norm_in": f(norm_in).reshape(1, 1024),
        "w_in": f(w_in).reshape(1024, 4296),
        "qag": np.ascontiguousarray(f(q_a_norm).reshape(3, 128).T),
        "w_q_b": f(w_q_b).reshape(384, 768),
        "kvag": np.ascontiguousarray(f(kv_a_norm).reshape(2, 128).T),
        "w_kv_b": f(w_kv_b).reshape(256, 1024),
        "convw": np.ascontiguousarray(f(gdn_conv).reshape(4, 12, 128).transpose(2, 1, 0).reshape(128, 48)),
        "alog": np.ascontiguousarray(np.broadcast_to(f(gdn_a_log).reshape(1, 4), (128, 4))),
        "dtb": np.ascontiguousarray(np.broadcast_to(f(gdn_dt_bias).reshape(1, 4), (128, 4))),
        "gdng": np.ascontiguousarray(f(gdn_norm).reshape(128, 1)),
        "mem_norm": f(mem_norm).reshape(1, 1024),
        "w_mem_kv": f(w_mem_kv).reshape(1024, 1024),
        "w_out": f(w_out).reshape(1536, 1024),
        "norm_final": f(norm_final).reshape(1, 1024),
        "ident": ident, "tri": tri, "ropec": ropec,
    }
    key = bool(_debug)
    if key not in _NC_CACHE:
        _NC_CACHE[key] = build_program(debug=key, stop=_STOP)
    nc = _NC_CACHE[key]
    in_maps = []
    for c in range(8):
        m = dict(shared)
        m["x"] = np.ascontiguousarray(x[2 * c:2 * c + 2].reshape(4096, 1024))
        m["mem"] = np.ascontiguousarray(mem[2 * c:2 * c + 2].reshape(512, 1024))
        m["pos"] = np.ascontiguousarray(positions[2 * c:2 * c + 2])
        in_maps.append(m)
    res = run_bass_kernel_spmd(nc, in_maps, core_ids=list(range(8)))
    out = np.concatenate([np.asarray(r["y"], dtype=np.float32).reshape(2, 2048, 1024) for r in res.results], axis=0)
    if _debug:
        return out, [np.asarray(r["dbg"]) for r in res.results]
    return out
```

```python
import numpy as np
from contextlib import ExitStack
import concourse.bass as bass
import concourse.mybir as mybir
from concourse.bass_utils import run_bass_kernel_spmd

F32 = mybir.dt.float32
BF16 = mybir.dt.bfloat16
I32 = mybir.dt.int32
AF = mybir.ActivationFunctionType
ALU = mybir.AluOpType

EPS = 1e-6
S_LEN = 2048
NBLK = 4
TB = 512


class Sched:
    ENGS = ("pe", "act", "dve", "pool", "sp")

    def __init__(self, nc):
        self.nc = nc
        self.stream = {e: [] for e in self.ENGS}
        self.cnt = {e: 0 for e in self.ENGS}
        self.waited = {e: {} for e in self.ENGS}
        self.lastw = {}
        self.readers = {}
        self.dma_cnt = {}

    def _deps(self, reads, writes):
        deps = []
        for k in reads:
            ev = self.lastw.get(k)
            if ev is not None:
                deps.append((ev, "raw"))
            if k == "PB" or (isinstance(k, tuple) and k[0] == "P"):
                for r in self.readers.get(k, ()):
                    deps.append((r, "war"))
        for k in writes:
            ev = self.lastw.get(k)
            if ev is not None:
                deps.append((ev, "waw"))
            for r in self.readers.get(k, ()):
                deps.append((r, "war"))
        return deps

    def _add_waits(self, eng, deps):
        for (semkey, val), kind in deps:
            if semkey == eng and eng == "pe":
                continue
            if self.waited[eng].get(semkey, 0) >= val:
                continue
            self.waited[eng][semkey] = val
            self.stream[eng].append(("wait", semkey, val))

    def _record(self, ev, reads, writes):
        for k in reads:
            self.readers.setdefault(k, []).append(ev)
        for k in writes:
            self.lastw[k] = ev
            self.readers[k] = []

    def op(self, eng, fn, reads=(), writes=(), inc=True):
        self._add_waits(eng, self._deps(reads, writes))
        if inc:
            self.cnt[eng] += 1
            ev = (eng, self.cnt[eng])
        else:
            ev = (eng, self.cnt[eng] + 1)
        self.stream[eng].append(("op", fn, inc))
        self._record(ev, reads, writes)
        return ev

    def dma(self, eng, fn, reads=(), writes=(), semkey=None):
        sk = ("dma", semkey)
        n = self.dma_cnt.get(sk, 0)
        deps = self._deps(reads, writes)
        if n > 0:
            deps.append(((sk, 16 * n), "raw"))
        self._add_waits(eng, deps)
        n += 1
        self.dma_cnt[sk] = n
        ev = (sk, 16 * n)
        self.stream[eng].append(("dma", fn, sk))
        self._record(ev, reads, writes)
        return ev

    def wait_event(self, eng, ev):
        self._add_waits(eng, [(ev, "raw")])

    def barrier(self):
        for e in self.ENGS:
            for f in self.ENGS:
                if f != e and self.cnt[f] > 0:
                    self.wait_event(e, (f, self.cnt[f]))
            for sk, n in self.dma_cnt.items():
                self.wait_event(e, (sk, 16 * n))
        self.lastw.clear()
        self.readers.clear()

    def replay(self):
        nc = self.nc
        with ExitStack() as es:
            sems = {}
            for e in self.ENGS:
                sems[e] = es.enter_context(nc.semaphore("s_" + e))
            for i, sk in enumerate(self.dma_cnt):
                sems[sk] = es.enter_context(nc.semaphore("d%d" % i))
            block = es.enter_context(nc.Block())
            reg = {"pe": block.tensor, "act": block.scalar, "dve": block.vector,
                   "pool": block.gpsimd, "sp": block.sync}

            def make(e):
                def body(eng):
                    for it in self.stream[e]:
                        if it[0] == "wait":
                            eng.wait_ge(sems[it[1]], it[2])
                        elif it[0] == "op":
                            ins = it[1](eng)
                            if it[2]:
                                ins.then_inc(sems[e], 1)
                        else:
                            it[1](eng).then_inc(sems[it[2]], 16)
                return body

            for e in self.ENGS:
                if self.stream[e]:
                    reg[e](make(e))


def interleave(gens):
    gens = [g for g in gens if g is not None]
    while gens:
        nxt = []
        for g in gens:
            try:
                next(g)
                nxt.append(g)
            except StopIteration:
                pass
        gens = nxt


class _Stop(Exception):
    pass


def build_program(debug=False, stop=None):
    nc = bass.Bass("TRN2", target_bir_lowering=False)
    D = {}

    def din(name, shape, dt=F32):
        D[name] = nc.dram_tensor(name, shape, dt, kind="ExternalInput").ap()

    din("x", [4096, 1024]); din("mem", [512, 1024]); din("pos", [2, 2048], I32)
    din("norm_in", [1, 1024]); din("w_in", [1024, 4296]); din("qag", [128, 3]); din("w_q_b", [384, 768])
    din("kvag", [128, 2]); din("w_kv_b", [256, 1024]); din("convw", [128, 48]); din("alog", [128, 4])
    din("dtb", [128, 4]); din("gdng", [128, 1]); din("mem_norm", [1, 1024]); din("w_mem_kv", [1024, 1024])
    din("w_out", [1536, 1024]); din("norm_final", [1, 1024]); din("ident", [128, 128]); din("tri", [128, 128])
    din("ropec", [64, 3])
    y = nc.dram_tensor("y", [4096, 1024], F32, kind="ExternalOutput").ap()
    dbg = None
    if debug:
        dbg = nc.dram_tensor("dbg", [128, 12 * 2048], F32, kind="ExternalOutput").ap()

    S = Sched(nc)
    w_in_v = D["w_in"].rearrange("(k p) c -> p k c", p=128)

    def ACT(out, in_, func, reads, writes, **kw):
        S.op("act", lambda e: e.activation(out=out, in_=in_, func=func, **kw), reads, writes)

    def TT(eng, out, a, b, op, reads, writes):
        S.op(eng, lambda e: e.tensor_tensor(out=out, in0=a, in1=b, op=op), reads, writes)

    def TS(eng, out, a, s1, s2, op0, op1, reads, writes):
        if s2 is None:
            S.op(eng, lambda e: e.tensor_scalar(out=out, in0=a, scalar1=s1, scalar2=None, op0=op0), reads, writes)
        else:
            S.op(eng, lambda e: e.tensor_scalar(out=out, in0=a, scalar1=s1, scalar2=s2, op0=op0, op1=op1), reads, writes)

    def STT(eng, out, a, sc, b, op0, op1, reads, writes):
        S.op(eng, lambda e: e.scalar_tensor_tensor(out=out, in0=a, scalar=sc, in1=b, op0=op0, op1=op1), reads, writes)

    def CP(eng, out, in_, reads, writes):
        if eng == "act":
            S.op("act", lambda e: e.copy(out=out, in_=in_), reads, writes)
        else:
            S.op(eng, lambda e: e.tensor_copy(out=out, in_=in_), reads, writes)

    def RECIP(out, in_, reads, writes):
        ACT(out, in_, AF.Ln, reads, writes)
        ACT(out, out, AF.Exp, writes, writes, scale=-1.0)

    def MMG(out, pairs, reads, pkey, last_inc=True):
        n = len(pairs)
        for i, (l, r) in enumerate(pairs):
            S.op("pe", lambda e, l=l, r=r, i=i: e.matmul(out, lhsT=l, rhs=r, start=(i == 0), stop=(i == n - 1)),
                 reads if i == 0 else (), [pkey], inc=(last_inc and i == n - 1))

    def TR(out, in_, ident, reads, pkey, inc=True):
        S.op("pe", lambda e: e.transpose(out=out, in_=in_, identity=ident), reads, [pkey], inc=inc)

    def RSQRT(out, in_, scale, tmp, reads, writes, tmpkey):
        ACT(tmp, in_, AF.Ln, reads, [tmpkey], bias=EPS, scale=scale)
        ACT(out, tmp, AF.Exp, [tmpkey], writes, scale=-0.5)

    def chk(n):
        if stop == n:
            S.barrier()
            raise _Stop()

    with ExitStack() as top:
        def _emit():
            arena = {"t": None, "off": 0, "size": 0}
            arena_use = {}

            def sbt(es, name, shape, dt=F32):
                if es is top:
                    return es.enter_context(nc.sbuf_tensor(name, shape, dt))
                esz = 2 if dt == BF16 else 4
                n = 1
                for d in shape[1:]:
                    n *= d
                nbytes = n * esz
                off = arena["off"]
                arena["off"] = off + (nbytes + 63) // 64 * 64
                assert arena["off"] <= arena["size"], (name, arena["off"], arena["size"])
                ap = arena["t"][0:shape[0], off // 2:off // 2 + nbytes // 2]
                if dt != BF16:
                    ap = ap.bitcast(dt)
                if len(shape) == 3:
                    ap = ap.rearrange("p (a b) -> p a b", a=shape[1])
                elif len(shape) == 4:
                    ap = ap.rearrange("p (a b c) -> p a b c", a=shape[1], b=shape[2])
                return ap

            PS = [top.enter_context(nc.psum_tensor("ps%d" % i, [128, 512], F32)) for i in range(7)]
            PB = top.enter_context(nc.psum_tensor("psb", [128, 1024], BF16))
            pctr = [0]

            rot = {"banks": list(range(7))}
            actr = [0]

            def pnext():
                b = rot["banks"]
                k = b[pctr[0] % len(b)]
                pctr[0] += 1
                return ("P", k), PS[k]

            def pacc():
                i = actr[0] % 2
                actr[0] += 1
                return (("P", 3 + 2 * i), PS[3 + 2 * i]), (("P", 4 + 2 * i), PS[4 + 2 * i])

            IDF = sbt(top, "IDF", [128, 128]); TRI = sbt(top, "TRI", [128, 128]); MINC = sbt(top, "MINC", [128, 128])
            IDB = sbt(top, "IDB", [128, 128], BF16); CMASK = sbt(top, "CMASK", [128, 128], BF16)
            ONESB = sbt(top, "ONESB", [128, 128], BF16); ONESF = sbt(top, "ONESF", [128, 128])
            QAG = sbt(top, "QAG", [128, 3]); KVAG = sbt(top, "KVAG", [128, 2]); CONVW = sbt(top, "CONVW", [128, 48])
            ALOG = sbt(top, "ALOG", [128, 4]); DTB = sbt(top, "DTB", [128, 4]); GDNG = sbt(top, "GDNG", [128, 1])
            NEGA = sbt(top, "NEGA", [128, 4]); ROPEC = sbt(top, "ROPEC", [64, 3])
            KM = sbt(top, "KM", [128, 2, 4, 256], BF16); VM = sbt(top, "VM", [128, 2, 2, 512], BF16)
            POSI = sbt(top, "POSI", [64, TB], I32); YI = sbt(top, "YI", [64, TB], I32)
            XT = sbt(top, "XT", [128, 8, S_LEN], BF16)
            MIX = sbt(top, "MIX", [128, 12, S_LEN], BF16)
            asz = (int(nc.sbuf_bytes_remaining) - 2048) // 64 * 64
            arena["t"] = top.enter_context(nc.sbuf_tensor("ARENA", [128, asz // 2], BF16))
            arena["size"] = asz

            def ld(dst, src, key, eng="sp"):
                S.dma(eng, lambda e: e.dma_start(out=dst, in_=src), writes=[key], semkey=key)

            ld(IDF[:], D["ident"], "IDF"); ld(TRI[:], D["tri"], "TRI"); ld(QAG[:], D["qag"], "QAG")
            ld(KVAG[:], D["kvag"], "KVAG"); ld(CONVW[:], D["convw"], "CONVW"); ld(ALOG[:], D["alog"], "ALOG")
            ld(DTB[:], D["dtb"], "DTB"); ld(GDNG[:], D["gdng"], "GDNG"); ld(ROPEC[:], D["ropec"], "ROPEC")
            CP("dve", IDB[:], IDF[:], ["IDF"], ["IDB"])
            CP("dve", CMASK[:], TRI[:], ["TRI"], ["CMASK"])
            TS("dve", MINC[:], TRI[:], -1.0, 30000.0, ALU.add, ALU.mult, ["TRI"], ["MINC"])
            S.op("pool", lambda e: e.memset(ONESB[:], 1.0), (), ["ONESB"])
            S.op("pool", lambda e: e.memset(ONESF[:], 1.0), (), ["ONESF"])
            ACT(NEGA[:], ALOG[:], AF.Exp, ["ALOG"], ["NEGA"])
            TS("dve", NEGA[:], NEGA[:], -1.0, None, ALU.mult, None, ["NEGA"], ["NEGA"])

            with ExitStack() as es:
                arena["off"] = 0
                WMEM = sbt(es, "WMEM", [128, 8, 1024], BF16)
                GMEM = sbt(es, "GMEM", [128, 1024])
                MLD = [sbt(es, "MLD%d" % i, [128, 1024]) for i in range(2)]
                HBm = sbt(es, "HBm", [128, 1024], BF16)
                SCRm = sbt(es, "SCRm", [128, 1024], BF16)
                MEMT = sbt(es, "MEMT", [128, 8, 256], BF16)
                SSm = sbt(es, "SSm", [128, 4])
                wm_v = D["w_mem_kv"].rearrange("(k p) c -> p k c", p=128)
                for hh in range(2):
                    S.dma("pool", lambda e, hh=hh: e.dma_start(out=WMEM[:, 4 * hh:4 * hh + 4, :], in_=wm_v[:, 4 * hh:4 * hh + 4, :]),
                          writes=[("WMEM", hh)], semkey=("WMEM", hh))
                ld(GMEM[:], D["mem_norm"].partition_broadcast(128), "GMEM")
                for s in range(2):
                    for mt in range(2):
                        r0 = s * 256 + mt * 128
                        S.dma("sp", lambda e, r0=r0, mt=mt: e.dma_start(out=MLD[mt][:], in_=D["mem"][r0:r0 + 128, :]),
                              writes=[("MLD", mt)], semkey=("MLD", mt))
                        S.op("pool", lambda e: e.memset(SSm[:, 0:1], 0.0), (), ["SSm0"])
                        ACT(SCRm[:], MLD[mt][:], AF.Square, [("MLD", mt), "SSm0"], ["SCRm", "SSm0"], accum_out=SSm[:, 0:1])
                        RSQRT(SSm[:, 2:3], SSm[:, 0:1], 1.0 / 1024, SSm[:, 1:2], ["SSm0"], ["SSm3"], "SSm2")
                        STT("dve", HBm[:], MLD[mt][:], SSm[:, 2:3], GMEM[:], ALU.mult, ALU.mult, [("MLD", mt), "SSm3", "GMEM"], ["HBm"])
                        for k in range(8):
                            TR(PB[:, k * 128:(k + 1) * 128], HBm[:, k * 128:(k + 1) * 128], IDB[:], ["HBm", "IDB"] if k == 0 else (), "PB", inc=(k == 7))
                        CP("act", MEMT[:, :, mt * 128:(mt + 1) * 128], PB[:].rearrange("p (k t) -> p k t", k=8), ["PB"], [("MEMT", mt)])
                    for h in range(4):
                        pk, pt = pnext()
                        MMG(pt[:, 0:256], [(WMEM[:, k, h * 128:(h + 1) * 128], MEMT[:, k, :]) for k in range(8)],
                            [("WMEM", 0), ("WMEM", 1), ("MEMT", 0), ("MEMT", 1)], pk)
                        CP("act", KM[:, s, h, :], pt[:, 0:256], [pk], [("KM", s, h)])
                    for mt in range(2):
                        pk, pt = pnext()
                        MMG(pt[:, 0:512], [(MEMT[:, k, mt * 128:(mt + 1) * 128], WMEM[:, k, 512:1024]) for k in range(8)],
                            [("WMEM", 0), ("WMEM", 1), ("MEMT", 0), ("MEMT", 1)], pk)
                        CP("dve", VM[:, s, mt, :], pt[:, 0:512], [pk], [("VM", s, mt)])
                S.barrier()
                if stop == 0:
                    raise _Stop()

            for s in range(2):
                def load_mla_weights(WA, WQB, WKK, WKV):
                        for hh in range(2):
                            S.dma("pool", lambda e, hh=hh: e.dma_start(out=WA[:, 4 * hh:4 * hh + 4, 0:704], in_=w_in_v[:, 4 * hh:4 * hh + 4, 0:704]),
                                  writes=[("WA", hh)], semkey=("WA", hh))
                        S.dma("pool", lambda e: e.dma_start(out=WA[:, :, 704:736], in_=w_in_v[:, :, 672:704]), writes=[("WA", 2)], semkey=("WA", 2))
                        S.dma("pool", lambda e: e.dma_start(out=WA[:, :, 736:768], in_=w_in_v[:, :, 640:672]), writes=[("WA", 3)], semkey=("WA", 3))
                        WAK = [("WA", i) for i in range(4)]
                        wq_v = D["w_q_b"].rearrange("(k p) c -> p k c", p=128)
                        S.dma("pool", lambda e: e.dma_start(out=WQB[:, :, 0:768], in_=wq_v), writes=[("WQB", 0)], semkey=("WQB", 0))
                        for h in range(4):
                            b0 = h * 192 + 128
                            S.dma("pool", lambda e, h=h, b0=b0: e.dma_start(out=WQB[:, :, 768 + h * 64:768 + h * 64 + 32], in_=wq_v[:, :, b0 + 32:b0 + 64]),
                                  writes=[("WQB", 1 + 2 * h)], semkey=("WQB", 1))
                            S.dma("pool", lambda e, h=h, b0=b0: e.dma_start(out=WQB[:, :, 768 + h * 64 + 32:768 + h * 64 + 64], in_=wq_v[:, :, b0:b0 + 32]),
                                  writes=[("WQB", 2 + 2 * h)], semkey=("WQB", 2))
                        WQK = [("WQB", i) for i in range(9)]
                        wkv_v = D["w_kv_b"].rearrange("(k p) (h t d) -> p k h t d", p=128, h=4, t=2)
                        for m in range(2):
                            S.dma("pool", lambda e, m=m: e.dma_start(out=WKK[:, m, :].rearrange("p (h d) -> p h d", h=4), in_=wkv_v[:, m, :, 0, :]), writes=["WKK"], semkey=("WKK", m))
                            S.dma("pool", lambda e, m=m: e.dma_start(out=WKV[:, m, :].rearrange("p (h d) -> p h d", h=4), in_=wkv_v[:, m, :, 1, :]), writes=["WKV"], semkey=("WKV", m))


                xflat = MIX[:, 4:12, :].rearrange("p a b -> p (a b)")
                xoff = [0]

                def xcarve(nel_bf16, dt):
                    ap = xflat[:, xoff[0]:xoff[0] + nel_bf16]
                    xoff[0] += nel_bf16
                    assert xoff[0] <= 16384
                    return ap if dt == BF16 else ap.bitcast(dt)

                GIN = xcarve(2048, F32)
                XLD = [xcarve(2048, F32) for i in range(3)]
                HB = [xcarve(1024, BF16) for i in range(2)]
                SCR = xcarve(1024, BF16)
                SS = [xcarve(8, F32) for i in range(2)]
                ld(GIN[:], D["norm_in"].partition_broadcast(128), "GIN")

                def xtile_a(t, s=s):
                    sl = t % 3
                    r0 = s * S_LEN + t * 128
                    S.dma("sp", lambda e, r0=r0, sl=sl: e.dma_start(out=XLD[sl][:], in_=D["x"][r0:r0 + 128, :]),
                          writes=[("XLD", sl)], semkey=("XLD", sl))
                    ss = SS[t % 2]
                    sk = ("SS", t % 2)
                    S.op("pool", lambda e, ss=ss: e.memset(ss[:, 0:1], 0.0), (), [sk + (0,)])
                    ACT(SCR[:], XLD[sl][:], AF.Square, [("XLD", sl), sk + (0,)], ["SCR", sk + (0,)], accum_out=ss[:, 0:1])
                    RSQRT(ss[:, 2:3], ss[:, 0:1], 1.0 / 1024, ss[:, 1:2], [sk + (0,)], [sk + (3,)], sk + (2,))
                    hb = HB[t % 2]
                    STT("dve", hb[:], XLD[sl][:], ss[:, 2:3], GIN[:], ALU.mult, ALU.mult, [("XLD", sl), sk + (3,), "GIN"], [("HB", t % 2)])

                def xtile_b(t):
                    hb = HB[t % 2]
                    for k in range(8):
                        TR(PB[:, k * 128:(k + 1) * 128], hb[:, k * 128:(k + 1) * 128], IDB[:], [("HB", t % 2), "IDB"] if k == 0 else (), "PB", inc=(k == 7))
                    CP("act" if t % 2 == 0 else "dve", XT[:, :, t * 128:(t + 1) * 128], PB[:].rearrange("p (k t) -> p k t", k=8), ["PB"], [("XT", t)])

                for t in range(4):
                    xtile_a(t)
                    xtile_b(t)

                def xt_keys(j):
                    return [("XT", 4 * j + i) for i in range(4)]

                with ExitStack() as es:
                    arena["off"] = 0
                    WA = sbt(es, "WA", [128, 8, 768], BF16)
                    WQB = sbt(es, "WQB", [128, 3, 1024], BF16)
                    WKK = sbt(es, "WKK", [128, 2, 512], BF16); WKV = sbt(es, "WKV", [128, 2, 512], BF16)
                    KN = sbt(es, "KN", [128, 4, S_LEN], BF16); KR = sbt(es, "KR", [128, S_LEN], BF16)
                    V = sbt(es, "V", [128, 16, 512], BF16)
                    CQ = sbt(es, "CQ", [128, 3, TB], BF16); CKV = sbt(es, "CKV", [128, 2, TB], BF16)
                    SQ = [sbt(es, "SQ%d" % i, [128, TB], BF16) for i in range(5)]
                    TMP = sbt(es, "TMP", [128, TB]); TMPK = sbt(es, "TMPK", [128, TB]); RQ = sbt(es, "RQ", [128, TB]); RKV = sbt(es, "RKV", [128, TB])
                    RKC = sbt(es, "RKC", [128, 4]); TMPC = sbt(es, "TMPC", [128, 4])
                    COS = sbt(es, "COS", [64, TB]); SINS = sbt(es, "SINS", [64, TB])
                    COSR = sbt(es, "COSR", [64, TB]); SINR = sbt(es, "SINR", [64, TB])
                    T1 = sbt(es, "T1", [64, TB]); T2 = sbt(es, "T2", [64, TB])
                    YY = sbt(es, "YY", [64, TB])
                    YF = sbt(es, "YF", [64, TB])
                    QN = sbt(es, "QN", [128, 4, TB], BF16); QR = sbt(es, "QR", [128, 4, TB], BF16)
                    PT = [sbt(es, "PT%d" % i, [128, TB], BF16) for i in range(3)]
                    RD = sbt(es, "RD", [128, TB])
                    load_mla_weights(WA, WQB, WKK, WKV)
                    S.op("pool", lambda e: e.memset(KR[64:128, :], 0.0), (), ["KRpad"])
                    S.op("pool", lambda e: e.memset(QR[64:128, :, :], 0.0), (), ["QRpad"])
                    WAK = [("WA", i) for i in range(4)]
                    WQK = [("WQB", i) for i in range(9)]
                    SC_ATT = float(192 ** -0.5)
                    for j in range(NBLK):
                        T0 = j * TB
                        xk = xt_keys(j)
                        S.dma("sp", lambda e, T0=T0, s=s: e.dma_start(out=POSI[:], in_=D["pos"][s:s + 1, T0:T0 + TB].partition_broadcast(64)),
                              writes=["POSI"], semkey="POSI")
                        CP("dve", YF[:], POSI[:], ["POSI"], ["PF"])
                        for (tab, col, tk) in ((COS, 2, "COS"), (SINS, 1, "SINS")):
                            TS("dve", YY[:], YF[:], ROPEC[:, 0:1], ROPEC[:, col:col + 1], ALU.mult, ALU.add, ["PF", "ROPEC"], ["YY"])
                            CP("dve", YI[:], YY[:], ["YY"], ["YI"])
                            CP("dve", T1[:], YI[:], ["YI"], ["T1"])
                            TT("dve", YY[:], YY[:], T1[:], ALU.subtract, ["YY", "T1"], ["YY"])
                            TS("dve", T2[:], YY[:], 0.5, None, ALU.is_gt, None, ["YY"], ["T2"])
                            TT("dve", YY[:], YY[:], T2[:], ALU.subtract, ["YY", "T2"], ["YY"])
                            ACT(tab[:], YY[:], AF.Sin, ["YY"], [tk], scale=float(2 * np.pi * (1 - 1e-6)))
                        chk(20)
                        for m in range(3):
                            pk, pt = pnext()
                            MMG(pt[:, :], [(WA[:, k, m * 128:(m + 1) * 128], XT[:, k, T0:T0 + TB]) for k in range(8)], WAK + xk, pk)
                            ACT(SQ[m][:], pt[:, :], AF.Square, [pk], [("SQ", m)])
                            TS("dve", CQ[:, m, :], pt[:, :], QAG[:, m:m + 1], None, ALU.mult, None, [pk, "QAG"], [("CQ", m)])
                        for m in range(2):
                            pk, pt = pnext()
                            MMG(pt[:, :], [(WA[:, k, 384 + m * 128:384 + (m + 1) * 128], XT[:, k, T0:T0 + TB]) for k in range(8)], WAK + xk, pk)
                            ACT(SQ[3 + m][:], pt[:, :], AF.Square, [pk], [("SQ", 3 + m)])
                            TS("dve", CKV[:, m, :], pt[:, :], KVAG[:, m:m + 1], None, ALU.mult, None, [pk, "KVAG"], [("CKV", m)])
                        pss, pst = pnext()
                        for m in range(3):
                            S.op("pe", lambda e, m=m, pst=pst: e.matmul(pst[:, :], lhsT=ONESB[:], rhs=SQ[m][:], start=(m == 0), stop=(m == 2)),
                                 [("SQ", m), "ONESB"], [pss], inc=(m == 2))
                        RSQRT(RQ[:], pst[:, :], 1.0 / 384, TMP[:], [pss], ["RQ"], "TMP")
                        pss, pst = pnext()
                        for m in range(2):
                            S.op("pe", lambda e, m=m, pst=pst: e.matmul(pst[:, :], lhsT=ONESB[:], rhs=SQ[3 + m][:], start=(m == 0), stop=(m == 1)),
                                 [("SQ", 3 + m), "ONESB"], [pss], inc=(m == 1))
                        pcs, pct = pnext()
                        for tt in range(4):
                            for m in range(2):
                                S.op("pe", lambda e, m=m, tt=tt, pct=pct: e.matmul(pct[:, tt:tt + 1], lhsT=SQ[3 + m][:, tt * 128:(tt + 1) * 128], rhs=ONESB[:, 0:1],
                                                                                  start=(m == 0), stop=(m == 1)),
                                     [("SQ", 3), ("SQ", 4)], [pcs], inc=(tt == 3 and m == 1))
                        RSQRT(RKV[:], pst[:, :], 1.0 / 256, TMPK[:], [pss], ["RKV"], "TMPK")
                        RSQRT(RKC[:], pct[:, 0:4], 1.0 / 256, TMPC[:], [pcs], ["RKC"], "TMPC")
                        chk(22)
                        pk1, pt1 = pnext()
                        MMG(pt1[0:64, :], [(WA[:, k, 640:704], XT[:, k, T0:T0 + TB]) for k in range(8)], WAK + xk, pk1)
                        pk2, pt2 = pnext()
                        MMG(pt2[0:64, :], [(WA[:, k, 704:768], XT[:, k, T0:T0 + TB]) for k in range(8)], WAK + xk, pk2)
                        TT("dve", T1[:], pt1[0:64, :], COS[:], ALU.mult, [pk1, "COS"], ["T1"])
                        TT("dve", T2[:], pt2[0:64, :], SINS[:], ALU.mult, [pk2, "SINS"], ["T2"])
                        TT("pool", KR[0:64, T0:T0 + TB], T1[:], T2[:], ALU.add, ["T1", "T2"], [("KR", j)])
                        chk(23)
                        TT("pool", COSR[:], COS[:], RQ[0:64, :], ALU.mult, ["COS", "RQ"], ["COSR"])
                        TT("pool", SINR[:], SINS[:], RQ[0:64, :], ALU.mult, ["SINS", "RQ"], ["SINR"])
                        cqk = [("CQ", m) for m in range(3)]
                        for h in range(4):
                            pk, pt = pnext()
                            MMG(pt[:, :], [(WQB[:, m, h * 192:h * 192 + 128], CQ[:, m, :]) for m in range(3)], WQK + cqk, pk)
                            TT("dve", QN[:, h, :], pt[:, :], RQ[:], ALU.mult, [pk, "RQ"], [("QN", h)])
                            pk1, pt1 = pnext()
                            MMG(pt1[0:64, :], [(WQB[:, m, h * 192 + 128:h * 192 + 192], CQ[:, m, :]) for m in range(3)], WQK + cqk, pk1)
                            pk2, pt2 = pnext()
                            MMG(pt2[0:64, :], [(WQB[:, m, 768 + h * 64:768 + (h + 1) * 64], CQ[:, m, :]) for m in range(3)], WQK + cqk, pk2)
                            TT("dve", T1[:], pt1[0:64, :], COSR[:], ALU.mult, [pk1, "COSR"], ["T1"])
                            TT("dve", T2[:], pt2[0:64, :], SINR[:], ALU.mult, [pk2, "SINR"], ["T2"])
                            TT("pool", QR[0:64, h, :], T1[:], T2[:], ALU.add, ["T1", "T2"], [("QR", h)])
                        chk(24)
                        ckk = [("CKV", m) for m in range(2)]
                        for h in range(4):
                            pk, pt = pnext()
                            MMG(pt[:, :], [(WKK[:, m, h * 128:(h + 1) * 128], CKV[:, m, :]) for m in range(2)], ["WKK"] + ckk, pk)
                            TT("dve", KN[:, h, T0:T0 + TB], pt[:, :], RKV[:], ALU.mult, [pk, "RKV"], [("KN", h, j)])
                        for tt in range(4):
                            pk, pt = pnext()
                            MMG(pt[:, :], [(CKV[:, m, tt * 128:(tt + 1) * 128], WKV[:, m, :]) for m in range(2)], ["WKV"] + ckk, pk)
                            TS("dve", V[:, 4 * j + tt, :], pt[:, :], RKC[:, tt:tt + 1], None, ALU.mult, None, [pk, "RKC"], [("V", 4 * j + tt)])
                        chk(25)
                        rot["banks"] = [0, 1, 2]
                        nkt = 4 * j + 4
                        items = [(h, kt) for h in range(4) for kt in range(nkt)]
                        LA = 2
                        pend = {}
                        accs = {}
                        fin = []

                        def emit_S(idx):
                            h, kt = items[idx]
                            r = kt - 4 * j
                            qo = max(r, 0) * 128
                            N = TB - qo
                            pk, pt = pnext()
                            kb = kt // 4
                            MMG(pt[:, 0:N], [(KN[:, h, kt * 128:(kt + 1) * 128], QN[:, h, qo:TB]),
                                             (KR[:, kt * 128:(kt + 1) * 128], QR[:, h, qo:TB])],
                                [("KN", h, kb), ("KR", kb), ("QN", h), ("QR", h), "KRpad", "QRpad"], pk)
                            pend[idx] = (pk, pt, r, qo, N)

                        for idx in range(min(LA, len(items))):
                            emit_S(idx)
                        for idx in range(len(items)):
                            if idx + LA < len(items):
                                emit_S(idx + LA)
                            if j + 1 < NBLK and idx < 16:
                                if idx % 4 == 0:
                                    xtile_a(4 * (j + 1) + idx // 4)
                                elif idx % 4 == 3:
                                    xtile_b(4 * (j + 1) + idx // 4)
                            h, kt = items[idx]
                            if kt == 0:
                                accs[h] = pacc()
                            (pok, pot), (pdk, pdt) = accs[h]
                            pk, pt, r, qo, N = pend.pop(idx)
                            P_ = PT[idx % 3]
                            ptk = ("PT", idx % 3)
                            ACT(P_[:, 0:N], pt[:, 0:N], AF.Exp, [pk], [ptk], scale=SC_ATT)
                            if r >= 0:
                                TT("pool", P_[:, 0:128], P_[:, 0:128], CMASK[:], ALU.mult, [ptk, "CMASK"], [ptk])
                            first = (kt == 0)
                            last = (kt == nkt - 1)
                            S.op("pe", lambda e, pot=pot, kt=kt, h=h, P_=P_, qo=qo, N=N, first=first, last=last:
                                 e.matmul(pot[:, qo:TB], lhsT=V[:, kt, h * 128:(h + 1) * 128], rhs=P_[:, 0:N], start=first, stop=last),
                                 [("V", kt), ptk], [pok], inc=False)
                            S.op("pe", lambda e, pdt=pdt, P_=P_, qo=qo, N=N, first=first, last=last:
                                 e.matmul(pdt[:, qo:TB], lhsT=ONESB[:], rhs=P_[:, 0:N], start=first, stop=last),
                                 [ptk, "ONESB"], [pdk], inc=True)
                            if last:
                                fin.append((idx + 3, h, pok, pot, pdk, pdt))
                            while fin and (fin[0][0] <= idx or idx == len(items) - 1):
                                _, fh, fpok, fpot, fpdk, fpdt = fin.pop(0)
                                RECIP(RD[:], fpdt[:, :], [fpdk], ["RD"])
                                TT("dve", MIX[:, fh, T0:T0 + TB], fpot[:, :], RD[:], ALU.mult, [fpok, "RD"], [("MIX", fh, j)])
                        rot["banks"] = list(range(7))
                    S.barrier()
                    if stop == 2:
                        raise _Stop()

                with ExitStack() as es:
                    arena["off"] = 0
                    WG = sbt(es, "WG", [128, 8, 1544], BF16)
                    HALO = sbt(es, "HALO", [128, 12, 3])
                    RAW = [sbt(es, "RAW%d" % i, [128, 515]) for i in range(2)]
                    ACCD = [sbt(es, "ACCD%d" % i, [128, TB]) for i in range(2)]
                    SL = [sbt(es, "SL%d" % i, [128, TB], BF16) for i in range(2)]
                    SQg = [sbt(es, "SQg%d" % i, [128, TB], BF16) for i in range(2)]
                    RNM = sbt(es, "RNM", [128, TB])
                    GQ2 = [sbt(es, "GQ%d" % i, [128, 4, TB], BF16) for i in range(2)]
                    GK2 = [sbt(es, "GK%d" % i, [128, 4, TB], BF16) for i in range(2)]
                    GV2 = [sbt(es, "GV%d" % i, [128, 4, TB], BF16) for i in range(2)]
                    SCN = ["AB", "BETA", "NBETA", "Z", "AZ", "MX", "E", "L", "G", "GC", "GL", "EG", "ED", "EGL"]
                    SCT = [{n: sbt(es, "%s%d" % (n, i), [128, 32] if n == "AB" else [128, 16]) for n in SCN} for i in range(2)]
                    mixflat = MIX[:, 8:12, :].rearrange("p a b -> p (a b)")
                    moff = [0]

                    def mcarve(dt):
                        n = 512 if dt == BF16 else 1024
                        ap = mixflat[:, moff[0]:moff[0] + n]
                        moff[0] += n
                        assert moff[0] <= 8192
                        return ap if dt == BF16 else ap.bitcast(dt)

                    def iset(i):
                        mk = (lambda nm, dt: sbt(es, nm + "0", [128, 512], dt)) if i == 0 else (lambda nm, dt: mcarve(dt))
                        d = {}
                        d["KEG"] = mk("KEG", BF16); d["VT"] = mk("VT", BF16); d["DINC"] = mk("DINC", F32); d["DSB"] = mk("DSB", F32)
                        d["EGB"] = mk("EGB", BF16)
                        d["ZZ"] = [mk("ZZa", BF16), mk("ZZb", BF16)]
                        d["ZT"] = [mk("ZTa", BF16), mk("ZTb", BF16)]
                        d["PP"] = [mk("PPa", BF16), mk("PPb", BF16)]
                        return d
                    ISET = [iset(0), iset(1)]
                    KD = [sbt(es, "KD%d" % i, [128, 512], BF16) for i in range(3)]
                    AT = [sbt(es, "AT%d" % i, [128, 512], BF16) for i in range(3)]
                    QG = [sbt(es, "QG%d" % i, [128, 512], BF16) for i in range(3)]
                    WT = [sbt(es, "WT%d" % i, [128, 512], BF16) for i in range(3)]
                    UU = [sbt(es, "UU%d" % i, [128, 512]) for i in range(3)]
                    TV = sbt(es, "TV", [128, 512]); VNEW = sbt(es, "VNEW", [128, 512], BF16)
                    ST = sbt(es, "ST", [128, 512]); STT_ = sbt(es, "STT", [128, 512]); STB = sbt(es, "STB", [128, 512], BF16)
                    OSB = sbt(es, "OSB", [128, 512]); SQO = sbt(es, "SQO", [128, 512], BF16)
                    RN = sbt(es, "RN", [128, 512]); TMP2 = sbt(es, "TMP2", [128, 512])

                    S.dma("pool", lambda e: e.dma_start(out=WG[:, :, 1536:1544], in_=w_in_v[:, :, 2240:2248]), writes=[("WG", "ab")], semkey=("WG", 3))
                    for g3 in range(3):
                        S.dma("pool", lambda e, g3=g3: e.dma_start(out=WG[:, :, g3 * 512:(g3 + 1) * 512], in_=w_in_v[:, :, 704 + g3 * 512:704 + (g3 + 1) * 512]),
                              writes=[("WG", g3)], semkey=("WG", g3))
                    S.op("pool", lambda e: e.memset(HALO[:], 0.0), (), ["HALO"])
                    S.op("pool", lambda e: e.memset(ST[:], 0.0), (), ["ST"])
                    S.op("pool", lambda e: e.memset(STB[:], 0.0), (), ["STB"])

                    def h4(ap):
                        return ap.rearrange("p (h i) -> p h i", h=4)

                    def bc4(ap4):
                        return ap4.unsqueeze(2).to_broadcast([128, 4, 128])

                    def blockprep(j):
                        T0 = j * TB
                        xk = xt_keys(j)
                        bp = j % 2
                        GQ, GK, GV = GQ2[bp], GK2[bp], GV2[bp]
                        sc = SCT[j % 2]
                        K = lambda n: (n, j % 2)
                        pk, pt = pnext()
                        for c in range(4):
                            MMG(pt[:, c * 8:(c + 1) * 8], [(XT[:, k, T0 + c * 128:T0 + (c + 1) * 128], WG[:, k, 1536:1544]) for k in range(8)],
                                [("WG", "ab")] + xk, pk, last_inc=(c == 3))
                        CP("act", sc["AB"][:], pt[:, 0:32], [pk], [K("AB")])
                        ab3 = sc["AB"][:].rearrange("p (c e) -> p c e", c=4)
                        v3 = lambda n: sc[n][:].rearrange("p (c h) -> p c h", c=4)
                        ACT(v3("BETA"), ab3[:, :, 4:8], AF.Sigmoid, [K("AB")], [K("BETA")])
                        TS("pool", sc["NBETA"][:], sc["BETA"][:], -1.0, None, ALU.mult, None, [K("BETA")], [K("NBETA")])
                        TT("dve", v3("Z"), ab3[:, :, 0:4], DTB[:].unsqueeze(1).to_broadcast([128, 4, 4]), ALU.add, [K("AB"), "DTB"], [K("Z")])
                        TS("dve", sc["AZ"][:], sc["Z"][:], -1.0, None, ALU.mult, None, [K("Z")], [K("AZ")])
                        TT("dve", sc["AZ"][:], sc["AZ"][:], sc["Z"][:], ALU.max, [K("AZ"), K("Z")], [K("AZ")])
                        ACT(sc["E"][:], sc["AZ"][:], AF.Exp, [K("AZ")], [K("E")], scale=-1.0)
                        ACT(sc["L"][:], sc["E"][:], AF.Ln, [K("E")], [K("L")], bias=1.0)
                        TS("dve", sc["MX"][:], sc["Z"][:], 0.0, None, ALU.max, None, [K("Z")], [K("MX")])
                        TT("dve", sc["L"][:], sc["L"][:], sc["MX"][:], ALU.add, [K("L"), K("MX")], [K("L")])
                        TT("dve", v3("G"), sc["L"][:].rearrange("p (c h) -> p c h", c=4), NEGA[:].unsqueeze(1).to_broadcast([128, 4, 4]), ALU.mult,
                           [K("L"), "NEGA"], [K("G")])
                        pk, pt = pnext()
                        S.op("pe", lambda e, pt=pt: e.matmul(pt[:, 0:16], lhsT=TRI[:], rhs=sc["G"][:], start=True, stop=True), [K("G"), "TRI"], [pk])
                        CP("dve", sc["GC"][:], pt[:, 0:16], [pk], [K("GC")])
                        pk, pt = pnext()
                        S.op("pe", lambda e, pt=pt: e.matmul(pt[:, 0:16], lhsT=ONESF[:], rhs=sc["G"][:], start=True, stop=True), [K("G"), "ONESF"], [pk])
                        CP("dve", sc["GL"][:], pt[:, 0:16], [pk], [K("GL")])
                        ACT(sc["EG"][:], sc["GC"][:], AF.Exp, [K("GC")], [K("EG")])
                        ACT(sc["EGL"][:], sc["GL"][:], AF.Exp, [K("GL")], [K("EGL")])
                        TT("dve", sc["ED"][:], sc["GL"][:], sc["GC"][:], ALU.subtract, [K("GL"), K("GC")], [K("ED")])
                        ACT(sc["ED"][:], sc["ED"][:], AF.Exp, [K("ED")], [K("ED")])
                        yield
                        def finish_norm(ch, h, sl, slk, sq, sqk):
                            pk, pt = pnext()
                            S.op("pe", lambda e, pt=pt, sq=sq: e.matmul(pt[:, :], lhsT=ONESB[:], rhs=sq[:], start=True, stop=True), [sqk, "ONESB"], [pk])
                            RSQRT(RNM[:], pt[:, :], 1.0, RNM[:], [pk], ["RNM"], "RNM")
                            if ch < 4:
                                STT("dve", GQ[:, h, :], sl[:], float(128 ** -0.5), RNM[:], ALU.mult, ALU.mult, [slk, "RNM"], [("GQ", bp, h)])
                            else:
                                TT("dve", GK[:, h, :], sl[:], RNM[:], ALU.mult, [slk, "RNM"], [("GK", bp, h)])

                        pend_norm = None
                        for ch in range(12):
                            raw = RAW[ch % 2]
                            rk = ("RAW", ch % 2)
                            pk, pt = pnext()
                            MMG(pt[:, :], [(WG[:, k, ch * 128:(ch + 1) * 128], XT[:, k, T0:T0 + TB]) for k in range(8)], [("WG", ch // 4)] + xk, pk)
                            CP("pool", raw[:, 0:3], HALO[:, ch, :], ["HALO"], [rk + (0,)])
                            CP("act", raw[:, 3:515], pt[:, :], [pk], [rk + (1,)])
                            CP("pool", HALO[:, ch, :], raw[:, 512:515], [rk + (1,)], ["HALO"])
                            acc = ACCD[ch % 2]
                            ak = ("ACCD", ch % 2)
                            rr = [rk + (0,), rk + (1,), "CONVW"]
                            TS("dve", acc[:], pt[:, :], CONVW[:, ch * 4 + 3:ch * 4 + 4], None, ALU.mult, None, [pk, "CONVW"], [ak])
                            for jt in (2, 1, 0):
                                STT("dve", acc[:], raw[:, jt:jt + TB], CONVW[:, ch * 4 + jt:ch * 4 + jt + 1], acc[:], ALU.mult, ALU.add, rr + [ak], [ak])
                            h = ch % 4
                            if pend_norm is not None:
                                finish_norm(*pend_norm)
                                pend_norm = None
                            if ch >= 8:
                                ACT(GV[:, h, :], acc[:], AF.Silu, [ak], [("GV", bp, h)])
                            else:
                                sl = SL[ch % 2]
                                slk = ("SL", ch % 2)
                                sq = SQg[ch % 2]
                                sqk = ("SQg", ch % 2)
                                ACT(sl[:], acc[:], AF.Silu, [ak], [slk])
                                TT("pool", sq[:], sl[:], sl[:], ALU.mult, [slk], [sqk])
                                pend_norm = (ch, h, sl, slk, sq, sqk)
                            yield

                    def prep(cc):
                        j, c = divmod(cc, 4)
                        sc = SCT[j % 2]
                        K = lambda n: (n, j % 2)
                        si = cc % 2
                        I_ = ISET[si]
                        IK = lambda n: (n, "set", si)
                        par = cc % 3
                        KEG, VT, DINC, DSB, EGB, ZZ, ZT, PP = I_["KEG"], I_["VT"], I_["DINC"], I_["DSB"], I_["EGB"], I_["ZZ"], I_["ZT"], I_["PP"]
                        cs = slice(c * 128, (c + 1) * 128)
                        col4 = lambda n: sc[n][:, c * 4:(c + 1) * 4]
                        bp = j % 2
                        GQ, GK, GV = GQ2[bp], GK2[bp], GV2[bp]
                        gkk = [("GK", bp, h) for h in range(4)]
                        gqk = [("GQ", bp, h) for h in range(4)]
                        gvk = [("GV", bp, h) for h in range(4)]
                        for h in range(4):
                            TR(PB[:, h * 128:(h + 1) * 128], GK[:, h, cs], IDB[:], gkk + ["IDB"] if h == 0 else (), "PB", inc=False)
                        for h in range(4):
                            TR(PB[:, 512 + h * 128:512 + (h + 1) * 128], GV[:, h, cs], IDB[:], gvk if h == 0 else (), "PB", inc=(h == 3))
                        TT("dve", h4(KEG[:]), h4(PB[:, 0:512]), bc4(col4("EG")), ALU.mult, ["PB", K("EG")], [IK("KEG")])
                        TT("dve", h4(KD[par][:]), h4(PB[:, 0:512]), bc4(col4("ED")), ALU.mult, ["PB", K("ED")], [("KD", par)])
                        CP("act", VT[:], PB[:, 512:1024], ["PB"], [IK("VT")])
                        yield
                        pgk, pgt = pnext()
                        for h in range(4):
                            S.op("pe", lambda e, h=h, pgt=pgt: e.matmul(pgt[:, h * 128:(h + 1) * 128],
                                                                        lhsT=sc["G"][:, c * 4 + h:c * 4 + h + 1].to_broadcast([128, 128]), rhs=TRI[:],
                                                                        start=True, stop=True),
                                 [K("G"), "TRI"] if h == 0 else (), [pgk], inc=(h == 3))
                        TT("dve", h4(DINC[:]), h4(pgt[:, :]), bc4(col4("GC")), ALU.subtract, [pgk, K("GC")], [IK("DINC")])
                        ACT(EGB[:], pgt[:, :], AF.Exp, [pgk], [IK("EGB")])
                        TT("pool", h4(DINC[:]), h4(DINC[:]), MINC[:].unsqueeze(1).to_broadcast([128, 4, 128]), ALU.add, [IK("DINC"), "MINC"], [IK("DINC")])
                        ACT(DINC[:], DINC[:], AF.Exp, [IK("DINC")], [IK("DINC")])
                        TT("pool", h4(DSB[:]), h4(DINC[:]), IDF[:].unsqueeze(1).to_broadcast([128, 4, 128]), ALU.subtract, [IK("DINC"), "IDF"], [IK("DSB")])
                        TT("pool", h4(DSB[:]), h4(DSB[:]), bc4(col4("NBETA")), ALU.mult, [IK("DSB"), K("NBETA")], [IK("DSB")])
                        TT("pool", h4(QG[par][:]), GQ[:, :, cs], h4(EGB[:]), ALU.mult, gqk + [IK("EGB")], [("QG", par)])
                        yield
                        pk, pt = pnext()
                        for h in range(4):
                            S.op("pe", lambda e, h=h, pt=pt: e.matmul(pt[:, h * 128:(h + 1) * 128], lhsT=GK[:, h, cs], rhs=GK[:, h, cs], start=True, stop=True),
                                 gkk if h == 0 else (), [pk], inc=(h == 3))
                        TT("dve", ZZ[0][:], pt[:, :], DSB[:], ALU.mult, [pk, IK("DSB")], [IK("ZZ0")])
                        pk, pt = pnext()
                        for h in range(4):
                            S.op("pe", lambda e, h=h, pt=pt: e.matmul(pt[:, h * 128:(h + 1) * 128], lhsT=GK[:, h, cs], rhs=GQ[:, h, cs], start=True, stop=True),
                                 gkk + gqk if h == 0 else (), [pk], inc=(h == 3))
                        TT("dve", AT[par][:], pt[:, :], DINC[:], ALU.mult, [pk, IK("DINC")], [("AT", par)])
                        TT("pool", h4(PP[0][:]), h4(ZZ[0][:]), IDB[:].unsqueeze(1).to_broadcast([128, 4, 128]), ALU.add, [IK("ZZ0"), "IDB"], [IK("PP0")])
                        yield
                        for h in range(4):
                            TR(PB[:, h * 128:(h + 1) * 128], ZZ[0][:, h * 128:(h + 1) * 128], IDB[:], [IK("ZZ0"), "IDB"] if h == 0 else (), "PB", inc=(h == 3))
                        CP("act", ZT[0][:], PB[:, 0:512], ["PB"], [IK("ZT0")])
                        yield
                        def pupd(lv, zt_i):
                            pcur = (lv - 1) % 2
                            pnx = 1 - pcur
                            pkp, ptp = pnext()
                            for h in range(4):
                                hs = slice(h * 128, (h + 1) * 128)
                                S.op("pe", lambda e, hs=hs, ptp=ptp, pcur=pcur, zt_i=zt_i: e.matmul(ptp[:, hs], lhsT=ZT[zt_i][:, hs], rhs=PP[pcur][:, hs], start=True, stop=True),
                                     [IK("ZT%d" % zt_i), IK("PP%d" % pcur)] if h == 0 else (), [pkp], inc=(h == 3))
                            return pkp, ptp, pcur, pnx

                        cur = 0
                        for lv in range(1, 7):
                            nx = 1 - cur
                            zk, ztk = IK("ZZ%d" % cur), IK("ZT%d" % cur)
                            pkt, ptt = pnext()
                            for h in range(4):
                                hs = slice(h * 128, (h + 1) * 128)
                                S.op("pe", lambda e, hs=hs, ptt=ptt, cur=cur: e.matmul(ptt[:, hs], lhsT=ZZ[cur][:, hs], rhs=ZT[cur][:, hs], start=True, stop=True),
                                     [zk, ztk] if h == 0 else (), [pkt], inc=(h == 3))
                            if lv < 6:
                                pkz, ptz = pnext()
                                for h in range(4):
                                    hs = slice(h * 128, (h + 1) * 128)
                                    S.op("pe", lambda e, hs=hs, ptz=ptz, cur=cur: e.matmul(ptz[:, hs], lhsT=ZT[cur][:, hs], rhs=ZZ[cur][:, hs], start=True, stop=True),
                                         [zk, ztk] if h == 0 else (), [pkz], inc=(h == 3))
                            pu = pupd(lv - 1, cur) if lv >= 2 else None
                            CP("act", ZT[nx][:], ptt[:, :], [pkt], [IK("ZT%d" % nx)])
                            if lv < 6:
                                CP("dve", ZZ[nx][:], ptz[:, :], [pkz], [IK("ZZ%d" % nx)])
                            if pu is not None:
                                pkp, ptp, pcur, pnx = pu
                                TT("dve", PP[pnx][:], ptp[:, :], PP[pcur][:], ALU.add, [pkp, IK("PP%d" % pcur)], [IK("PP%d" % pnx)])
                            cur = nx
                            yield
                        pkp, ptp, pcur, pnx = pupd(6, cur)
                        TT("dve", PP[pnx][:], ptp[:, :], PP[pcur][:], ALU.add, [pkp, IK("PP%d" % pcur)], [IK("PP%d" % pnx)])
                        yield
                        TTm = PP[0]
                        tk = IK("PP0")
                        pk, pt = pnext()
                        for h in range(4):
                            hs = slice(h * 128, (h + 1) * 128)
                            S.op("pe", lambda e, hs=hs, pt=pt: e.matmul(pt[:, hs], lhsT=KEG[:, hs], rhs=TTm[:, hs], start=True, stop=True),
                                 [IK("KEG"), tk] if h == 0 else (), [pk], inc=(h == 3))
                        CP("act", WT[par][:], pt[:, :], [pk], [("WT", par)])
                        pk, pt = pnext()
                        for h in range(4):
                            hs = slice(h * 128, (h + 1) * 128)
                            S.op("pe", lambda e, hs=hs, pt=pt: e.matmul(pt[:, hs], lhsT=TTm[:, hs], rhs=VT[:, hs], start=True, stop=True),
                                 [IK("VT"), tk] if h == 0 else (), [pk], inc=(h == 3))
                        CP("dve", UU[par][:], pt[:, :], [pk], [("UU", par)])
                        yield

                    def seq(cc):
                        j, c = divmod(cc, 4)
                        sc = SCT[j % 2]
                        K = lambda n: (n, j % 2)
                        par = cc % 3
                        C0 = cc * 128
                        col4 = lambda n: sc[n][:, c * 4:(c + 1) * 4]
                        pk, pt = pnext()
                        for h in range(4):
                            hs = slice(h * 128, (h + 1) * 128)
                            S.op("pe", lambda e, hs=hs, pt=pt: e.matmul(pt[:, hs], lhsT=WT[par][:, hs], rhs=STB[:, hs], start=True, stop=True),
                                 [("WT", par), "STB"] if h == 0 else (), [pk], inc=(h == 3))
                        TT("dve", TV[:], UU[par][:], pt[:, :], ALU.subtract, [("UU", par), pk], ["TV"])
                        TT("pool", h4(VNEW[:]), h4(TV[:]), bc4(col4("BETA")), ALU.mult, ["TV", K("BETA")], ["VNEW"])
                        yield
                        pok, pot = pnext()
                        for h in range(4):
                            hs = slice(h * 128, (h + 1) * 128)
                            S.op("pe", lambda e, hs=hs, pot=pot: e.matmul(pot[:, hs], lhsT=STB[:, hs], rhs=QG[par][:, hs], start=True, stop=False),
                                 ["STB", ("QG", par)] if h == 0 else (), [pok], inc=False)
                            S.op("pe", lambda e, hs=hs, pot=pot: e.matmul(pot[:, hs], lhsT=VNEW[:, hs], rhs=AT[par][:, hs], start=False, stop=True),
                                 ["VNEW", ("AT", par)] if h == 0 else (), [pok], inc=(h == 3))
                        pk, pt = pnext()
                        for h in range(4):
                            hs = slice(h * 128, (h + 1) * 128)
                            S.op("pe", lambda e, hs=hs, pt=pt: e.matmul(pt[:, hs], lhsT=KD[par][:, hs], rhs=VNEW[:, hs], start=True, stop=True),
                                 [("KD", par), "VNEW"] if h == 0 else (), [pk], inc=(h == 3))
                        TT("pool", h4(STT_[:]), h4(ST[:]), bc4(col4("EGL")), ALU.mult, ["ST", K("EGL")], ["STT"])
                        TT("dve", ST[:], STT_[:], pt[:, :], ALU.add, ["STT", pk], ["ST"])
                        CP("pool", STB[:], ST[:], ["ST"], ["STB"])
                        ACT(SQO[:], pot[:, :], AF.Square, [pok], ["SQO"])
                        CP("dve", OSB[:], pot[:, :], [pok], ["OSB"])
                        yield
                        pk, pt = pnext()
                        S.op("pe", lambda e, pt=pt: e.matmul(pt[:, :], lhsT=ONESB[:], rhs=SQO[:], start=True, stop=True), ["SQO", "ONESB"], [pk])
                        RSQRT(RN[:], pt[:, :], 1.0 / 128, TMP2[:], [pk], ["RN"], "TMP2")
                        STT("dve", MIX[:, 4:8, C0:C0 + 128], h4(OSB[:]), GDNG[:, 0:1], h4(RN[:]), ALU.mult, ALU.mult, ["OSB", "RN", "GDNG"],
                            [("MIX", 4 + h, cc // 4) for h in range(4)])
                        yield

                    done = set()
                    active = {}
                    started = set()

                    def ready(t):
                        kind, i = t
                        if kind == "B":
                            return (i == 0) or ((("B", i - 1) in done) and (i < 2 or ("P", 4 * i - 5) in done) and (("Q", 4 * i - 5) in done or 4 * i - 5 < 0))
                        if kind == "P":
                            return (("B", i // 4) in done) and (i < 2 or ("P", i - 2) in done) and (i < 3 or ("Q", i - 3) in done)
                        return (("P", i) in done) and (i == 0 or ("Q", i - 1) in done)

                    tasks = [("B", j) for j in range(4)] + [("P", cc) for cc in range(16)] + [("Q", cc) for cc in range(16)]
                    mk = {"B": blockprep, "P": prep, "Q": seq}
                    while len(done) < len(tasks):
                        for t in tasks:
                            if t not in started and ready(t):
                                started.add(t)
                                active[t] = mk[t[0]](t[1])
                        assert active, "GDN scheduler stalled"
                        prio = {"Q": 0, "B": 1, "P": 2}
                        for t in sorted(active.keys(), key=lambda t: (prio[t[0]], t[1])):
                            try:
                                next(active[t])
                            except StopIteration:
                                del active[t]
                                done.add(t)
                    arena_use["gdn"] = arena["off"]
                    S.barrier()
                    if stop == 3:
                        raise _Stop()

                with ExitStack() as es:
                    arena["off"] = 0
                    WGT = sbt(es, "WGT", [128, 8, 1536], BF16)
                    WOUT = sbt(es, "WOUT", [128, 12, 1024], BF16)
                    GFIN = sbt(es, "GFIN", [128, 1024])
                    WM = sbt(es, "WM", [128, 8, 512], BF16)
                    MQ = sbt(es, "MQ", [128, 4, TB], BF16)
                    PTm = [sbt(es, "PTm%d" % i, [128, TB], BF16) for i in range(3)]
                    RDm = sbt(es, "RDm", [128, TB])
                    S.dma("pool", lambda e: e.dma_start(out=WM[:], in_=w_in_v[:, :, 2248:2760]), writes=["WM"], semkey="WM")
                    for hh in range(2):
                        S.dma("pool", lambda e, hh=hh, WGT=WGT: e.dma_start(out=WGT[:, 4 * hh:4 * hh + 4, :], in_=w_in_v[:, 4 * hh:4 * hh + 4, 2760:4296]),
                              writes=[("WGT", hh)], semkey=("WGT", hh))
                    wo_v = D["w_out"].rearrange("(k p) c -> p k c", p=128)
                    for q in range(3):
                        S.dma("pool", lambda e, q=q, WOUT=WOUT: e.dma_start(out=WOUT[:, 4 * q:4 * q + 4, :], in_=wo_v[:, 4 * q:4 * q + 4, :]),
                              writes=[("WOUT", q)], semkey=("WOUT", q))
                    S.dma("sp", lambda e, GFIN=GFIN: e.dma_start(out=GFIN[:], in_=D["norm_final"].partition_broadcast(128)), writes=["GFIN"], semkey="GFIN")
                    SC_M = float(128 ** -0.5)
                    pti = 0
                    for j in range(NBLK):
                        T0 = j * TB
                        xk = xt_keys(j)
                        for h in range(4):
                            pk, pt = pnext()
                            MMG(pt[:, :], [(WM[:, k, h * 128:(h + 1) * 128], XT[:, k, T0:T0 + TB]) for k in range(8)], ["WM"] + xk, pk)
                            CP("act" if h % 2 else "dve", MQ[:, h, :], pt[:, :], [pk], [("MQ", h)])
                        rot["banks"] = [0, 1, 2]
                        mitems = [(h, mt) for h in range(4) for mt in range(2)]
                        mpend = {}
                        maccs = {}
                        mfin = []

                        def emit_MS(idx):
                            h, mt = mitems[idx]
                            pk, pt = pnext()
                            S.op("pe", lambda e, pt=pt, h=h, mt=mt, s=s: e.matmul(pt[:, :], lhsT=KM[:, s, h, mt * 128:(mt + 1) * 128], rhs=MQ[:, h, :], start=True, stop=True),
                                 [("MQ", h)], [pk])
                            mpend[idx] = (pk, pt)

                        for idx in range(2):
                            emit_MS(idx)
                        for idx in range(len(mitems)):
                            if idx + 2 < len(mitems):
                                emit_MS(idx + 2)
                            h, mt = mitems[idx]
                            if mt == 0:
                                maccs[h] = pacc()
                            (pok, pot), (pdk, pdt) = maccs[h]
                            pk, pt = mpend.pop(idx)
                            P_ = PTm[pti % 3]
                            ptk = ("PTm", pti % 3)
                            pti += 1
                            ACT(P_[:], pt[:, :], AF.Exp, [pk], [ptk], scale=SC_M)
                            S.op("pe", lambda e, pot=pot, h=h, mt=mt, P_=P_, s=s: e.matmul(pot[:, :], lhsT=VM[:, s, mt, h * 128:(h + 1) * 128], rhs=P_[:], start=(mt == 0), stop=(mt == 1)),
                                 [ptk], [pok], inc=False)
                            S.op("pe", lambda e, pdt=pdt, mt=mt, P_=P_: e.matmul(pdt[:, :], lhsT=ONESB[:], rhs=P_[:], start=(mt == 0), stop=(mt == 1)),
                                 [ptk], [pdk], inc=True)
                            if mt == 1:
                                mfin.append((idx + 2, h, pok, pot, pdk, pdt))
                            while mfin and (mfin[0][0] <= idx or idx == len(mitems) - 1):
                                _, fh, fpok, fpot, fpdk, fpdt = mfin.pop(0)
                                RECIP(RDm[:], fpdt[:, :], [fpdk], ["RDm"])
                                TT("dve", MIX[:, 8 + fh, T0:T0 + TB], fpot[:, :], RDm[:], ALU.mult, [fpok, "RDm"], [("MIX", 8 + fh, j)])
                        rot["banks"] = list(range(7))
                    S.barrier()
                    if stop == 4:
                        raise _Stop()

                if dbg is not None and s == 0:
                    with ExitStack() as es:
                        arena["off"] = 96 * 1024
                        DB = sbt(es, "DB", [128, 2048])
                        for ch in range(12):
                            CP("dve", DB[:], MIX[:, ch, :], [], ["DB"])
                            S.dma("sp", lambda e, ch=ch: e.dma_start(out=dbg[:, ch * 2048:(ch + 1) * 2048], in_=DB[:]), reads=["DB"], semkey="DB")
                        S.barrier()
                        if stop == 5:
                            raise _Stop()

                with ExitStack() as es:
                    arena["off"] = 0
                    WGT = sbt(es, "WGT", [128, 8, 1536], BF16)
                    WOUT = sbt(es, "WOUT", [128, 12, 1024], BF16)
                    GFIN = sbt(es, "GFIN", [128, 1024])
                    XL = [sbt(es, "XL%d" % i, [128, 1024]) for i in range(3)]
                    SG = [sbt(es, "SG%d" % i, [128, TB], BF16) for i in range(2)]
                    SCRo = sbt(es, "SCRo", [128, 1024], BF16)
                    SSo = [sbt(es, "SSo%d" % i, [128, 4]) for i in range(2)]
                    WGTK = [("WGT", 0), ("WGT", 1)]
                    WOK = [("WOUT", q) for q in range(3)]
                    ti = 0

                    def gate_block(j):
                        T0 = j * TB
                        xk = xt_keys(j)
                        for ch in range(12):
                            pk, pt = pnext()
                            MMG(pt[:, :], [(WGT[:, k, ch * 128:(ch + 1) * 128], XT[:, k, T0:T0 + TB]) for k in range(8)], WGTK + xk, pk)
                            sg = SG[ch % 2]
                            ACT(sg[:], pt[:, :], AF.Silu, [pk], [("SG", ch % 2)])
                            TT("pool", MIX[:, ch, T0:T0 + TB], MIX[:, ch, T0:T0 + TB], sg[:], ALU.mult, [("SG", ch % 2), ("MIX", ch, j)], [("MIX", ch, j)])

                    gate_block(0)
                    for j in range(NBLK):
                        T0 = j * TB
                        if j + 1 < NBLK:
                            gate_block(j + 1)
                        mk = [("MIX", ch, j) for ch in range(12)]
                        for tt in range(4):
                            sl = ti % 3
                            ti += 1
                            xl = XL[sl]
                            xlk = ("XL", sl)
                            r0 = s * S_LEN + T0 + tt * 128
                            S.dma("sp", lambda e, r0=r0, xl=xl: e.dma_start(out=xl[:], in_=D["x"][r0:r0 + 128, :]), writes=[xlk], semkey=xlk)
                            for half in range(2):
                                pk, pt = pnext()
                                MMG(pt[:, :], [(MIX[:, ch, T0 + tt * 128:T0 + (tt + 1) * 128], WOUT[:, ch, half * 512:(half + 1) * 512]) for ch in range(12)],
                                    mk + WOK, pk)
                                TT("dve", xl[:, half * 512:(half + 1) * 512], xl[:, half * 512:(half + 1) * 512], pt[:, :], ALU.add, [pk, xlk], [xlk])
                            ss = SSo[tt % 2]
                            sk = ("SSo", tt % 2)
                            S.op("pool", lambda e, ss=ss: e.memset(ss[:, 0:1], 0.0), (), [sk + (0,)])
                            ACT(SCRo[:], xl[:], AF.Square, [xlk, sk + (0,)], ["SCRo", sk + (0,)], accum_out=ss[:, 0:1])
                            RSQRT(ss[:, 2:3], ss[:, 0:1], 1.0 / 1024, ss[:, 1:2], [sk + (0,)], [sk + (3,)], sk + (2,))
                            STT("dve", xl[:], xl[:], ss[:, 2:3], GFIN[:], ALU.mult, ALU.mult, [xlk, sk + (3,), "GFIN"], [xlk])
                            S.dma("sp", lambda e, r0=r0, xl=xl: e.dma_start(out=y[r0:r0 + 128, :], in_=xl[:]), reads=[xlk], semkey=("YST", sl))
                    S.barrier()
                    if stop == 6:
                        raise _Stop()

        try:
            _emit()
        except _Stop:
            pass
        for sk, n in list(S.dma_cnt.items()):
            S.wait_event("sp", (sk, 16 * n))
        S.replay()
    return nc


_NC_CACHE = {}
_STOP = None


def _consts():
    ident = np.eye(128, dtype=np.float32)
    tri = np.triu(np.ones((128, 128), dtype=np.float32))
    half = 32
    invf = 1.0 / (10000.0 ** (np.arange(half, dtype=np.float32) / half))
    invf64 = np.concatenate([invf, invf]).astype(np.float64)
    ropec = np.zeros((64, 3), dtype=np.float32)
    ropec[:, 0] = (invf64 / (2 * np.pi)).astype(np.float32)
    ropec[:32, 1] = 0.5
    ropec[32:, 1] = 0.0
    ropec[:, 2] = 0.25
    return ident, tri, ropec


def kernel(x, mem, positions, norm_in, w_in, q_a_norm, w_q_b, kv_a_norm, w_kv_b, gdn_conv,
           gdn_a_log, gdn_dt_bias, gdn_norm, mem_norm, w_mem_kv, w_out, norm_final, _debug=False):
    f = lambda a: np.ascontiguousarray(np.asarray(a, dtype=np.float32))
    x = f(x); mem = f(mem)
    positions = np.ascontiguousarray(np.asarray(positions, dtype=np.int32))
    ident, tri, ropec = _consts()
    shared = {
        "norm_in": f(norm_in).reshape(1, 1024),
        "w_in": f(w_in).reshape(1024, 4296),
        "qag": np.ascontiguousarray(f(q_a_norm).reshape(3, 128).T),
        "w_q_b": f(w_q_b).reshape(384, 768),
        "kvag": np.ascontiguousarray(f(kv_a_norm).reshape(2, 128).T),
        "w_kv_b": f(w_kv_b).reshape(256, 1024),
        "convw": np.ascontiguousarray(f(gdn_conv).reshape(4, 12, 128).transpose(2, 1, 0).reshape(128, 48)),
        "alog": np.ascontiguousarray(np.broadcast_to(f(gdn_a_log).reshape(1, 4), (128, 4))),
        "dtb": np.ascontiguousarray(np.broadcast_to(f(gdn_dt_bias).reshape(1, 4), (128, 4))),
        "gdng": np.ascontiguousarray(f(gdn_norm).reshape(128, 1)),
        "mem_norm": f(mem_norm).reshape(1, 1024),
        "w_mem_kv": f(w_mem_kv).reshape(1024, 1024),
        "w_out": f(w_out).reshape(1536, 1024),
        "norm_final": f(norm_final).reshape(1, 1024),
        "ident": ident, "tri": tri, "ropec": ropec,
    }
    key = bool(_debug)
    if key not in _NC_CACHE:
        _NC_CACHE[key] = build_program(debug=key, stop=_STOP)
    nc = _NC_CACHE[key]
    in_maps = []
    for c in range(8):
        m = dict(shared)
        m["x"] = np.ascontiguousarray(x[2 * c:2 * c + 2].reshape(4096, 1024))
        m["mem"] = np.ascontiguousarray(mem[2 * c:2 * c + 2].reshape(512, 1024))
        m["pos"] = np.ascontiguousarray(positions[2 * c:2 * c + 2])
        in_maps.append(m)
    res = run_bass_kernel_spmd(nc, in_maps, core_ids=list(range(8)))
    out = np.concatenate([np.asarray(r["y"], dtype=np.float32).reshape(2, 2048, 1024) for r in res.results], axis=0)
    if _debug:
        return out, [np.asarray(r["dbg"]) for r in res.results]
    return out
```
